# Optimizing a Trainium2 kernel written in Bass

```python
import jax, jax.numpy as jnp
from jax import lax
import numpy as np

D_MODEL = 1024
BATCH = 2
SEQ = 8192
DEPTH = 2

HEAD_DIM = 64
ROPE_THETA = 10000.0
EPS = 1e-6
Q_BLOCK = 128

A_HEADS = 4
MOBA_BLOCK = 256
MOBA_TOPK = 3
B_HEADS = 8
B_KV_HEADS = 2
WINDOW = 128
C_HEADS = 4
C_HALF = HEAD_DIM // 2

N_BRANCH = 3
D_FF = 2816
CONV_W = 3

A_W = A_HEADS * HEAD_DIM
B_QW = B_HEADS * HEAD_DIM
B_KVW = B_KV_HEADS * HEAD_DIM
C_W = C_HEADS * HEAD_DIM
IN_SIZES = (A_W, A_W, A_W, B_QW, B_KVW, B_KVW, C_W, C_W, C_W, N_BRANCH * D_MODEL)
IN_COLS = sum(IN_SIZES)
SPLIT_POINTS = tuple(np.cumsum(IN_SIZES)[:-1].tolist())

kernel_name = "hybrid_moba_swa_diff_convglu"


def rms_norm(x, g):
    x32 = x.astype(jnp.float32)
    y = x32 * lax.rsqrt(jnp.mean(x32 * x32, axis=-1, keepdims=True) + EPS)
    return (y * g.astype(jnp.float32)).astype(x.dtype)


def rope_tables(seq, dim):
    inv = 1.0 / (ROPE_THETA ** (jnp.arange(0, dim, 2, dtype=jnp.float32) / dim))
    ang = jnp.arange(seq, dtype=jnp.float32)[:, None] * inv[None, :]
    return jnp.cos(ang), jnp.sin(ang)


def apply_rope(x, cos, sin):
    half = x.shape[-1] // 2
    shape = (1, x.shape[1]) + (1,) * (x.ndim - 3) + (half,)
    c = cos.reshape(shape).astype(x.dtype)
    s = sin.reshape(shape).astype(x.dtype)
    x1, x2 = x[..., :half], x[..., half:]
    return jnp.concatenate([x1 * c - x2 * s, x2 * c + x1 * s], axis=-1)


def moba_attention(q, k, v):
    B, S, H, d = q.shape
    n_blk = -(-S // MOBA_BLOCK)
    pad = n_blk * MOBA_BLOCK - S
    padw = ((0, 0), (0, pad), (0, 0), (0, 0))
    kb = jnp.pad(k, padw).reshape(B, n_blk, MOBA_BLOCK, H, d).transpose(0, 3, 1, 2, 4)
    vb = jnp.pad(v, padw).reshape(B, n_blk, MOBA_BLOCK, H, d).transpose(0, 3, 1, 2, 4)
    k_mean = kb.mean(axis=3)
    topk = min(MOBA_TOPK, n_blk)
    n_chunks = S // Q_BLOCK
    qc = q.reshape(B, n_chunks, Q_BLOCK, H, d).transpose(1, 0, 3, 2, 4)
    b_idx = jnp.arange(B)[:, None, None, None]
    h_idx = jnp.arange(H)[None, :, None, None]
    scale = d ** -0.5
    n_sel = topk * MOBA_BLOCK

    def one_chunk(args):
        c, qh = args
        q_pos = c * Q_BLOCK + jnp.arange(Q_BLOCK)
        own = (c * Q_BLOCK) // MOBA_BLOCK
        gate = jnp.einsum('bhqd,bhnd->bhqn', qh, k_mean).astype(jnp.float32)
        gate = jnp.where(jnp.arange(n_blk) < own, gate, -jnp.inf)
        _, idx = lax.top_k(gate, topk)
        valid = jnp.arange(topk) < own
        k_sel = kb[b_idx, h_idx, idx]
        v_sel = vb[b_idx, h_idx, idx]
        s_sel = jnp.einsum('bhqd,bhqjld->bhqjl', qh, k_sel).astype(jnp.float32) * scale
        s_sel = jnp.where(valid[:, None], s_sel, -jnp.inf).reshape(B, H, Q_BLOCK, n_sel)
        k_own = lax.dynamic_index_in_dim(kb, own, axis=2, keepdims=False)
        v_own = lax.dynamic_index_in_dim(vb, own, axis=2, keepdims=False)
        s_own = jnp.einsum('bhqd,bhld->bhql', qh, k_own).astype(jnp.float32) * scale
        k_pos = own * MOBA_BLOCK + jnp.arange(MOBA_BLOCK)
        s_own = jnp.where(k_pos[None, :] <= q_pos[:, None], s_own, -jnp.inf)
        p = jax.nn.softmax(jnp.concatenate([s_sel, s_own], axis=-1), axis=-1).astype(v.dtype)
        p_sel = p[..., :n_sel].reshape(B, H, Q_BLOCK, topk, MOBA_BLOCK)
        p_own = p[..., n_sel:]
        return (jnp.einsum('bhqjl,bhqjld->bhqd', p_sel, v_sel)
                + jnp.einsum('bhql,bhld->bhqd', p_own, v_own))

    o = lax.map(one_chunk, (jnp.arange(n_chunks), qc))
    return o.transpose(1, 0, 3, 2, 4).reshape(B, S, H * d)


def sliding_window_sink_attention(q, k, v, sinks):
    B, S, Hq, d = q.shape
    Hkv = k.shape[2]
    G = Hq // Hkv
    nb = S // WINDOW
    qb = q.reshape(B, nb, WINDOW, Hkv, G, d)

    def band(t):
        tb = t.reshape(B, nb, WINDOW, Hkv, d)
        prev = jnp.concatenate([jnp.zeros_like(tb[:, :1]), tb[:, :-1]], axis=1)
        return jnp.concatenate([prev, tb], axis=2)

    kband, vband = band(k), band(v)
    s = jnp.einsum('bnqkgd,bnjkd->bnkgqj', qb, kband).astype(jnp.float32) * (d ** -0.5)
    rel = jnp.arange(WINDOW)[:, None] + WINDOW - jnp.arange(2 * WINDOW)[None, :]
    in_win = (rel >= 0) & (rel < WINDOW)
    k_abs = jnp.arange(nb)[:, None] * WINDOW - WINDOW + jnp.arange(2 * WINDOW)[None, :]
    mask = in_win[None, :, :] & (k_abs >= 0)[:, None, :]
    s = jnp.where(mask[None, :, None, None], s, -jnp.inf)
    sink = sinks.astype(jnp.float32).reshape(1, 1, Hkv, G, 1, 1)
    m = jnp.maximum(s.max(axis=-1, keepdims=True), sink)
    p = jnp.exp(s - m)
    denom = p.sum(axis=-1, keepdims=True) + jnp.exp(sink - m)
    o = jnp.einsum('bnkgqj,bnjkd->bnqkgd', (p / denom).astype(v.dtype), vband)
    return o.reshape(B, S, Hq * d)


def differential_attention(q, k, v, lam, subln_g, lam_init):
    B, S, H, _, dh = q.shape
    n_chunks = S // Q_BLOCK
    qc = q.reshape(B, n_chunks, Q_BLOCK, H, 2, dh).transpose(1, 0, 2, 3, 4, 5)
    scale = dh ** -0.5
    k_pos = jnp.arange(S)

    def one_chunk(args):
        c, qq = args
        q_pos = c * Q_BLOCK + jnp.arange(Q_BLOCK)
        s = jnp.einsum('bqhcd,bkhcd->bhcqk', qq, k).astype(jnp.float32) * scale
        s = jnp.where(k_pos[None, :] <= q_pos[:, None], s, -jnp.inf)
        p = jax.nn.softmax(s, axis=-1)
        a = (p[:, :, 0] - lam * p[:, :, 1]).astype(v.dtype)
        return jnp.einsum('bhqk,bkhd->bqhd', a, v)

    o = lax.map(one_chunk, (jnp.arange(n_chunks), qc))
    o = o.transpose(1, 0, 2, 3, 4).reshape(B, S, H, 2 * dh)
    o = rms_norm(o, subln_g) * (1.0 - lam_init)
    return o.reshape(B, S, H * 2 * dh)


def causal_depthwise_conv(u, w, b):
    S = u.shape[1]
    up = jnp.pad(u, ((0, 0), (CONV_W - 1, 0), (0, 0)))
    y = b.astype(u.dtype)
    for j in range(CONV_W):
        y = y + up[:, j:j + S] * w[j]
    return y


def setup_inputs(seed: int = 0) -> dict:
    key = jax.random.key(seed)
    ks = jax.random.split(key, 24)
    n = jax.random.normal
    f32 = jnp.float32

    def gain(k, shape):
        return 1.0 + 0.02 * n(k, shape, f32)

    return {
        "x": n(ks[0], (BATCH, SEQ, D_MODEL), f32),
        "attn_norm": gain(ks[1], (DEPTH, D_MODEL)),
        "w_in": n(ks[2], (DEPTH, D_MODEL, IN_COLS), f32) * D_MODEL ** -0.5,
        "qn_a": gain(ks[3], (DEPTH, HEAD_DIM)),
        "kn_a": gain(ks[4], (DEPTH, HEAD_DIM)),
        "qn_b": gain(ks[5], (DEPTH, HEAD_DIM)),
        "kn_b": gain(ks[6], (DEPTH, HEAD_DIM)),
        "sinks": 0.5 * n(ks[7], (DEPTH, B_HEADS), f32),
        "qn_c": gain(ks[8], (DEPTH, C_HALF)),
        "kn_c": gain(ks[9], (DEPTH, C_HALF)),
        "lam_q1": 0.1 * n(ks[10], (DEPTH, C_HALF), f32),
        "lam_k1": 0.1 * n(ks[11], (DEPTH, C_HALF), f32),
        "lam_q2": 0.1 * n(ks[12], (DEPTH, C_HALF), f32),
        "lam_k2": 0.1 * n(ks[13], (DEPTH, C_HALF), f32),
        "subln": gain(ks[14], (DEPTH, HEAD_DIM)),
        "w_pa": n(ks[15], (DEPTH, A_W, D_MODEL), f32) * A_W ** -0.5,
        "w_pb": n(ks[16], (DEPTH, B_QW, D_MODEL), f32) * B_QW ** -0.5,
        "w_pc": n(ks[17], (DEPTH, C_W, D_MODEL), f32) * C_W ** -0.5,
        "w_out": n(ks[18], (DEPTH, D_MODEL, D_MODEL), f32) * D_MODEL ** -0.5,
        "mlp_norm": gain(ks[19], (DEPTH, D_MODEL)),
        "w_up": n(ks[20], (DEPTH, D_MODEL, 2 * D_FF), f32) * D_MODEL ** -0.5,
        "conv_w": 0.5 * n(ks[21], (DEPTH, CONV_W, 2 * D_FF), f32),
        "conv_b": 0.01 * n(ks[22], (DEPTH, 2 * D_FF), f32),
        "w_down": n(ks[23], (DEPTH, D_FF, D_MODEL), f32) * D_FF ** -0.5,
    }


def reference(x, attn_norm, w_in, qn_a, kn_a, qn_b, kn_b, sinks, qn_c, kn_c,
              lam_q1, lam_k1, lam_q2, lam_k2, subln, w_pa, w_pb, w_pc, w_out,
              mlp_norm, w_up, conv_w, conv_b, w_down):
    B, S, D = x.shape
    cos64, sin64 = rope_tables(S, HEAD_DIM)
    cos32, sin32 = rope_tables(S, C_HALF)
    for i in range(DEPTH):
        lam_init = 0.8 - 0.6 * float(np.exp(-0.3 * i))
        h = rms_norm(x, attn_norm[i])
        proj = h @ w_in[i]
        qa, ka, va, qb, kb, vb, qc, kc, vc, gates = jnp.split(proj, SPLIT_POINTS, axis=-1)

        qa = apply_rope(rms_norm(qa.reshape(B, S, A_HEADS, HEAD_DIM), qn_a[i]), cos64, sin64)
        ka = apply_rope(rms_norm(ka.reshape(B, S, A_HEADS, HEAD_DIM), kn_a[i]), cos64, sin64)
        ya = moba_attention(qa, ka, va.reshape(B, S, A_HEADS, HEAD_DIM))

        qb = apply_rope(rms_norm(qb.reshape(B, S, B_HEADS, HEAD_DIM), qn_b[i]), cos64, sin64)
        kb = apply_rope(rms_norm(kb.reshape(B, S, B_KV_HEADS, HEAD_DIM), kn_b[i]), cos64, sin64)
        yb = sliding_window_sink_attention(qb, kb, vb.reshape(B, S, B_KV_HEADS, HEAD_DIM), sinks[i])

        qc = apply_rope(rms_norm(qc.reshape(B, S, C_HEADS, 2, C_HALF), qn_c[i]), cos32, sin32)
        kc = apply_rope(rms_norm(kc.reshape(B, S, C_HEADS, 2, C_HALF), kn_c[i]), cos32, sin32)
        lam = (jnp.exp(jnp.sum(lam_q1[i].astype(jnp.float32) * lam_k1[i].astype(jnp.float32)))
               - jnp.exp(jnp.sum(lam_q2[i].astype(jnp.float32) * lam_k2[i].astype(jnp.float32)))
               + lam_init)
        yc = differential_attention(qc, kc, vc.reshape(B, S, C_HEADS, HEAD_DIM), lam, subln[i], lam_init)

        g = jax.nn.sigmoid(gates.reshape(B, S, N_BRANCH, D))
        merged = g[:, :, 0] * (ya @ w_pa[i]) + g[:, :, 1] * (yb @ w_pb[i]) + g[:, :, 2] * (yc @ w_pc[i])
        x = x + merged @ w_out[i]

        h = rms_norm(x, mlp_norm[i])
        u = causal_depthwise_conv(h @ w_up[i], conv_w[i], conv_b[i])
        gate_u, val_u = jnp.split(u, 2, axis=-1)
        x = x + (jax.nn.silu(gate_u) * val_u) @ w_down[i]
    return x
```

```python
import numpy as np
import ml_dtypes
from contextlib import ExitStack
import concourse.bass as bass
import concourse.mybir as mybir
from concourse.bass_utils import run_bass_kernel_spmd


F32 = mybir.dt.float32
BF16 = mybir.dt.bfloat16
AF = mybir.ActivationFunctionType
ALU = mybir.AluOpType
AX = mybir.AxisListType


class Ev:
    __slots__ = ("sem", "val")

    def __init__(self, sem, val):
        self.sem = sem
        self.val = val


class Res:
    def __init__(self, name):
        self.name = name
        self.w = None
        self.r = []
        self.dsem = None
        self.dcnt = 0


class KB:
    ENGS = ("pe", "dve", "act", "pool", "sp")

    def __init__(self, nc, stack):
        self.nc = nc
        self.stack = stack
        self.eng = {"pe": nc.tensor, "dve": nc.vector, "act": nc.scalar,
                    "pool": nc.gpsimd, "sp": nc.sync}
        self.sem = {e: stack.enter_context(nc.semaphore("s_" + e)) for e in self.ENGS}
        self.cnt = {e: 0 for e in self.ENGS}
        self.seen = {e: {} for e in self.ENGS}
        self.stream = {e: [] for e in self.ENGS}
        self.nsem = len(self.ENGS)
        self.allres = []

    def res(self, name):
        r = Res(name)
        self.allres.append(r)
        return r

    def sb(self, name, shape, dt):
        t = self.stack.enter_context(self.nc.sbuf_tensor(name, list(shape), dt))
        return t

    def ps(self, name, shape, dt):
        return self.stack.enter_context(self.nc.psum_tensor(name, list(shape), dt))

    def _waits(self, e, reads, writes, nosame):
        need = {}

        def add(ev, r):
            if ev is None:
                return
            if r in nosame and ev.sem is self.sem[e]:
                return
            k = id(ev.sem)
            if k not in need or need[k][1] < ev.val:
                need[k] = (ev.sem, ev.val)

        for r in reads:
            add(r.w, r)
        for w in writes:
            add(w.w, w)
            for ev in w.r:
                add(ev, w)
        out = []
        seen = self.seen[e]
        for k, (s, v) in need.items():
            if seen.get(k, 0) >= v:
                continue
            seen[k] = v
            out.append((s, v))
        return out

    def op(self, e, fn, reads=(), writes=(), signal=True, nosame=()):
        waits = self._waits(e, reads, writes, nosame)
        if signal:
            self.cnt[e] += 1
            ev = Ev(self.sem[e], self.cnt[e])
            sig = (self.sem[e], 1)
        else:
            ev = Ev(self.sem[e], self.cnt[e] + 1)
            sig = None
        self.stream[e].append((waits, fn, sig))
        for r in reads:
            r.r.append(ev)
        for w in writes:
            w.w = ev
            w.r = []
        return ev

    def dma(self, q, out_ap, in_ap, reads=(), writes=(), **kw):
        tr = (list(writes) + list(reads))[0]
        if tr.dsem is None:
            tr.dsem = self.stack.enter_context(self.nc.semaphore("d_" + tr.name))
            self.nsem += 1
        waits = [w for w in self._waits(q, reads, writes, ()) if w[0] is not tr.dsem]
        tr.dcnt += 16
        ev = Ev(tr.dsem, tr.dcnt)
        self.stream[q].append(
            (waits, lambda en: en.dma_start(out=out_ap, in_=in_ap, **kw), (tr.dsem, 16)))
        for r in reads:
            r.r.append(ev)
        for w in writes:
            w.w = ev
            w.r = []
        return ev

    def finish(self, final_res):
        waits = self._waits("sp", final_res, final_res, ())
        self.stream["sp"].append((waits, None, None))
        nc = self.nc
        with nc.Block() as block:
            def mk(e):
                def body(en):
                    for waits, fn, sig in self.stream[e]:
                        for s, v in waits:
                            en.wait_ge(s, v)
                        if fn is None:
                            continue
                        ins = fn(en)
                        if sig is not None:
                            ins.then_inc(sig[0], sig[1])
                return body
            block.tensor(mk("pe"))
            block.vector(mk("dve"))
            block.scalar(mk("act"))
            block.gpsimd(mk("pool"))
            block.sync(mk("sp"))


def _mk(KBc):
    def tt(self, e, out, in0, in1, op, reads, writes):
        return self.op(e, lambda en: en.tensor_tensor(out=out, in0=in0, in1=in1, op=op), reads, writes)

    def ts(self, e, out, in0, s1, s2, op0, op1=None, reads=(), writes=()):
        if op1 is None:
            return self.op(e, lambda en: en.tensor_scalar(out=out, in0=in0, scalar1=s1, scalar2=None, op0=op0), reads, writes)
        return self.op(e, lambda en: en.tensor_scalar(out=out, in0=in0, scalar1=s1, scalar2=s2, op0=op0, op1=op1), reads, writes)

    def stt(self, out, in0, scalar, in1, op0, op1, reads, writes):
        return self.op("dve", lambda en: en.scalar_tensor_tensor(out=out, in0=in0, scalar=scalar, in1=in1, op0=op0, op1=op1), reads, writes)

    def cp(self, e, out, in_, reads, writes):
        if e == "act":
            return self.op(e, lambda en: en.activation(out=out, in_=in_, func=AF.Copy), reads, writes)
        return self.op(e, lambda en: en.tensor_copy(out=out, in_=in_), reads, writes)

    def act(self, out, in_, func, reads, writes, bias=None, scale=None, accum_out=None):
        kw = {}
        if bias is not None:
            kw["bias"] = bias
        if scale is not None:
            kw["scale"] = scale
        if accum_out is not None:
            kw["accum_out"] = accum_out
        return self.op("act", lambda en: en.activation(out=out, in_=in_, func=func, **kw), reads, writes)

    def mm(self, out, lhsT, rhs, start, stop, reads, writes, signal=None):
        if signal is None:
            signal = stop
        return self.op("pe", lambda en: en.matmul(out, lhsT=lhsT, rhs=rhs, start=start, stop=stop),
                       reads, writes, signal=signal, nosame=writes)

    def tr(self, out, in_, ident, reads, writes, signal=True):
        return self.op("pe", lambda en: en.transpose(out=out, in_=in_, identity=ident), reads, writes,
                       signal=signal, nosame=writes)

    def red(self, out, in_, reads, writes, op=None):
        op = op or ALU.add
        return self.op("dve", lambda en: en.tensor_reduce(out=out, in_=in_, axis=AX.X, op=op), reads, writes)

    def ms(self, e, ap, val, writes):
        return self.op(e, lambda en: en.memset(ap, val), (), writes)

    for f in (tt, ts, stt, cp, act, mm, tr, red, ms):
        setattr(KBc, f.__name__, f)


_mk(KB)


NEG = -30000.0
EPS = 1e-6
GROUPS = [[j, 7 - j, 8 + j, 15 - j] for j in range(4)]
NT = 16
SEGS = [(0, 8, 64), (768, 10, 64), (1536, 16, 32)]
TBLK = [0, 128, 256, 384, 768, 896, 1024, 1152, 1280, 1536, 1664, 1792, 1920]
VSEG = [(512, 256), (1408, 128), (2048, 256)]


def emit_P(kb, D, ident):
    nc = kb.nc
    x, anorm, w_in, gall, ct64, st64, ct32, st32 = (D[k] for k in ("x", "anorm", "w_in", "gall", "ct64", "st64", "ct32", "st32"))
    qkT, vout, gT = D["qkT"], D["v"], D["gT"]
    idb, r_id = ident

    wq = kb.sb("wq", [128, 8, 2304], BF16); r_wq = kb.res("wq")
    wg = [kb.sb(f"wg{i}", [128, 8, 512], BF16) for i in range(2)]; r_wg = [kb.res(f"wg{i}") for i in range(2)]
    hT = kb.sb("hT", [128, 8, 2048], BF16); r_hT = [kb.res(f"hT{t}") for t in range(NT)]
    xt = [kb.sb(f"xt{i}", [128, 1024], F32) for i in range(2)]; r_xt = [kb.res(f"xt{i}") for i in range(2)]
    h16 = [kb.sb(f"h16{i}", [128, 1024], BF16) for i in range(2)]; r_h16 = [kb.res(f"h16{i}") for i in range(2)]
    pj = [kb.sb(f"pj{i}", [128, 2304], F32) for i in range(2)]; r_pj = [kb.res(f"pj{i}") for i in range(2)]
    xc = kb.sb("xc", [128, 2048], F32); r_xc = kb.res("xc")
    xs = kb.sb("xs", [128, 2048], F32); r_xs = kb.res("xs")
    qk16 = [kb.sb(f"qk16{i}", [128, 2048], BF16) for i in range(2)]; r_qk16 = [kb.res(f"qk16{i}") for i in range(2)]
    qkTs = [kb.sb(f"qkTs{i}", [128, 13, 128], BF16) for i in range(2)]; r_qkTs = [kb.res(f"qkTs{i}") for i in range(2)]
    v16 = [kb.sb(f"v16{i}", [128, 640], BF16) for i in range(2)]; r_v16 = [kb.res(f"v16{i}") for i in range(2)]
    g16 = [kb.sb(f"g16{i}", [128, 512], BF16) for i in range(2)]; r_g16 = [kb.res(f"g16{i}") for i in range(2)]
    an = kb.sb("an", [128, 1024], F32); r_an = kb.res("an")
    ga = kb.sb("ga", [128, 2048], F32); r_ga = kb.res("ga")
    c64 = kb.sb("c64", [128, NT, 64], F32); s64 = kb.sb("s64", [128, NT, 64], F32)
    c32 = kb.sb("c32", [128, NT, 32], F32); s32 = kb.sb("s32", [128, NT, 32], F32)
    r_tab = kb.res("tabs")
    epst = kb.sb("epst", [128, 1], F32); r_eps = kb.res("eps")
    st = [kb.sb(f"st{i}", [128, 40], F32) for i in range(2)]; r_st = [kb.res(f"st{i}") for i in range(2)]
    sx = [kb.sb(f"sx{i}", [128, 4], F32) for i in range(2)]; r_sx = [kb.res(f"sx{i}") for i in range(2)]
    ps_t = [kb.ps(f"ps_t{i}", [128, 1024], BF16) for i in range(2)]; r_ps_t = [kb.res(f"ps_t{i}") for i in range(2)]
    ps_m = [kb.ps(f"ps_m{i}", [128, 512], F32) for i in range(4)]; r_ps_m = [kb.res(f"ps_m{i}") for i in range(4)]
    ps_q = [kb.ps(f"ps_q{i}", [128, 1024], BF16) for i in range(2)]; r_ps_q = [kb.res(f"ps_q{i}") for i in range(2)]

    kb.ms("pool", epst[:], EPS, [r_eps])
    kb.dma("sp", an[:], anorm[:, :], writes=[r_an])
    kb.dma("sp", ga[:], gall[:, :], writes=[r_ga])
    kb.dma("sp", c64[:], ct64.rearrange("(t p) d -> p t d", p=128), writes=[r_tab])
    kb.dma("sp", s64[:], st64.rearrange("(t p) d -> p t d", p=128), writes=[r_tab])
    kb.dma("sp", c32[:], ct32.rearrange("(t p) d -> p t d", p=128), writes=[r_tab])
    kb.dma("sp", s32[:], st32.rearrange("(t p) d -> p t d", p=128), writes=[r_tab])
    wv = w_in.rearrange("(k p) c -> p k c", p=128)
    for k in range(8):
        for c0 in range(0, 2304, 1152):
            kb.dma("pool", wq[:, k, c0:c0 + 1152], wv[:, k, c0:c0 + 1152], writes=[r_wq])

    for t in range(NT):
        b = t % 2
        kb.dma("sp", xt[b][:], x[t * 128:(t + 1) * 128, :], writes=[r_xt[b]])
        kb.act(h16[b][:], xt[b][:], AF.Square, [r_xt[b]], [r_h16[b], r_sx[b]], accum_out=sx[b][:, 0:1])
        kb.act(sx[b][:, 1:2], sx[b][:, 0:1], AF.Ln, [r_sx[b], r_eps], [r_sx[b]], bias=epst[:], scale=1.0 / 1024)
        kb.act(sx[b][:, 2:3], sx[b][:, 1:2], AF.Exp, [r_sx[b]], [r_sx[b]], scale=-0.5)
        kb.stt(h16[b][:], xt[b][:], sx[b][:, 2:3], an[:], ALU.mult, ALU.mult, [r_xt[b], r_sx[b], r_an], [r_h16[b]])
        for k in range(8):
            kb.tr(ps_t[b][:, k * 128:(k + 1) * 128], h16[b][:, k * 128:(k + 1) * 128], idb[:],
                  [r_h16[b], r_id], [r_ps_t[b]], signal=(k == 7))
        kb.cp("act", hT[:, :, t * 128:(t + 1) * 128], ps_t[b][:].rearrange("p (k t) -> p k t", k=8), [r_ps_t[b]], [r_hT[t]])

    mcnt = 0
    for t in range(NT):
        b = t % 2
        for ci, (c0, cw) in enumerate([(0, 512), (512, 512), (1024, 512), (1536, 512), (2048, 256)]):
            pm = mcnt % 4; mcnt += 1
            for k in range(8):
                kb.mm(ps_m[pm][:, 0:cw], hT[:, k, t * 128:(t + 1) * 128], wq[:, k, c0:c0 + cw], k == 0, k == 7,
                      [r_hT[t], r_wq], [r_ps_m[pm]])
            kb.cp("act", pj[b][:, c0:c0 + cw], ps_m[pm][:, 0:cw], [r_ps_m[pm]], [r_pj[b]])
        vo = 0
        for (c0, cw) in VSEG:
            kb.cp("pool", v16[b][:, vo:vo + cw], pj[b][:, c0:c0 + cw], [r_pj[b]], [r_v16[b]])
            vo += cw
        kb.dma("sp", vout[t * 128:(t + 1) * 128, :], v16[b][:], reads=[r_v16[b]])
        so = 0
        for (c0, nh, d) in SEGS:
            kb.act(xs[:, c0:c0 + nh * d], pj[b][:, c0:c0 + nh * d], AF.Square, [r_pj[b]], [r_xs])
            kb.red(st[b][:, so:so + nh], xs[:, c0:c0 + nh * d].rearrange("p (h d) -> p h d", d=d), [r_xs], [r_st[b]])
            so += nh
        kb.act(st[b][:, 0:18], st[b][:, 0:18], AF.Ln, [r_st[b], r_eps], [r_st[b]], bias=epst[:], scale=1.0 / 64)
        kb.act(st[b][:, 18:34], st[b][:, 18:34], AF.Ln, [r_st[b], r_eps], [r_st[b]], bias=epst[:], scale=1.0 / 32)
        kb.act(st[b][:, 0:34], st[b][:, 0:34], AF.Exp, [r_st[b]], [r_st[b]], scale=-0.5)
        so = 0
        for si, (c0, nh, d) in enumerate(SEGS):
            w = nh * d
            e1 = "dve" if si != 1 else "pool"
            pv = pj[b][:, c0:c0 + w].rearrange("p (h d) -> p h d", d=d)
            rb = st[b][:, so:so + nh].rearrange("p (h o) -> p h o", o=1).to_broadcast([128, nh, d])
            kb.tt(e1, pv, pv, rb, ALU.mult, [r_pj[b], r_st[b]], [r_pj[b]])
            kb.tt(e1, pj[b][:, c0:c0 + w], pj[b][:, c0:c0 + w], ga[:, c0:c0 + w], ALU.mult, [r_pj[b], r_ga], [r_pj[b]])
            hd = d // 2
            ctab = (c64 if d == 64 else c32)[:, t, :]
            stab = (s64 if d == 64 else s32)[:, t, :]
            cb = ctab.rearrange("p (o d) -> p o d", o=1).to_broadcast([128, nh, d])
            kb.tt(e1, xc[:, c0:c0 + w].rearrange("p (h d) -> p h d", d=d), pv, cb, ALU.mult, [r_pj[b], r_tab], [r_xc])
            p4 = pj[b][:, c0:c0 + w].rearrange("p (h two e) -> p h two e", two=2, e=hd)
            x4 = xs[:, c0:c0 + w].rearrange("p (h two e) -> p h two e", two=2, e=hd)
            s0 = stab[:, 0:hd].rearrange("p (o d) -> p o d", o=1).to_broadcast([128, nh, hd])
            s1 = stab[:, hd:d].rearrange("p (o d) -> p o d", o=1).to_broadcast([128, nh, hd])
            kb.tt(e1, x4[:, :, 0, :], p4[:, :, 1, :], s0, ALU.mult, [r_pj[b], r_tab], [r_xs])
            kb.tt(e1, x4[:, :, 1, :], p4[:, :, 0, :], s1, ALU.mult, [r_pj[b], r_tab], [r_xs])
            kb.tt(e1, qk16[b][:, c0:c0 + w], xc[:, c0:c0 + w], xs[:, c0:c0 + w], ALU.add, [r_xc, r_xs], [r_qk16[b]])
            so += nh
        for bi, c0 in enumerate(TBLK):
            half = 0 if bi < 8 else 1
            col = (bi % 8) * 128
            kb.tr(ps_q[half][:, col:col + 128], qk16[b][:, c0:c0 + 128], idb[:], [r_qk16[b], r_id], [r_ps_q[half]],
                  signal=(bi == 7 or bi == 12))
        kb.cp("dve", qkTs[b][:, 0:8, :], ps_q[0][:].rearrange("p (k t) -> p k t", k=8), [r_ps_q[0]], [r_qkTs[b]])
        kb.cp("dve", qkTs[b][:, 8:13, :], ps_q[1][:, 0:640].rearrange("p (k t) -> p k t", k=5), [r_ps_q[1]], [r_qkTs[b]])
        kb.dma("sp", qkT[:, t * 128:(t + 1) * 128].rearrange("(k p) t -> p k t", p=128), qkTs[b][:], reads=[r_qkTs[b]])

    gcnt = 0
    for wc in range(6):
        wb = wc % 2
        for k in range(8):
            kb.dma("pool", wg[wb][:, k, :], wv[:, k, 2304 + wc * 512:2304 + (wc + 1) * 512], writes=[r_wg[wb]])
        for cc in range(4):
            for g in range(4):
                pm = mcnt % 4; mcnt += 1
                for k in range(8):
                    kb.mm(ps_m[pm][:], wg[wb][:, k, cc * 128:(cc + 1) * 128], hT[:, k, g * 512:(g + 1) * 512],
                          k == 0, k == 7, [r_wg[wb]] + r_hT[4 * g:4 * g + 4], [r_ps_m[pm]])
                gb = gcnt % 2; gcnt += 1
                kb.act(g16[gb][:], ps_m[pm][:], AF.Sigmoid, [r_ps_m[pm]], [r_g16[gb]])
                row = (wc * 4 + cc) * 128
                kb.dma("sp", gT[row:row + 128, g * 512:(g + 1) * 512], g16[gb][:], reads=[r_g16[gb]])
    return [r_v16[0], r_v16[1], r_qkTs[0], r_qkTs[1], r_g16[0], r_g16[1]]


EPS = 1e-6
NT = 16
NC2 = 22


def emit_F(kb, D, ident, pfx="f"):
    xm, xhalo, mnorm, w_up, convp, w_down, xo = (D[k] for k in ("xm", "xhalo", "mnorm", "w_up", "convp", "w_down", "xo"))
    idb, r_id = ident
    P = pfx
    hT = kb.sb(P + "hT", [128, 8, 2048], BF16); r_hT = [kb.res(P + f"hT{t}") for t in range(NT)]
    hTh = kb.sb(P + "hTh", [128, 8, 8], BF16); r_hTh = kb.res(P + "hTh")
    mT = kb.sb(P + "mT", [128, NC2, 1024], BF16); r_mT = [kb.res(P + f"mT{g}") for g in range(2)]
    wd = kb.sb(P + "wd", [128, NC2, 1024], BF16); r_wd = kb.res(P + "wd")
    wu = [kb.sb(P + f"wu{i}", [128, 8, 2, 128], BF16) for i in range(2)]; r_wu = [kb.res(P + f"wu{i}") for i in range(2)]
    xt = [kb.sb(P + f"xt{i}", [128, 1024], F32) for i in range(2)]; r_xt = [kb.res(P + f"xt{i}") for i in range(2)]
    h16 = [kb.sb(P + f"h16{i}", [128, 1024], BF16) for i in range(2)]; r_h16 = [kb.res(P + f"h16{i}") for i in range(2)]
    an = kb.sb(P + "an", [128, 1024], F32); r_an = kb.res(P + "an")
    cpar = kb.sb(P + "cpar", [128, 44, 4], F32); r_cp = kb.res(P + "cpar")
    epst = kb.sb(P + "epst", [128, 1], F32); r_eps = kb.res(P + "eps")
    sx = [kb.sb(P + f"sx{i}", [128, 4], F32) for i in range(2)]; r_sx = [kb.res(P + f"sx{i}") for i in range(2)]
    ub = [[kb.sb(P + f"ub{i}{s}", [128, 514], F32) for s in range(2)] for i in range(2)]
    r_ub = [[kb.res(P + f"ub{i}{s}") for s in range(2)] for i in range(2)]
    yb = [[kb.sb(P + f"yb{i}{s}", [128, 512], F32) for s in range(2)] for i in range(2)]
    r_yb = [[kb.res(P + f"yb{i}{s}") for s in range(2)] for i in range(2)]
    ob = [kb.sb(P + f"ob{i}", [128, 1024], F32) for i in range(2)]; r_ob = [kb.res(P + f"ob{i}") for i in range(2)]
    ps_t = kb.ps(P + "ps_t", [128, 1024], BF16); r_ps_t = kb.res(P + "ps_t")
    pu = [[kb.ps(P + f"pu{i}{s}", [128, 512], F32) for s in range(2)] for i in range(2)]
    r_pu = [[kb.res(P + f"pu{i}{s}") for s in range(2)] for i in range(2)]
    ph = kb.ps(P + "ph", [128, 16], F32); r_ph = kb.res(P + "ph")
    po = [kb.ps(P + f"po{i}", [128, 512], F32) for i in range(2)]; r_po = [kb.res(P + f"po{i}") for i in range(2)]

    kb.ms("pool", epst[:], EPS, [r_eps])
    kb.dma("sp", an[:], mnorm[:, :], writes=[r_an])
    kb.dma("sp", cpar[:], convp[:, :, :], writes=[r_cp])
    wdv = w_down.rearrange("(c p) n -> p c n", p=128)
    for c in range(NC2):
        kb.dma("pool", wd[:, c, :], wdv[:, c, :], writes=[r_wd])

    for t in range(NT + 1):
        b = t % 2
        n = 128 if t < NT else 8
        src = xm[t * 128:(t + 1) * 128, :] if t < NT else xhalo[:, :]
        kb.dma("sp", xt[b][0:n, :], src, writes=[r_xt[b]])
        kb.act(h16[b][0:n, :], xt[b][0:n, :], AF.Square, [r_xt[b]], [r_h16[b], r_sx[b]], accum_out=sx[b][0:n, 0:1])
        kb.act(sx[b][0:n, 1:2], sx[b][0:n, 0:1], AF.Ln, [r_sx[b], r_eps], [r_sx[b]], bias=epst[0:n, :], scale=1.0 / 1024)
        kb.act(sx[b][0:n, 2:3], sx[b][0:n, 1:2], AF.Exp, [r_sx[b]], [r_sx[b]], scale=-0.5)
        kb.stt(h16[b][0:n, :], xt[b][0:n, :], sx[b][0:n, 2:3], an[0:n, :], ALU.mult, ALU.mult, [r_xt[b], r_sx[b], r_an], [r_h16[b]])
        for k in range(8):
            kb.tr(ps_t[:, k * 128:k * 128 + n], h16[b][0:n, k * 128:(k + 1) * 128], idb[0:n, 0:n],
                  [r_h16[b], r_id], [r_ps_t], signal=(k == 7))
        if t < NT:
            kb.cp("act", hT[:, :, t * 128:(t + 1) * 128], ps_t[:].rearrange("p (k t) -> p k t", k=8), [r_ps_t], [r_hT[t]])
        else:
            kb.cp("act", hTh[:], ps_t[:].rearrange("p (k t) -> p k t", k=8)[:, :, 0:8], [r_ps_t], [r_hTh])

    wuv = w_up.rearrange("(k p) c -> p k c", p=128)
    it = 0
    for half in range(2):
        for c in range(NC2):
            wb = it % 2; it += 1
            for s in range(2):
                col = s * 2816 + c * 128
                kb.dma("pool", wu[wb][:, :, s, :], wuv[:, :, col:col + 128], writes=[r_wu[wb]])
            for s in range(2):
                for k in range(8):
                    kb.mm(ph[:, s * 8:(s + 1) * 8], wu[wb][:, k, s, :], hTh[:, k, :], k == 0, k == 7,
                          [r_wu[wb], r_hTh], [r_ph], signal=(s == 1 and k == 7))
            for gi in range(2):
                g = half * 2 + gi
                ib = (c * 2 + gi) % 2
                for s in range(2):
                    for k in range(8):
                        kb.mm(pu[ib][s][:], wu[wb][:, k, s, :], hT[:, k, g * 512:(g + 1) * 512], k == 0, k == 7,
                              [r_wu[wb]] + r_hT[4 * g:4 * g + 4], [r_pu[ib][s]])
                for s in range(2):
                    ci = s * NC2 + c
                    u, ru = ub[ib][s], r_ub[ib][s]
                    y, ry = yb[ib][s], r_yb[ib][s]
                    kb.cp("dve", u[:, 0:2], ph[:, s * 8 + g * 2:s * 8 + g * 2 + 2], [r_ph], [ru])
                    kb.cp("act", u[:, 2:514], pu[ib][s][:], [r_pu[ib][s]], [ru])
                    kb.act(y[:], pu[ib][s][:], AF.Identity, [r_pu[ib][s], r_cp], [ry], bias=cpar[:, ci, 3:4], scale=cpar[:, ci, 2:3])
                    kb.stt(y[:], u[:, 1:513], cpar[:, ci, 1:2], y[:], ALU.mult, ALU.add, [ru, ry, r_cp], [ry])
                    kb.stt(y[:], u[:, 0:512], cpar[:, ci, 0:1], y[:], ALU.mult, ALU.add, [ru, ry, r_cp], [ry])
                yg, yv = yb[ib][0], yb[ib][1]
                kb.act(yg[:], yg[:], AF.Silu, [r_yb[ib][0]], [r_yb[ib][0]])
                kb.tt("dve", mT[:, c, gi * 512:(gi + 1) * 512], yg[:], yv[:], ALU.mult, [r_yb[ib][0], r_yb[ib][1]], [r_mT[gi]])
        for tt_ in range(8):
            t = half * 8 + tt_
            b = t % 2
            kb.dma("sp", xt[b][:], xm[t * 128:(t + 1) * 128, :], writes=[r_xt[b]])
            for hc in range(2):
                for c in range(NC2):
                    kb.mm(po[hc][:], mT[:, c, tt_ * 128:(tt_ + 1) * 128], wd[:, c, hc * 512:(hc + 1) * 512], c == 0, c == NC2 - 1,
                          [r_mT[tt_ // 4], r_wd], [r_po[hc]])
                kb.tt("dve", ob[b][:, hc * 512:(hc + 1) * 512], po[hc][:], xt[b][:, hc * 512:(hc + 1) * 512], ALU.add,
                      [r_po[hc], r_xt[b]], [r_ob[b]])
            kb.dma("sp", xo[t * 128:(t + 1) * 128, :], ob[b][:], reads=[r_ob[b]])
    return [r_ob[0], r_ob[1]]


NEG = -30000.0
BIG = 30000.0
EPS = 1e-6
NSTREAM = [12, 28, 44, 60]
FAST_RECIP = False


def RECIP(en):
    return en.reciprocal_approx_fast if FAST_RECIP else en.reciprocal


STOP = 99


def emit_A(kb, D, ident, lam_init, pfx="a"):
    P = pfx
    idb, r_id = ident
    qkT, vown_d, kTf, vhp, kTbh, vbh_d = (D[k] for k in ("qkT", "v_own", "kT_full", "v_hp", "kTb_halo", "vb_halo"))

    def S(name, shape, dt):
        return kb.sb(P + name, shape, dt), kb.res(P + name)

    kbuf, r_kbuf = S("kbuf", [128, 2, 8192], BF16)
    vaug, r_vaug = S("vaug", [128, 64, 128], BF16)
    vstg, r_vstg = S("vstg", [128, 64, 64], BF16)
    kown, r_kown = S("kown", [128, 2, 2048], BF16)
    vown, r_vown = S("vown", [128, 16, 128], BF16)
    vostg, r_vostg = S("vostg", [128, 16, 64], BF16)
    qt, r_qt = S("qt", [128, 2, 2048], BF16)
    yT, r_yT = S("yT", [128, 8, 2048], BF16)
    mdiag, r_md = S("mdiag", [128, 4, 512], BF16)
    mB, r_mB = S("mB", [128, 2, 512], BF16)
    pmA, r_pmA = S("pmA", [128, 16, 4, 32], F32)
    hval, r_hval = S("hval", [128, 4], F32)
    lamv, r_lamv = S("lamv", [128, 4, 32], F32)
    lamt, r_lamt = S("lamt", [128, 8], F32)
    sgc, r_sgc = S("sgc", [128, 1], F32)
    sinkt, r_sinkt = S("sinkt", [1, 8], F32)
    sinke, r_sinke = S("sinke", [1, 8], F32)
    sh16, r_sh = S("sh16", [1, 8], BF16)
    sl16, r_sl = S("sl16", [1, 8], BF16)
    shf, r_shf = S("shf", [1, 8], F32)
    sinkrow, r_sinkrow = S("sinkrow", [1, 2, 8, 128], BF16)
    srow, r_srow = S("srow", [1, 128], BF16)
    ones64, r_ones64 = S("ones64", [64, 64], BF16)
    epst, r_eps = S("epst", [128, 1], F32)
    pt = [S(f"pt{i}", [128, 512], BF16) for i in range(4)]
    rcp = [S(f"rcp{i}", [64, 512], F32) for i in range(2)]
    t1, r_t1 = S("t1", [64, 512], F32)
    t2, r_t2 = S("t2", [64, 512], F32)
    sq16, r_sq16 = S("sq16", [64, 512], BF16)
    rs, r_rs = S("rs", [64, 512], F32)
    kmean, r_kmean = S("kmean", [64, 32], F32)
    kmean16, r_km16 = S("kmean16", [64, 32], BF16)
    gm, r_gm = S("gm", [128, 16, 32], F32)
    t8, r_t8 = S("t8", [128, 16, 8], F32)
    sel, r_sel = S("sel", [128, 16, 32], F32)
    bst, r_bst = S("bst", [128, 16, 64], BF16)
    tmpb, r_tmpb = S("tmpb", [128, 16, 32], F32)
    ps_s = [(kb.ps(P + f"ps_s{i}", [128, 512], F32), kb.res(P + f"ps_s{i}")) for i in range(3)]
    ps_o = [(kb.ps(P + f"ps_o{i}", [128, 512], F32), kb.res(P + f"ps_o{i}")) for i in range(2)]
    ps_x = [(kb.ps(P + f"ps_x{i}", [128, 512], F32), kb.res(P + f"ps_x{i}")) for i in range(2)]
    ps_b, r_ps_b = kb.ps(P + "ps_b", [128, 1024], BF16), kb.res(P + "ps_b")

    kb.ms("pool", epst[:], EPS, [r_eps])
    kb.ms("pool", vaug[:, :, 64:128], 1.0, [r_vaug])
    kb.ms("pool", vown[:, :, 64:128], 1.0, [r_vown])
    kb.ms("pool", ones64[:], 1.0, [r_ones64])
    kb.ms("pool", srow[:, 0:64], 0.0, [r_srow])
    kb.ms("pool", srow[:, 64:128], 1.0, [r_srow])
    kb.dma("sp", mdiag[:], D["mdiag"][:, :, :], writes=[r_md])
    kb.dma("sp", mB[:], D["mB"][:, :, :], writes=[r_mB])
    kb.dma("sp", pmA[:], D["pmA"][:, :, :, :], writes=[r_pmA])
    kb.dma("sp", hval[:], D["hval"][:, :], writes=[r_hval])
    kb.dma("sp", lamv[:], D["lamv"][:, :, :], writes=[r_lamv])
    kb.dma("sp", sgc[:], D["sgc"][:, :], writes=[r_sgc])
    kb.dma("sp", sinkt[:], D["sinks"][0:1, :], writes=[r_sinkt])
    sinkf, r_sinkf = S("sinkf", [128, 8], F32)
    esink, r_esink = S("esink", [128, 8], F32)
    kb.dma("sp", sinkf[:], D["sinks"][:, :], writes=[r_sinkf])
    kb.act(esink[:], sinkf[:], AF.Exp, [r_sinkf], [r_esink])
    kb.tt("dve", lamv[:, 0, :], lamv[:, 0, :], lamv[:, 1, :], ALU.mult, [r_lamv], [r_lamv])
    kb.tt("dve", lamv[:, 2, :], lamv[:, 2, :], lamv[:, 3, :], ALU.mult, [r_lamv], [r_lamv])
    kb.red(lamt[:, 0:1], lamv[:, 0, :], [r_lamv], [r_lamt])
    kb.red(lamt[:, 1:2], lamv[:, 2, :], [r_lamv], [r_lamt])
    kb.act(lamt[:, 2:4], lamt[:, 0:2], AF.Exp, [r_lamt], [r_lamt])
    kb.tt("dve", lamt[:, 4:5], lamt[:, 3:4], lamt[:, 2:3], ALU.subtract, [r_lamt], [r_lamt])
    kb.ts("dve", lamt[:, 4:5], lamt[:, 4:5], -float(lam_init), None, ALU.add, None, [r_lamt], [r_lamt])
    kb.act(sinke[:], sinkt[:], AF.Exp, [r_sinkt], [r_sinke])
    kb.cp("dve", sh16[:], sinke[:], [r_sinke], [r_sh])
    kb.cp("dve", shf[:], sh16[:], [r_sh], [r_shf])
    kb.tt("dve", shf[:], sinke[:], shf[:], ALU.subtract, [r_sinke, r_shf], [r_shf])
    kb.cp("dve", sl16[:], shf[:], [r_shf], [r_sl])
    kb.cp("dve", sinkrow[:, 0, :, :], sh16[:].rearrange("p (h o) -> p h o", o=1).to_broadcast([1, 8, 128]), [r_sh], [r_sinkrow])
    kb.cp("dve", sinkrow[:, 1, :, :], sl16[:].rearrange("p (h o) -> p h o", o=1).to_broadcast([1, 8, 128]), [r_sl], [r_sinkrow])

    def _fin():
        kb.dma("sp", D["yT_out"].rearrange("(k p) t -> p k t", p=128), yT[:], reads=[r_yT])
        return [r_yT]

    if STOP <= 0:
        return []
    cnt = {"s": 0, "p": 0, "o": 0, "x": 0}

    def nxt(k, n):
        v = cnt[k] % n
        cnt[k] += 1
        return v

    def attn_tile(kl, ql, vl, acc, first, last, scale, rds, mask=None):
        ps, r_ps = ps_s[nxt("s", 3)]
        kb.mm(ps[:], kl, ql, True, mask is None, rds, [r_ps])
        if mask is not None:
            kb.mm(ps[:], idb[:], mask[0], False, True, [r_id, mask[1]], [r_ps])
        p, r_p = pt[nxt("p", 4)]
        kb.act(p[:], ps[:], AF.Exp, [r_ps], [r_p], scale=scale)
        kb.mm(acc[0][:], vl, p[:], first, last, [r_p] + rds, [acc[1]])

    kb.ms("pool", kbuf[64:128, 0, :], 0.0, [r_kbuf])
    kb.ms("pool", kown[64:96, 0, :], 0.0, [r_kown])
    for h in range(4):
        kb.dma("sp", kbuf[0:64, 0, :], kTf[h * 64:(h + 1) * 64, :], writes=[r_kbuf])
        kb.dma("sp", kbuf[64:96, 0, :], D["ohA"][:, :], writes=[r_kbuf])
        kb.dma("sp", qt[0:64, 0, :], qkT[h * 64:(h + 1) * 64, :], writes=[r_qt])
        kb.dma("sp", vstg[:], vhp[h], writes=[r_vstg])
        kb.cp("pool", vaug[:, :, 0:64], vstg[:], [r_vstg], [r_vaug])
        kb.dma("sp", kown[0:64, 0, :], qkT[256 + h * 64:256 + (h + 1) * 64, :], writes=[r_kown])
        kb.dma("sp", kown[96:128, 0, :], D["ohA_own"][:, :], writes=[r_kown])
        kb.dma("sp", vostg[:], vown_d[:, h * 64:(h + 1) * 64].rearrange("(c p) d -> p c d", p=128), writes=[r_vostg])
        kb.cp("pool", vown[:, :, 0:64], vostg[:], [r_vostg], [r_vown])
        kb.red(kmean[:], kbuf[0:64, 0, :].rearrange("p (n l) -> p n l", l=256), [r_kbuf], [r_kmean])
        kb.ts("dve", kmean16[:], kmean[:], 1.0 / 256, None, ALU.mult, None, [r_kmean], [r_km16])
        pg, r_pg = ps_x[nxt("x", 2)]
        for c in range(16):
            kb.mm(pg[:, c * 32:(c + 1) * 32], qt[0:64, 0, c * 128:(c + 1) * 128], kmean16[:], True, True,
                  [r_qt, r_km16], [r_pg], signal=(c == 15))
        kb.tt("dve", gm[:], pg[:].rearrange("p (c n) -> p c n", n=32), pmA[:, :, 0, :], ALU.add, [r_pg, r_pmA], [r_gm])
        for c in range(16):
            kb.op("dve", (lambda c: lambda en: en.max(out=t8[:, c, :], in_=gm[:, c, :]))(c), [r_gm], [r_t8])
        kb.tt("dve", sel[:], gm[:], t8[:, :, 2:3].to_broadcast([128, 16, 32]), ALU.is_ge, [r_gm, r_t8], [r_sel])
        for which in range(2):
            kb.tt("dve", tmpb[:], sel[:], pmA[:, :, 1 + which, :], ALU.mult, [r_sel, r_pmA], [r_tmpb])
            if which == 1:
                kb.tt("dve", tmpb[:], tmpb[:], pmA[:, :, 3, :], ALU.add, [r_tmpb, r_pmA], [r_tmpb])
            kb.ts("dve", bst[:, :, which * 32:(which + 1) * 32], tmpb[:], BIG, -BIG, ALU.mult, ALU.add, [r_tmpb], [r_bst])
        for hf in range(2):
            for c8 in range(8):
                c = hf * 8 + c8
                kb.tr(ps_b[0:64, c8 * 128:(c8 + 1) * 128], bst[:, c, :], idb[:], [r_bst, r_id], [r_ps_b], signal=(c8 == 7))
            kb.cp("dve", qt[64:128, 0, hf * 1024:(hf + 1) * 1024], ps_b[0:64, :], [r_ps_b], [r_qt])
        if STOP <= 1:
            return _fin()
        for i in range(4):
            acc = ps_o[nxt("o", 2)]
            qs = qt[:, 0, i * 512:(i + 1) * 512]
            for kt in range(NSTREAM[i]):
                attn_tile(kbuf[:, 0, kt * 128:(kt + 1) * 128], qs, vaug[:, kt, :], acc, kt == 0, False, 0.125,
                          [r_kbuf, r_qt, r_vaug])
            for t in range(4):
                c = 4 * i + t
                attn_tile(kown[:, 0, c * 128:(c + 1) * 128], qs, vown[:, c, :], acc, False, t == 3, 0.125,
                          [r_kown, r_qt, r_vown], mask=(mdiag[:, t, :], r_md))
            rc, r_rc = rcp[nxt("x", 2)]
            kb.op("dve", (lambda rc, acc: lambda en: RECIP(en)(out=rc[:], in_=acc[0][64:128, :]))(rc, acc), [acc[1]], [r_rc])
            kb.tt("dve", yT[(h % 2) * 64:(h % 2) * 64 + 64, h // 2, i * 512:(i + 1) * 512], acc[0][0:64, :], rc[:], ALU.mult,
                  [acc[1], r_rc], [r_yT])

    if STOP <= 2:
        return _fin()
    sc_c = float(32 ** -0.5)
    for m in range(2):
        kb.dma("sp", kbuf[32:48, m, :], D["ohC"][:, :], writes=[r_kbuf])
        kb.dma("sp", qt[32:48, m, :], D["cbC"][:, :], writes=[r_qt])
    for h in range(4):
        for m in range(2):
            r0 = 384 + h * 64 + m * 32
            kb.dma("sp", kbuf[0:32, m, :], kTf[r0:r0 + 32, :], writes=[r_kbuf])
            q0 = 1152 + h * 64 + m * 32
            kb.dma("sp", qt[0:32, m, :], qkT[q0:q0 + 32, :], writes=[r_qt])
            k0 = 1408 + h * 64 + m * 32
            kb.dma("sp", kown[0:32, m, :], qkT[k0:k0 + 32, :], writes=[r_kown])
        kb.dma("sp", vstg[:], vhp[6 + h], writes=[r_vstg])
        kb.cp("pool", vaug[:, :, 0:64], vstg[:], [r_vstg], [r_vaug])
        kb.dma("sp", vostg[:], vown_d[:, 384 + h * 64:384 + (h + 1) * 64].rearrange("(c p) d -> p c d", p=128), writes=[r_vostg])
        kb.cp("pool", vown[:, :, 0:64], vostg[:], [r_vostg], [r_vown])
        for i in range(4):
            accs = [ps_o[0], ps_o[1]]
            for kt in range(NSTREAM[i]):
                for m in range(2):
                    attn_tile(kbuf[0:48, m, kt * 128:(kt + 1) * 128], qt[0:48, m, i * 512:(i + 1) * 512], vaug[:, kt, :],
                              accs[m], kt == 0, False, sc_c, [r_kbuf, r_qt, r_vaug])
            for t in range(4):
                c = 4 * i + t
                for m in range(2):
                    attn_tile(kown[0:32, m, c * 128:(c + 1) * 128], qt[0:32, m, i * 512:(i + 1) * 512], vown[:, c, :],
                              accs[m], False, t == 3, sc_c, [r_kown, r_qt, r_vown], mask=(mdiag[:, t, :], r_md))
            for m in range(2):
                kb.op("dve", (lambda m: lambda en: RECIP(en)(out=rcp[m][0][:], in_=accs[m][0][64:128, :]))(m), [accs[m][1]], [rcp[m][1]])
            kb.tt("dve", t1[:], accs[0][0][0:64, :], rcp[0][0][:], ALU.mult, [accs[0][1], rcp[0][1]], [r_t1])
            kb.tt("dve", t2[:], accs[1][0][0:64, :], rcp[1][0][:], ALU.mult, [accs[1][1], rcp[1][1]], [r_t2])
            kb.stt(t1[:], t2[:], lamt[0:64, 4:5], t1[:], ALU.mult, ALU.add, [r_t1, r_t2, r_lamt], [r_t1])
            kb.act(sq16[:], t1[:], AF.Square, [r_t1], [r_sq16])
            px, r_px = ps_x[nxt("x", 2)]
            kb.mm(px[0:64, :], ones64[:], sq16[:], True, True, [r_ones64, r_sq16], [r_px])
            kb.act(rs[:], px[0:64, :], AF.Ln, [r_px, r_eps], [r_rs], bias=epst[0:64, :], scale=1.0 / 64)
            kb.act(rs[:], rs[:], AF.Exp, [r_rs], [r_rs], scale=-0.5)
            kb.tt("dve", t1[:], t1[:], rs[:], ALU.mult, [r_t1, r_rs], [r_t1])
            kb.ts("dve", yT[(h % 2) * 64:(h % 2) * 64 + 64, 6 + h // 2, i * 512:(i + 1) * 512], t1[:], sgc[0:64, :], float(1.0 - lam_init),
                  ALU.mult, ALU.mult, [r_t1, r_sgc], [r_yT])

    if STOP <= 3:
        return _fin()
    qb = qt
    for k in range(2):
        qv = kbuf[0:64, 0, :].rearrange("p (g t) -> p g t", g=4)
        kb.dma("sp", qv, qkT[512 + k * 256:512 + (k + 1) * 256, :].rearrange("(g d) t -> d g t", d=64), writes=[r_kbuf])
        kb.dma("sp", kown[0:64, 0, :], qkT[1024 + k * 64:1024 + (k + 1) * 64, :], writes=[r_kown])
        kb.dma("sp", kown[0:64, 1, 0:512], kTbh[k * 64:(k + 1) * 64, :], writes=[r_kown])
        kb.dma("sp", vostg[:], vown_d[:, 256 + k * 64:256 + (k + 1) * 64].rearrange("(c p) d -> p c d", p=128), writes=[r_vostg])
        kb.cp("pool", vown[:, :, 0:64], vostg[:], [r_vostg], [r_vown])
        kb.dma("sp", vstg[:, 0:4, :], vbh_d[:, k * 64:(k + 1) * 64].rearrange("(s p) d -> p s d", p=128), writes=[r_vstg])
        kb.cp("pool", vaug[:, 0:4, 0:64], vstg[:, 0:4, :], [r_vstg], [r_vaug])
        if k == 0:
            kb.ms("pool", vaug[:, 4:8, 64:128], 1.0, [r_vaug])
        for s in range(4):
            kb.ts("dve", vaug[:, s, 0:64], vaug[:, s, 0:64], hval[:, s:s + 1], None, ALU.mult, None, [r_vaug, r_hval], [r_vaug])
            kb.ts("dve", vaug[:, s, 64:128], vaug[:, 4 + s, 64:128], hval[:, s:s + 1], None, ALU.mult, None, [r_vaug, r_hval], [r_vaug])
        for c in range(16):
            s = c // 4
            acc = ps_o[nxt("o", 2)]
            qs = qv[:, :, c * 128:(c + 1) * 128]
            if c % 4 == 0:
                kprev, vprev = kown[0:64, 1, s * 128:(s + 1) * 128], vaug[:, s, :]
            else:
                kprev, vprev = kown[0:64, 0, (c - 1) * 128:c * 128], vown[:, c - 1, :]
            rds = [r_kown, r_kbuf, r_vown, r_vaug]

            def b_tile(kl, vl, first, last, maskap):
                ps, r_ps = ps_s[nxt("s", 3)]
                kb.mm(ps[:], idb[:], maskap, True, False, [r_id, r_mB], [r_ps], signal=False)
                for gi in range(4):
                    kb.mm(ps[:, gi * 128:(gi + 1) * 128], kl, qv[:, gi, c * 128:(c + 1) * 128], False, gi == 3, rds, [r_ps])
                p, r_p = pt[nxt("p", 4)]
                kb.act(p[:], ps[:], AF.Exp, [r_ps], [r_p], scale=0.125)
                kb.mm(acc[0][:], vl, p[:], first, last, [r_p] + rds, [acc[1]])

            b_tile(kprev, vprev, True, False, mB[:, 0, :])
            b_tile(kown[0:64, 0, c * 128:(c + 1) * 128], vown[:, c, :], False, True, mB[:, 1, :])
            rc, r_rc = rcp[nxt("x", 2)]
            kb.cp("dve", t2[:], acc[0][64:128, :], [acc[1]], [r_t2])
            for gi in range(4):
                hh = 4 * k + gi
                kb.ts("dve", t2[:, gi * 128:(gi + 1) * 128], t2[:, gi * 128:(gi + 1) * 128], esink[0:64, hh:hh + 1], None, ALU.add, None,
                      [r_t2, r_esink], [r_t2])
            kb.op("dve", (lambda rc: lambda en: en.reciprocal(out=rc[:], in_=t2[:]))(rc), [r_t2], [r_rc])
            for gi in range(4):
                hh = 4 * k + gi
                kb.tt("dve", yT[(hh % 2) * 64:(hh % 2) * 64 + 64, 2 + hh // 2, c * 128:(c + 1) * 128],
                      acc[0][0:64, gi * 128:(gi + 1) * 128], rc[:, gi * 128:(gi + 1) * 128], ALU.mult, [acc[1], r_rc], [r_yT])
    return _fin()


def emit_M(kb, D, pfx="m"):
    P = pfx
    gT, x, xmid = D["gT"], D["x"], D["xmid"]

    def S(name, shape, dt):
        return kb.sb(P + name, shape, dt), kb.res(P + name)

    cnt = {"x": 0}

    def nxt(k, n):
        v = cnt[k] % n
        cnt[k] += 1
        return v

    ps_x = [(kb.ps(P + f"ps_x{i}", [128, 512], F32), kb.res(P + f"ps_x{i}")) for i in range(4)]
    yT, r_yT = S("yT", [128, 8, 2048], BF16)
    kb.dma("sp", yT[:], D["yT_in"].rearrange("(k p) t -> p k t", p=128), writes=[r_yT])
    wp, r_wp = S("wp", [128, 8, 1024], BF16)
    wo, r_wo = S("wo", [128, 8, 1024], BF16)
    for nm, k0, nk in (("w_pa", 0, 2), ("w_pb", 2, 4), ("w_pc", 6, 2)):
        wv = D[nm].rearrange("(k p) n -> p k n", p=128)
        for k in range(nk):
            kb.dma("pool", wp[:, k0 + k, :], wv[:, k, :], writes=[r_wp])
    wov = D["w_out"].rearrange("(k p) n -> p k n", p=128)
    for k in range(8):
        kb.dma("pool", wo[:, k, :], wov[:, k, :], writes=[r_wo])
    gt = [S(f"gt{i}", [128, 3, 512], BF16) for i in range(2)]
    macc, r_macc = S("macc", [128, 512], F32)
    mtmp, r_mtmp = S("mtmp", [128, 512], F32)
    mT, r_mT = S("mT", [128, 8, 512], BF16)
    xt = [S(f"xt{i}", [128, 1024], F32) for i in range(2)]
    ob = [S(f"ob{i}", [128, 1024], F32) for i in range(2)]
    gTv = gT.rearrange("(br f) t -> f br t", br=3)
    BR = ((0, 2), (2, 4), (6, 2))
    gi_ = 0
    for g in range(4):
        for fc in range(8):
            gtt, r_gtt = gt[gi_ % 2]; gi_ += 1
            kb.dma("sp", gtt[:], gTv[fc * 128:(fc + 1) * 128, :, g * 512:(g + 1) * 512], writes=[r_gtt])
            for br, (k0, nk) in enumerate(BR):
                px, r_px = ps_x[nxt("x", 4)]
                for k in range(nk):
                    kb.mm(px[:], wp[:, k0 + k, fc * 128:(fc + 1) * 128], yT[:, k0 + k, g * 512:(g + 1) * 512], k == 0, k == nk - 1,
                          [r_wp, r_yT], [r_px])
                if br == 0:
                    kb.tt("dve", macc[:], px[:], gtt[:, 0, :], ALU.mult, [r_px, r_gtt], [r_macc])
                else:
                    kb.tt("dve", mtmp[:], px[:], gtt[:, br, :], ALU.mult, [r_px, r_gtt], [r_mtmp])
                    if br == 1:
                        kb.tt("dve", macc[:], macc[:], mtmp[:], ALU.add, [r_macc, r_mtmp], [r_macc])
                    else:
                        kb.tt("dve", mT[:, fc, :], macc[:], mtmp[:], ALU.add, [r_macc, r_mtmp], [r_mT])
        for tt_ in range(4):
            t = g * 4 + tt_
            b = t % 2
            kb.dma("sp", xt[b][0][:], x[t * 128:(t + 1) * 128, :], writes=[xt[b][1]])
            for hc in range(2):
                px, r_px = ps_x[nxt("x", 4)]
                for fc in range(8):
                    kb.mm(px[:], mT[:, fc, tt_ * 128:(tt_ + 1) * 128], wo[:, fc, hc * 512:(hc + 1) * 512], fc == 0, fc == 7,
                          [r_mT, r_wo], [r_px])
                kb.tt("dve", ob[b][0][:, hc * 512:(hc + 1) * 512], px[:], xt[b][0][:, hc * 512:(hc + 1) * 512], ALU.add,
                      [r_px, xt[b][1]], [ob[b][1]])
            kb.dma("sp", xmid[t * 128:(t + 1) * 128, :], ob[b][0][:], reads=[ob[b][1]])
    return [ob[0][1], ob[1][1]]


GROUPS = [[j, 7 - j, 8 + j, 15 - j] for j in range(4)]
S = 8192
BF = ml_dtypes.bfloat16


def core_pos(j):
    return np.concatenate([np.arange(g * 512, (g + 1) * 512) for g in GROUPS[j]])


def rope_tabs(pos, dim):
    inv = (1.0 / (np.float32(10000.0) ** (np.arange(0, dim, 2, dtype=np.float32) / np.float32(dim)))).astype(np.float32)
    ang = pos.astype(np.float32)[:, None] * inv[None, :]
    c = np.cos(ang).astype(np.float32)
    s = np.sin(ang).astype(np.float32)
    ct = np.concatenate([c, c], axis=1)
    st = np.concatenate([-s, s], axis=1)
    return np.ascontiguousarray(ct), np.ascontiguousarray(st)


def rep128(v):
    return np.ascontiguousarray(np.broadcast_to(np.asarray(v, np.float32)[None, :], (128, v.shape[-1])))


def p_inputs(x_core, l, j, inp):
    pos = core_pos(j)
    ct64, st64 = rope_tabs(pos, 64)
    ct32, st32 = rope_tabs(pos, 32)
    gall = np.ones((2048,), np.float32)
    gall[0:256] = np.tile(inp["qn_a"][l], 4)
    gall[256:512] = np.tile(inp["kn_a"][l], 4)
    gall[768:1280] = np.tile(inp["qn_b"][l], 8)
    gall[1280:1408] = np.tile(inp["kn_b"][l], 2)
    gall[1536:1792] = np.tile(inp["qn_c"][l], 8)
    gall[1792:2048] = np.tile(inp["kn_c"][l], 8)
    return {"x": np.ascontiguousarray(x_core), "anorm": rep128(inp["attn_norm"][l]),
            "w_in": np.ascontiguousarray(inp["w_in"][l]), "gall": rep128(gall),
            "ct64": ct64, "st64": st64, "ct32": ct32, "st32": st32}


def f_inputs(xm_core, xm_full_b, l, j, inp):
    halo = np.zeros((8, 1024), np.float32)
    for s, g in enumerate(GROUPS[j]):
        if g > 0:
            halo[2 * s:2 * s + 2] = xm_full_b[g * 512 - 2:g * 512]
    cw = inp["conv_w"][l]
    cb = inp["conv_b"][l]
    convp = np.zeros((128, 44, 4), np.float32)
    convp[:, :, 0:3] = cw.T.reshape(44, 128, 3).transpose(1, 0, 2)
    convp[:, :, 3] = cb.reshape(44, 128).T
    return {"xm": np.ascontiguousarray(xm_core), "xhalo": halo, "mnorm": rep128(inp["mlp_norm"][l]),
            "w_up": np.ascontiguousarray(inp["w_up"][l]), "convp": convp,
            "w_down": np.ascontiguousarray(inp["w_down"][l])}


NEGV = -30000.0


def a_consts(j):
    gl = GROUPS[j]
    f32 = np.float32
    mdiag = np.zeros((128, 4, 512), f32)
    p = np.arange(128)[:, None]
    f = np.arange(512)[None, :]
    for t in range(4):
        mdiag[:, t, :] = np.where(t * 128 + p > f, NEGV, 0.0)
    mB = np.zeros((128, 2, 512), f32)
    fi = (np.arange(512) % 128)[None, :]
    mB[:, 0, :] = np.where(p <= fi, NEGV, 0.0)
    mB[:, 1, :] = np.where(p > fi, NEGV, 0.0)
    pmA = np.zeros((128, 16, 4, 32), f32)
    n = np.arange(32)
    for c in range(16):
        g = gl[c // 4]
        own = 2 * g + (c % 4) // 2
        pmA[:, c, 0, :] = np.where(n < own, 0.0, -1e30)[None, :]
        pmA[:, c, 1, :] = (n < 2 * g).astype(f32)[None, :]
        pmA[:, c, 2, :] = ((n >= 2 * g) & (n < own)).astype(f32)[None, :]
        pmA[:, c, 3, :] = (n == own).astype(f32)[None, :]
    keys = np.arange(S)
    ohA = (keys[None, :] // 256 == np.arange(32)[:, None]).astype(f32)
    ohC = (keys[None, :] // 512 == np.arange(16)[:, None]).astype(f32)
    pos = core_pos(j)
    ohA_own = (pos[None, :] // 256 == np.arange(32)[:, None]).astype(f32)
    cbC = np.where(np.arange(16)[:, None] < (pos[None, :] // 512), 0.0, NEGV).astype(f32)
    hval = np.zeros((128, 4), f32)
    for s, g in enumerate(gl):
        hval[:, s] = 1.0 if g > 0 else 0.0
    return {"mdiag": mdiag.astype(BF), "mB": mB.astype(BF), "pmA": pmA, "ohA": ohA.astype(BF), "ohC": ohC.astype(BF),
            "ohA_own": ohA_own.astype(BF), "cbC": cbC.astype(BF), "hval": hval}


def gather_kv(qkT_list, v_list):
    kT = np.zeros((640, S), BF)
    vf = np.zeros((S, 640), BF)
    for j in range(4):
        pos = core_pos(j)
        q = qkT_list[j]
        kT[0:256, pos] = q[256:512]
        kT[256:384, pos] = q[1024:1152]
        kT[384:640, pos] = q[1408:1664]
        vf[pos] = v_list[j]
    return kT, vf


def a_inputs(j, l, qkT_own, v_own, kT_full, v_full, gT_own, x_core, inp):
    d = dict(a_consts(j))
    d["qkT"] = np.ascontiguousarray(qkT_own)
    d["v_own"] = np.ascontiguousarray(v_own)
    d["kT_full"] = np.ascontiguousarray(kT_full)
    d["v_hp"] = np.ascontiguousarray(v_full.reshape(64, 128, 10, 64).transpose(2, 1, 0, 3))
    kh = np.zeros((128, 512), BF)
    vh = np.zeros((512, 128), BF)
    for s, g in enumerate(GROUPS[j]):
        if g > 0:
            kh[:, s * 128:(s + 1) * 128] = kT_full[256:384, g * 512 - 128:g * 512]
            vh[s * 128:(s + 1) * 128, :] = v_full[g * 512 - 128:g * 512, 256:384]
    d["kTb_halo"] = kh
    d["vb_halo"] = vh
    d["gT"] = np.ascontiguousarray(gT_own)
    d["x"] = np.ascontiguousarray(x_core)
    lamv = np.stack([inp["lam_q1"][l], inp["lam_k1"][l], inp["lam_q2"][l], inp["lam_k2"][l]]).astype(np.float32)
    d["lamv"] = np.ascontiguousarray(np.broadcast_to(lamv[None], (128, 4, 32)))
    d["sgc"] = np.ascontiguousarray(np.tile(inp["subln"][l], 2).reshape(128, 1).astype(np.float32))
    d["sinks"] = rep128(inp["sinks"][l])
    for nm in ("w_pa", "w_pb", "w_pc", "w_out"):
        d[nm] = np.ascontiguousarray(inp[nm][l])
    return d


A_IN_SPECS = [("qkT", [1664, 2048], "bf"), ("v_own", [2048, 640], "bf"), ("kT_full", [640, 8192], "bf"),
              ("v_hp", [10, 128, 64, 64], "bf"), ("kTb_halo", [128, 512], "bf"), ("vb_halo", [512, 128], "bf"),
              ("gT", [3072, 2048], "bf"), ("x", [2048, 1024], "f"), ("mdiag", [128, 4, 512], "bf"), ("mB", [128, 2, 512], "bf"),
              ("pmA", [128, 16, 4, 32], "f"), ("ohA", [32, 8192], "bf"), ("ohC", [16, 8192], "bf"), ("ohA_own", [32, 2048], "bf"),
              ("cbC", [16, 2048], "bf"), ("hval", [128, 4], "f"), ("lamv", [128, 4, 32], "f"), ("sgc", [128, 1], "f"),
              ("sinks", [128, 8], "f"), ("w_pa", [256, 1024], "f"), ("w_pb", [512, 1024], "f"), ("w_pc", [256, 1024], "f"),
              ("w_out", [1024, 1024], "f")]


def _make_ident(kb):
    idb = kb.sb("idb", [128, 128], BF16)
    r_id = kb.res("idb")
    idf = kb.sb("idf", [128, 128], F32)
    kb.op("pool", lambda e: e.memset(idf[:], 0.0), writes=[r_id])
    kb.op("pool", lambda e: e.affine_select(out=idf[:], in_=idf[:], pattern=[[-1, 128]], compare_op=ALU.not_equal,
                                            fill=1.0, base=0, channel_multiplier=1), reads=[r_id], writes=[r_id])
    kb.op("pool", lambda e: e.tensor_copy(out=idb[:], in_=idf[:]), reads=[r_id], writes=[r_id])
    return idb, r_id


def _build(kind, lam_init=0.0):
    nc = bass.Bass("TRN2", target_bir_lowering=False)

    def di(n, s, dt=F32):
        return nc.dram_tensor(n, s, dt, kind="ExternalInput").ap()

    def do(n, s, dt):
        return nc.dram_tensor(n, s, dt, kind="ExternalOutput").ap()

    with ExitStack() as st:
        kb = KB(nc, st)
        if kind == "P":
            D = {"x": di("x", [2048, 1024]), "anorm": di("anorm", [128, 1024]), "w_in": di("w_in", [1024, 5376]),
                 "gall": di("gall", [128, 2048]), "ct64": di("ct64", [2048, 64]), "st64": di("st64", [2048, 64]),
                 "ct32": di("ct32", [2048, 32]), "st32": di("st32", [2048, 32]),
                 "qkT": do("qkT", [1664, 2048], BF16), "v": do("v", [2048, 640], BF16), "gT": do("gT", [3072, 2048], BF16)}
            fin = emit_P(kb, D, _make_ident(kb))
        elif kind == "A":
            D = {}
            for n, s, dt in A_IN_SPECS:
                if n in ("gT", "x", "w_pa", "w_pb", "w_pc", "w_out"):
                    continue
                D[n] = di(n, s, BF16 if dt == "bf" else F32)
            D["yT_out"] = do("yT_out", [1024, 2048], BF16)
            fin = emit_A(kb, D, _make_ident(kb), lam_init)
        elif kind == "M":
            D = {"yT_in": di("yT_in", [1024, 2048], BF16), "gT": di("gT", [3072, 2048], BF16), "x": di("x", [2048, 1024]),
                 "w_pa": di("w_pa", [256, 1024]), "w_pb": di("w_pb", [512, 1024]), "w_pc": di("w_pc", [256, 1024]),
                 "w_out": di("w_out", [1024, 1024]), "xmid": do("xmid", [2048, 1024], F32)}
            fin = emit_M(kb, D)
        else:
            D = {"xm": di("xm", [2048, 1024]), "xhalo": di("xhalo", [8, 1024]), "mnorm": di("mnorm", [128, 1024]),
                 "w_up": di("w_up", [1024, 5632]), "convp": di("convp", [128, 44, 4]), "w_down": di("w_down", [2816, 1024]),
                 "xo": do("xo", [2048, 1024], F32)}
            fin = emit_F(kb, D, _make_ident(kb))
        kb.finish(fin)
    return nc


def _run(nc, in_maps):
    res = run_bass_kernel_spmd(nc, in_maps, core_ids=list(range(8)))
    return res.results


def kernel(**inp):
    inp = {k: np.asarray(v) for k, v in inp.items()}
    x = inp["x"].astype(np.float32)
    cores = [(c // 4, c % 4) for c in range(8)]
    xs = [np.ascontiguousarray(x[b][core_pos(j)]) for b, j in cores]
    for l in range(2):
        lam_init = 0.8 - 0.6 * float(np.exp(-0.3 * l))
        rp = _run(_build("P"), [p_inputs(xs[c], l, cores[c][1], inp) for c in range(8)])
        full = {}
        for b in range(2):
            full[b] = gather_kv([rp[4 * b + j]["qkT"] for j in range(4)], [rp[4 * b + j]["v"] for j in range(4)])
        a_maps = []
        for c, (b, j) in enumerate(cores):
            d = a_inputs(j, l, rp[c]["qkT"], rp[c]["v"], full[b][0], full[b][1], rp[c]["gT"], xs[c], inp)
            for k in ("gT", "x", "w_pa", "w_pb", "w_pc", "w_out"):
                d.pop(k)
            a_maps.append(d)
        ra = _run(_build("A", lam_init), a_maps)
        m_maps = []
        for c in range(8):
            d = {"yT_in": ra[c]["yT_out"], "gT": rp[c]["gT"], "x": xs[c]}
            for nm in ("w_pa", "w_pb", "w_pc", "w_out"):
                d[nm] = np.ascontiguousarray(inp[nm][l])
            m_maps.append(d)
        rm = _run(_build("M"), m_maps)
        xm_full = np.zeros((2, S, 1024), np.float32)
        for c, (b, j) in enumerate(cores):
            xm_full[b][core_pos(j)] = rm[c]["xmid"]
        rf = _run(_build("F"), [f_inputs(rm[c]["xmid"], xm_full[cores[c][0]], l, cores[c][1], inp) for c in range(8)])
        xs = [np.ascontiguousarray(rf[c]["xo"]) for c in range(8)]
    out = np.zeros((2, S, 1024), np.float32)
    for c, (b, j) in enumerate(cores):
        out[b][core_pos(j)] = xs[c]
    return out
```

```python
import numpy as np
import ml_dtypes
from contextlib import ExitStack
import concourse.bass as bass
import concourse.mybir as mybir
from concourse.bass_utils import run_bass_kernel_spmd


F32 = mybir.dt.float32
BF16 = mybir.dt.bfloat16
AF = mybir.ActivationFunctionType
ALU = mybir.AluOpType
AX = mybir.AxisListType


class Ev:
    __slots__ = ("sem", "val")

    def __init__(self, sem, val):
        self.sem = sem
        self.val = val


class Res:
    def __init__(self, name):
        self.name = name
        self.w = None
        self.r = []
        self.dsem = None
        self.dcnt = 0


class KB:
    ENGS = ("pe", "dve", "act", "pool", "sp")

    def __init__(self, nc, stack):
        self.nc = nc
        self.stack = stack
        self.eng = {"pe": nc.tensor, "dve": nc.vector, "act": nc.scalar,
                    "pool": nc.gpsimd, "sp": nc.sync}
        self.sem = {e: stack.enter_context(nc.semaphore("s_" + e)) for e in self.ENGS}
        self.cnt = {e: 0 for e in self.ENGS}
        self.seen = {e: {} for e in self.ENGS}
        self.stream = {e: [] for e in self.ENGS}
        self.nsem = len(self.ENGS)
        self.allres = []

    def res(self, name):
        r = Res(name)
        self.allres.append(r)
        return r

    def sb(self, name, shape, dt):
        t = self.stack.enter_context(self.nc.sbuf_tensor(name, list(shape), dt))
        return t

    def ps(self, name, shape, dt):
        return self.stack.enter_context(self.nc.psum_tensor(name, list(shape), dt))

    def _waits(self, e, reads, writes, nosame):
        need = {}

        def add(ev, r):
            if ev is None:
                return
            if r in nosame and ev.sem is self.sem[e]:
                return
            k = id(ev.sem)
            if k not in need or need[k][1] < ev.val:
                need[k] = (ev.sem, ev.val)

        for r in reads:
            add(r.w, r)
        for w in writes:
            add(w.w, w)
            for ev in w.r:
                add(ev, w)
        out = []
        seen = self.seen[e]
        for k, (s, v) in need.items():
            if seen.get(k, 0) >= v:
                continue
            seen[k] = v
            out.append((s, v))
        return out

    def op(self, e, fn, reads=(), writes=(), signal=True, nosame=()):
        waits = self._waits(e, reads, writes, nosame)
        if signal:
            self.cnt[e] += 1
            ev = Ev(self.sem[e], self.cnt[e])
            sig = (self.sem[e], 1)
        else:
            ev = Ev(self.sem[e], self.cnt[e] + 1)
            sig = None
        self.stream[e].append((waits, fn, sig))
        for r in reads:
            r.r.append(ev)
        for w in writes:
            w.w = ev
            w.r = []
        return ev

    def dma(self, q, out_ap, in_ap, reads=(), writes=(), **kw):
        tr = (list(writes) + list(reads))[0]
        if tr.dsem is None:
            tr.dsem = self.stack.enter_context(self.nc.semaphore("d_" + tr.name))
            self.nsem += 1
        waits = [w for w in self._waits(q, reads, writes, ()) if w[0] is not tr.dsem]
        tr.dcnt += 16
        ev = Ev(tr.dsem, tr.dcnt)
        self.stream[q].append(
            (waits, lambda en: en.dma_start(out=out_ap, in_=in_ap, **kw), (tr.dsem, 16)))
        for r in reads:
            r.r.append(ev)
        for w in writes:
            w.w = ev
            w.r = []
        return ev

    def finish(self, final_res):
        waits = self._waits("sp", final_res, final_res, ())
        self.stream["sp"].append((waits, None, None))
        nc = self.nc
        with nc.Block() as block:
            def mk(e):
                def body(en):
                    for waits, fn, sig in self.stream[e]:
                        for s, v in waits:
                            en.wait_ge(s, v)
                        if fn is None:
                            continue
                        ins = fn(en)
                        if sig is not None:
                            ins.then_inc(sig[0], sig[1])
                return body
            block.tensor(mk("pe"))
            block.vector(mk("dve"))
            block.scalar(mk("act"))
            block.gpsimd(mk("pool"))
            block.sync(mk("sp"))


def _mk(KBc):
    def tt(self, e, out, in0, in1, op, reads, writes):
        return self.op(e, lambda en: en.tensor_tensor(out=out, in0=in0, in1=in1, op=op), reads, writes)

    def ts(self, e, out, in0, s1, s2, op0, op1=None, reads=(), writes=()):
        if op1 is None:
            return self.op(e, lambda en: en.tensor_scalar(out=out, in0=in0, scalar1=s1, scalar2=None, op0=op0), reads, writes)
        return self.op(e, lambda en: en.tensor_scalar(out=out, in0=in0, scalar1=s1, scalar2=s2, op0=op0, op1=op1), reads, writes)

    def stt(self, out, in0, scalar, in1, op0, op1, reads, writes):
        return self.op("dve", lambda en: en.scalar_tensor_tensor(out=out, in0=in0, scalar=scalar, in1=in1, op0=op0, op1=op1), reads, writes)

    def cp(self, e, out, in_, reads, writes):
        if e == "act":
            return self.op(e, lambda en: en.activation(out=out, in_=in_, func=AF.Copy), reads, writes)
        return self.op(e, lambda en: en.tensor_copy(out=out, in_=in_), reads, writes)

    def act(self, out, in_, func, reads, writes, bias=None, scale=None, accum_out=None):
        kw = {}
        if bias is not None:
            kw["bias"] = bias
        if scale is not None:
            kw["scale"] = scale
        if accum_out is not None:
            kw["accum_out"] = accum_out
        return self.op("act", lambda en: en.activation(out=out, in_=in_, func=func, **kw), reads, writes)

    def mm(self, out, lhsT, rhs, start, stop, reads, writes, signal=None):
        if signal is None:
            signal = stop
        return self.op("pe", lambda en: en.matmul(out, lhsT=lhsT, rhs=rhs, start=start, stop=stop),
                       reads, writes, signal=signal, nosame=writes)

    def tr(self, out, in_, ident, reads, writes, signal=True):
        return self.op("pe", lambda en: en.transpose(out=out, in_=in_, identity=ident), reads, writes,
                       signal=signal, nosame=writes)

    def red(self, out, in_, reads, writes, op=None):
        op = op or ALU.add
        return self.op("dve", lambda en: en.tensor_reduce(out=out, in_=in_, axis=AX.X, op=op), reads, writes)

    def ms(self, e, ap, val, writes):
        return self.op(e, lambda en: en.memset(ap, val), (), writes)

    for f in (tt, ts, stt, cp, act, mm, tr, red, ms):
        setattr(KBc, f.__name__, f)


_mk(KB)


def _mk2(KBc):
    def coll(self, kind, in_ap, out_ap, groups, reads=(), writes=(), inc=1):
        if not hasattr(self, "cc_sem"):
            self.cc_sem = self.stack.enter_context(self.nc.semaphore("cc_sem"))
            self.cc_cnt = 0
        waits = self._waits("pool", reads, writes, ())
        self.cc_cnt += inc
        ev = Ev(self.cc_sem, self.cc_cnt)
        op = ALU.bypass
        self.stream["pool"].append(
            (waits, lambda en: en.collective_compute(kind, op, replica_groups=groups, ins=[in_ap], outs=[out_ap]),
             (self.cc_sem, inc)))
        for r in reads:
            r.r.append(ev)
        for w in writes:
            w.w = ev
            w.r = []
        return ev

    def barrier(self):
        evs = [(self.sem[e], self.cnt[e]) for e in self.ENGS if self.cnt[e] > 0]
        for r in self.allres:
            if r.dsem is not None and r.dcnt > 0:
                evs.append((r.dsem, r.dcnt))
        if hasattr(self, "cc_sem") and self.cc_cnt > 0:
            evs.append((self.cc_sem, self.cc_cnt))
        for e in self.ENGS:
            seen = self.seen[e]
            waits = []
            for s, v in evs:
                if s is self.sem[e]:
                    continue
                if seen.get(id(s), 0) >= v:
                    continue
                seen[id(s)] = v
                waits.append((s, v))
            self.stream[e].append((waits, None, None))

    KBc.coll = coll
    KBc.barrier = barrier


_mk2(KB)


NEG = -30000.0
EPS = 1e-6
GROUPS = [[j, 7 - j, 8 + j, 15 - j] for j in range(4)]
NT = 16
SEGS = [(0, 8, 64), (768, 10, 64), (1536, 16, 32)]
TBLK = [0, 128, 256, 384, 768, 896, 1024, 1152, 1280, 1536, 1664, 1792, 1920]
VSEG = [(512, 256), (1408, 128), (2048, 256)]


def emit_P(kb, D, ident):
    nc = kb.nc
    x, anorm, w_in, gall, ct64, st64, ct32, st32 = (D[k] for k in ("x", "anorm", "w_in", "gall", "ct64", "st64", "ct32", "st32"))
    qkT, vout, gT = D["qkT"], D["v"], D["gT"]
    idb, r_id = ident

    wq = kb.sb("wq", [128, 8, 2304], BF16); r_wq = kb.res("wq")
    wg = [kb.sb(f"wg{i}", [128, 8, 512], BF16) for i in range(2)]; r_wg = [kb.res(f"wg{i}") for i in range(2)]
    hT = kb.sb("hT", [128, 8, 2048], BF16); r_hT = [kb.res(f"hT{t}") for t in range(NT)]
    xt = [kb.sb(f"xt{i}", [128, 1024], F32) for i in range(2)]; r_xt = [kb.res(f"xt{i}") for i in range(2)]
    h16 = [kb.sb(f"h16{i}", [128, 1024], BF16) for i in range(2)]; r_h16 = [kb.res(f"h16{i}") for i in range(2)]
    pj = [kb.sb(f"pj{i}", [128, 2304], F32) for i in range(2)]; r_pj = [kb.res(f"pj{i}") for i in range(2)]
    xc = kb.sb("xc", [128, 2048], F32); r_xc = kb.res("xc")
    xs = kb.sb("xs", [128, 2048], F32); r_xs = kb.res("xs")
    qk16 = [kb.sb(f"qk16{i}", [128, 2048], BF16) for i in range(2)]; r_qk16 = [kb.res(f"qk16{i}") for i in range(2)]
    qkTs = [kb.sb(f"qkTs{i}", [128, 13, 128], BF16) for i in range(2)]; r_qkTs = [kb.res(f"qkTs{i}") for i in range(2)]
    v16 = [kb.sb(f"v16{i}", [128, 640], BF16) for i in range(2)]; r_v16 = [kb.res(f"v16{i}") for i in range(2)]
    g16 = [kb.sb(f"g16{i}", [128, 512], BF16) for i in range(2)]; r_g16 = [kb.res(f"g16{i}") for i in range(2)]
    an = kb.sb("an", [128, 1024], F32); r_an = kb.res("an")
    ga = kb.sb("ga", [128, 2048], F32); r_ga = kb.res("ga")
    c64 = kb.sb("c64", [128, NT, 64], F32); s64 = kb.sb("s64", [128, NT, 64], F32)
    c32 = kb.sb("c32", [128, NT, 32], F32); s32 = kb.sb("s32", [128, NT, 32], F32)
    r_tab = kb.res("tabs")
    epst = kb.sb("epst", [128, 1], F32); r_eps = kb.res("eps")
    st = [kb.sb(f"st{i}", [128, 40], F32) for i in range(2)]; r_st = [kb.res(f"st{i}") for i in range(2)]
    sx = [kb.sb(f"sx{i}", [128, 4], F32) for i in range(2)]; r_sx = [kb.res(f"sx{i}") for i in range(2)]
    ps_t = [kb.ps(f"ps_t{i}", [128, 1024], BF16) for i in range(2)]; r_ps_t = [kb.res(f"ps_t{i}") for i in range(2)]
    ps_m = [kb.ps(f"ps_m{i}", [128, 512], F32) for i in range(4)]; r_ps_m = [kb.res(f"ps_m{i}") for i in range(4)]
    ps_q = [kb.ps(f"ps_q{i}", [128, 1024], BF16) for i in range(2)]; r_ps_q = [kb.res(f"ps_q{i}") for i in range(2)]

    kb.ms("pool", epst[:], EPS, [r_eps])
    kb.dma("sp", an[:], anorm[:, :], writes=[r_an])
    kb.dma("sp", ga[:], gall[:, :], writes=[r_ga])
    kb.dma("sp", c64[:], ct64.rearrange("(t p) d -> p t d", p=128), writes=[r_tab])
    kb.dma("sp", s64[:], st64.rearrange("(t p) d -> p t d", p=128), writes=[r_tab])
    kb.dma("sp", c32[:], ct32.rearrange("(t p) d -> p t d", p=128), writes=[r_tab])
    kb.dma("sp", s32[:], st32.rearrange("(t p) d -> p t d", p=128), writes=[r_tab])
    wv = w_in.rearrange("(k p) c -> p k c", p=128)
    for k in range(8):
        for c0 in range(0, 2304, 1152):
            kb.dma("pool", wq[:, k, c0:c0 + 1152], wv[:, k, c0:c0 + 1152], writes=[r_wq])

    for t in range(NT):
        b = t % 2
        kb.dma("sp", xt[b][:], x[t * 128:(t + 1) * 128, :], writes=[r_xt[b]])
        kb.act(h16[b][:], xt[b][:], AF.Square, [r_xt[b]], [r_h16[b], r_sx[b]], accum_out=sx[b][:, 0:1])
        kb.act(sx[b][:, 1:2], sx[b][:, 0:1], AF.Ln, [r_sx[b], r_eps], [r_sx[b]], bias=epst[:], scale=1.0 / 1024)
        kb.act(sx[b][:, 2:3], sx[b][:, 1:2], AF.Exp, [r_sx[b]], [r_sx[b]], scale=-0.5)
        kb.stt(h16[b][:], xt[b][:], sx[b][:, 2:3], an[:], ALU.mult, ALU.mult, [r_xt[b], r_sx[b], r_an], [r_h16[b]])
        for k in range(8):
            kb.tr(ps_t[b][:, k * 128:(k + 1) * 128], h16[b][:, k * 128:(k + 1) * 128], idb[:],
                  [r_h16[b], r_id], [r_ps_t[b]], signal=(k == 7))
        kb.cp("act", hT[:, :, t * 128:(t + 1) * 128], ps_t[b][:].rearrange("p (k t) -> p k t", k=8), [r_ps_t[b]], [r_hT[t]])

    mcnt = 0
    for t in range(NT):
        b = t % 2
        for ci, (c0, cw) in enumerate([(0, 512), (512, 512), (1024, 512), (1536, 512), (2048, 256)]):
            pm = mcnt % 4; mcnt += 1
            for k in range(8):
                kb.mm(ps_m[pm][:, 0:cw], hT[:, k, t * 128:(t + 1) * 128], wq[:, k, c0:c0 + cw], k == 0, k == 7,
                      [r_hT[t], r_wq], [r_ps_m[pm]])
            kb.cp("act", pj[b][:, c0:c0 + cw], ps_m[pm][:, 0:cw], [r_ps_m[pm]], [r_pj[b]])
        vo = 0
        for (c0, cw) in VSEG:
            kb.cp("pool", v16[b][:, vo:vo + cw], pj[b][:, c0:c0 + cw], [r_pj[b]], [r_v16[b]])
            vo += cw
        kb.dma("sp", vout[t * 128:(t + 1) * 128, :], v16[b][:], reads=[r_v16[b]])
        so = 0
        for (c0, nh, d) in SEGS:
            kb.act(xs[:, c0:c0 + nh * d], pj[b][:, c0:c0 + nh * d], AF.Square, [r_pj[b]], [r_xs])
            kb.red(st[b][:, so:so + nh], xs[:, c0:c0 + nh * d].rearrange("p (h d) -> p h d", d=d), [r_xs], [r_st[b]])
            so += nh
        kb.act(st[b][:, 0:18], st[b][:, 0:18], AF.Ln, [r_st[b], r_eps], [r_st[b]], bias=epst[:], scale=1.0 / 64)
        kb.act(st[b][:, 18:34], st[b][:, 18:34], AF.Ln, [r_st[b], r_eps], [r_st[b]], bias=epst[:], scale=1.0 / 32)
        kb.act(st[b][:, 0:34], st[b][:, 0:34], AF.Exp, [r_st[b]], [r_st[b]], scale=-0.5)
        so = 0
        for si, (c0, nh, d) in enumerate(SEGS):
            w = nh * d
            e1 = "dve" if si != 1 else "pool"
            pv = pj[b][:, c0:c0 + w].rearrange("p (h d) -> p h d", d=d)
            rb = st[b][:, so:so + nh].rearrange("p (h o) -> p h o", o=1).to_broadcast([128, nh, d])
            kb.tt(e1, pv, pv, rb, ALU.mult, [r_pj[b], r_st[b]], [r_pj[b]])
            kb.tt(e1, pj[b][:, c0:c0 + w], pj[b][:, c0:c0 + w], ga[:, c0:c0 + w], ALU.mult, [r_pj[b], r_ga], [r_pj[b]])
            hd = d // 2
            ctab = (c64 if d == 64 else c32)[:, t, :]
            stab = (s64 if d == 64 else s32)[:, t, :]
            cb = ctab.rearrange("p (o d) -> p o d", o=1).to_broadcast([128, nh, d])
            kb.tt(e1, xc[:, c0:c0 + w].rearrange("p (h d) -> p h d", d=d), pv, cb, ALU.mult, [r_pj[b], r_tab], [r_xc])
            p4 = pj[b][:, c0:c0 + w].rearrange("p (h two e) -> p h two e", two=2, e=hd)
            x4 = xs[:, c0:c0 + w].rearrange("p (h two e) -> p h two e", two=2, e=hd)
            s0 = stab[:, 0:hd].rearrange("p (o d) -> p o d", o=1).to_broadcast([128, nh, hd])
            s1 = stab[:, hd:d].rearrange("p (o d) -> p o d", o=1).to_broadcast([128, nh, hd])
            kb.tt(e1, x4[:, :, 0, :], p4[:, :, 1, :], s0, ALU.mult, [r_pj[b], r_tab], [r_xs])
            kb.tt(e1, x4[:, :, 1, :], p4[:, :, 0, :], s1, ALU.mult, [r_pj[b], r_tab], [r_xs])
            kb.tt(e1, qk16[b][:, c0:c0 + w], xc[:, c0:c0 + w], xs[:, c0:c0 + w], ALU.add, [r_xc, r_xs], [r_qk16[b]])
            so += nh
        for bi, c0 in enumerate(TBLK):
            half = 0 if bi < 8 else 1
            col = (bi % 8) * 128
            kb.tr(ps_q[half][:, col:col + 128], qk16[b][:, c0:c0 + 128], idb[:], [r_qk16[b], r_id], [r_ps_q[half]],
                  signal=(bi == 7 or bi == 12))
        kb.cp("dve", qkTs[b][:, 0:8, :], ps_q[0][:].rearrange("p (k t) -> p k t", k=8), [r_ps_q[0]], [r_qkTs[b]])
        kb.cp("dve", qkTs[b][:, 8:13, :], ps_q[1][:, 0:640].rearrange("p (k t) -> p k t", k=5), [r_ps_q[1]], [r_qkTs[b]])
        kb.dma("sp", qkT[:, t * 128:(t + 1) * 128].rearrange("(k p) t -> p k t", p=128), qkTs[b][:], reads=[r_qkTs[b]])

    gcnt = 0
    for wc in range(6):
        wb = wc % 2
        for k in range(8):
            kb.dma("pool", wg[wb][:, k, :], wv[:, k, 2304 + wc * 512:2304 + (wc + 1) * 512], writes=[r_wg[wb]])
        for cc in range(4):
            for g in range(4):
                pm = mcnt % 4; mcnt += 1
                for k in range(8):
                    kb.mm(ps_m[pm][:], wg[wb][:, k, cc * 128:(cc + 1) * 128], hT[:, k, g * 512:(g + 1) * 512],
                          k == 0, k == 7, [r_wg[wb]] + r_hT[4 * g:4 * g + 4], [r_ps_m[pm]])
                gb = gcnt % 2; gcnt += 1
                kb.act(g16[gb][:], ps_m[pm][:], AF.Sigmoid, [r_ps_m[pm]], [r_g16[gb]])
                row = (wc * 4 + cc) * 128
                kb.dma("sp", gT[row:row + 128, g * 512:(g + 1) * 512], g16[gb][:], reads=[r_g16[gb]])
    return [r_v16[0], r_v16[1], r_qkTs[0], r_qkTs[1], r_g16[0], r_g16[1]]


EPS = 1e-6
NT = 16
NC2 = 22


def emit_F(kb, D, ident, pfx="f"):
    xm, xhalo, mnorm, w_up, convp, w_down, xo = (D[k] for k in ("xm", "xhalo", "mnorm", "w_up", "convp", "w_down", "xo"))
    idb, r_id = ident
    P = pfx
    hT = kb.sb(P + "hT", [128, 8, 2048], BF16); r_hT = [kb.res(P + f"hT{t}") for t in range(NT)]
    hTh = kb.sb(P + "hTh", [128, 8, 8], BF16); r_hTh = kb.res(P + "hTh")
    mT = kb.sb(P + "mT", [128, NC2, 1024], BF16); r_mT = [kb.res(P + f"mT{g}") for g in range(2)]
    wd = kb.sb(P + "wd", [128, NC2, 1024], BF16); r_wd = kb.res(P + "wd")
    wu = [kb.sb(P + f"wu{i}", [128, 8, 2, 128], BF16) for i in range(2)]; r_wu = [kb.res(P + f"wu{i}") for i in range(2)]
    xt = [kb.sb(P + f"xt{i}", [128, 1024], F32) for i in range(2)]; r_xt = [kb.res(P + f"xt{i}") for i in range(2)]
    h16 = [kb.sb(P + f"h16{i}", [128, 1024], BF16) for i in range(2)]; r_h16 = [kb.res(P + f"h16{i}") for i in range(2)]
    an = kb.sb(P + "an", [128, 1024], F32); r_an = kb.res(P + "an")
    cpar = kb.sb(P + "cpar", [128, 44, 4], F32); r_cp = kb.res(P + "cpar")
    epst = kb.sb(P + "epst", [128, 1], F32); r_eps = kb.res(P + "eps")
    sx = [kb.sb(P + f"sx{i}", [128, 4], F32) for i in range(2)]; r_sx = [kb.res(P + f"sx{i}") for i in range(2)]
    ub = [[kb.sb(P + f"ub{i}{s}", [128, 514], F32) for s in range(2)] for i in range(2)]
    r_ub = [[kb.res(P + f"ub{i}{s}") for s in range(2)] for i in range(2)]
    yb = [[kb.sb(P + f"yb{i}{s}", [128, 512], F32) for s in range(2)] for i in range(2)]
    r_yb = [[kb.res(P + f"yb{i}{s}") for s in range(2)] for i in range(2)]
    ob = [kb.sb(P + f"ob{i}", [128, 1024], F32) for i in range(2)]; r_ob = [kb.res(P + f"ob{i}") for i in range(2)]
    ps_t = kb.ps(P + "ps_t", [128, 1024], BF16); r_ps_t = kb.res(P + "ps_t")
    pu = [[kb.ps(P + f"pu{i}{s}", [128, 512], F32) for s in range(2)] for i in range(2)]
    r_pu = [[kb.res(P + f"pu{i}{s}") for s in range(2)] for i in range(2)]
    ph = kb.ps(P + "ph", [128, 16], F32); r_ph = kb.res(P + "ph")
    po = [kb.ps(P + f"po{i}", [128, 512], F32) for i in range(2)]; r_po = [kb.res(P + f"po{i}") for i in range(2)]

    kb.ms("pool", epst[:], EPS, [r_eps])
    kb.dma("sp", an[:], mnorm[:, :], writes=[r_an])
    kb.dma("sp", cpar[:], convp[:, :, :], writes=[r_cp])
    wdv = w_down.rearrange("(c p) n -> p c n", p=128)
    for c in range(NC2):
        kb.dma("pool", wd[:, c, :], wdv[:, c, :], writes=[r_wd])

    for t in range(NT + 1):
        b = t % 2
        n = 128 if t < NT else 8
        src = xm[t * 128:(t + 1) * 128, :] if t < NT else xhalo[:, :]
        kb.dma("sp", xt[b][0:n, :], src, writes=[r_xt[b]])
        kb.act(h16[b][0:n, :], xt[b][0:n, :], AF.Square, [r_xt[b]], [r_h16[b], r_sx[b]], accum_out=sx[b][0:n, 0:1])
        kb.act(sx[b][0:n, 1:2], sx[b][0:n, 0:1], AF.Ln, [r_sx[b], r_eps], [r_sx[b]], bias=epst[0:n, :], scale=1.0 / 1024)
        kb.act(sx[b][0:n, 2:3], sx[b][0:n, 1:2], AF.Exp, [r_sx[b]], [r_sx[b]], scale=-0.5)
        kb.stt(h16[b][0:n, :], xt[b][0:n, :], sx[b][0:n, 2:3], an[0:n, :], ALU.mult, ALU.mult, [r_xt[b], r_sx[b], r_an], [r_h16[b]])
        for k in range(8):
            kb.tr(ps_t[:, k * 128:k * 128 + n], h16[b][0:n, k * 128:(k + 1) * 128], idb[0:n, 0:n],
                  [r_h16[b], r_id], [r_ps_t], signal=(k == 7))
        if t < NT:
            kb.cp("act", hT[:, :, t * 128:(t + 1) * 128], ps_t[:].rearrange("p (k t) -> p k t", k=8), [r_ps_t], [r_hT[t]])
        else:
            kb.cp("act", hTh[:], ps_t[:].rearrange("p (k t) -> p k t", k=8)[:, :, 0:8], [r_ps_t], [r_hTh])

    wuv = w_up.rearrange("(k p) c -> p k c", p=128)
    it = 0
    for half in range(2):
        for c in range(NC2):
            wb = it % 2; it += 1
            for s in range(2):
                col = s * 2816 + c * 128
                kb.dma("pool", wu[wb][:, :, s, :], wuv[:, :, col:col + 128], writes=[r_wu[wb]])
            for s in range(2):
                for k in range(8):
                    kb.mm(ph[:, s * 8:(s + 1) * 8], wu[wb][:, k, s, :], hTh[:, k, :], k == 0, k == 7,
                          [r_wu[wb], r_hTh], [r_ph], signal=(s == 1 and k == 7))
            for gi in range(2):
                g = half * 2 + gi
                ib = (c * 2 + gi) % 2
                for s in range(2):
                    for k in range(8):
                        kb.mm(pu[ib][s][:], wu[wb][:, k, s, :], hT[:, k, g * 512:(g + 1) * 512], k == 0, k == 7,
                              [r_wu[wb]] + r_hT[4 * g:4 * g + 4], [r_pu[ib][s]])
                for s in range(2):
                    ci = s * NC2 + c
                    u, ru = ub[ib][s], r_ub[ib][s]
                    y, ry = yb[ib][s], r_yb[ib][s]
                    kb.cp("dve", u[:, 0:2], ph[:, s * 8 + g * 2:s * 8 + g * 2 + 2], [r_ph], [ru])
                    kb.cp("act", u[:, 2:514], pu[ib][s][:], [r_pu[ib][s]], [ru])
                    kb.act(y[:], pu[ib][s][:], AF.Identity, [r_pu[ib][s], r_cp], [ry], bias=cpar[:, ci, 3:4], scale=cpar[:, ci, 2:3])
                    kb.stt(y[:], u[:, 1:513], cpar[:, ci, 1:2], y[:], ALU.mult, ALU.add, [ru, ry, r_cp], [ry])
                    kb.stt(y[:], u[:, 0:512], cpar[:, ci, 0:1], y[:], ALU.mult, ALU.add, [ru, ry, r_cp], [ry])
                yg, yv = yb[ib][0], yb[ib][1]
                kb.act(yg[:], yg[:], AF.Silu, [r_yb[ib][0]], [r_yb[ib][0]])
                kb.tt("dve", mT[:, c, gi * 512:(gi + 1) * 512], yg[:], yv[:], ALU.mult, [r_yb[ib][0], r_yb[ib][1]], [r_mT[gi]])
        for tt_ in range(8):
            t = half * 8 + tt_
            b = t % 2
            kb.dma("sp", xt[b][:], xm[t * 128:(t + 1) * 128, :], writes=[r_xt[b]])
            for hc in range(2):
                for c in range(NC2):
                    kb.mm(po[hc][:], mT[:, c, tt_ * 128:(tt_ + 1) * 128], wd[:, c, hc * 512:(hc + 1) * 512], c == 0, c == NC2 - 1,
                          [r_mT[tt_ // 4], r_wd], [r_po[hc]])
                kb.tt("dve", ob[b][:, hc * 512:(hc + 1) * 512], po[hc][:], xt[b][:, hc * 512:(hc + 1) * 512], ALU.add,
                      [r_po[hc], r_xt[b]], [r_ob[b]])
            kb.dma("sp", xo[t * 128:(t + 1) * 128, :], ob[b][:], reads=[r_ob[b]])
    return [r_ob[0], r_ob[1]]


NEG = -30000.0
BIG = 30000.0
EPS = 1e-6
NSTREAM = [12, 28, 44, 60]
FAST_RECIP = False


def RECIP(en):
    return en.reciprocal_approx_fast if FAST_RECIP else en.reciprocal


STOP = 99


def emit_A(kb, D, ident, lam_init, pfx="a"):
    P = pfx
    idb, r_id = ident
    qkT, vown_d, kTf, vhp, kTbh, vbh_d = (D[k] for k in ("qkT", "v_own", "kT_full", "v_hp", "kTb_halo", "vb_halo"))

    def S(name, shape, dt):
        return kb.sb(P + name, shape, dt), kb.res(P + name)

    kbuf, r_kbuf = S("kbuf", [128, 2, 8192], BF16)
    vaug, r_vaug = S("vaug", [128, 64, 128], BF16)
    vstg, r_vstg = S("vstg", [128, 64, 64], BF16)
    kown, r_kown = S("kown", [128, 2, 2048], BF16)
    vown, r_vown = S("vown", [128, 16, 128], BF16)
    vostg, r_vostg = S("vostg", [128, 16, 64], BF16)
    qt, r_qt = S("qt", [128, 2, 2048], BF16)
    yT, r_yT = S("yT", [128, 8, 2048], BF16)
    mdiag, r_md = S("mdiag", [128, 4, 512], BF16)
    mB, r_mB = S("mB", [128, 2, 512], BF16)
    pmA, r_pmA = S("pmA", [128, 16, 4, 32], F32)
    hval, r_hval = S("hval", [128, 4], F32)
    lamv, r_lamv = S("lamv", [128, 4, 32], F32)
    lamt, r_lamt = S("lamt", [128, 8], F32)
    sgc, r_sgc = S("sgc", [128, 1], F32)
    sinkt, r_sinkt = S("sinkt", [1, 8], F32)
    sinke, r_sinke = S("sinke", [1, 8], F32)
    sh16, r_sh = S("sh16", [1, 8], BF16)
    sl16, r_sl = S("sl16", [1, 8], BF16)
    shf, r_shf = S("shf", [1, 8], F32)
    sinkrow, r_sinkrow = S("sinkrow", [1, 2, 8, 128], BF16)
    srow, r_srow = S("srow", [1, 128], BF16)
    ones64, r_ones64 = S("ones64", [64, 64], BF16)
    epst, r_eps = S("epst", [128, 1], F32)
    pt = [S(f"pt{i}", [128, 512], BF16) for i in range(4)]
    rcp = [S(f"rcp{i}", [64, 512], F32) for i in range(2)]
    t1, r_t1 = S("t1", [64, 512], F32)
    t2, r_t2 = S("t2", [64, 512], F32)
    sq16, r_sq16 = S("sq16", [64, 512], BF16)
    rs, r_rs = S("rs", [64, 512], F32)
    kmean, r_kmean = S("kmean", [64, 32], F32)
    kmean16, r_km16 = S("kmean16", [64, 32], BF16)
    gm, r_gm = S("gm", [128, 16, 32], F32)
    t8, r_t8 = S("t8", [128, 16, 8], F32)
    sel, r_sel = S("sel", [128, 16, 32], F32)
    bst, r_bst = S("bst", [128, 16, 64], BF16)
    tmpb, r_tmpb = S("tmpb", [128, 16, 32], F32)
    ps_s = [(kb.ps(P + f"ps_s{i}", [128, 512], F32), kb.res(P + f"ps_s{i}")) for i in range(2)]
    ps_o = [(kb.ps(P + f"ps_o{i}", [128, 512], F32), kb.res(P + f"ps_o{i}")) for i in range(4)]
    ps_x = [(kb.ps(P + f"ps_x{i}", [128, 512], F32), kb.res(P + f"ps_x{i}")) for i in range(1)]
    ps_b, r_ps_b = kb.ps(P + "ps_b", [128, 1024], BF16), kb.res(P + "ps_b")

    kb.ms("pool", epst[:], EPS, [r_eps])
    kb.ms("pool", vaug[:, :, 64:128], 1.0, [r_vaug])
    kb.ms("pool", vown[:, :, 64:128], 1.0, [r_vown])
    kb.ms("pool", ones64[:], 1.0, [r_ones64])
    kb.ms("pool", srow[:, 0:64], 0.0, [r_srow])
    kb.ms("pool", srow[:, 64:128], 1.0, [r_srow])
    kb.dma("sp", mdiag[:], D["mdiag"][:, :, :], writes=[r_md])
    kb.dma("sp", mB[:], D["mB"][:, :, :], writes=[r_mB])
    kb.dma("sp", pmA[:], D["pmA"][:, :, :, :], writes=[r_pmA])
    kb.dma("sp", hval[:], D["hval"][:, :], writes=[r_hval])
    kb.dma("sp", lamv[:], D["lamv"][:, :, :], writes=[r_lamv])
    kb.dma("sp", sgc[:], D["sgc"][:, :], writes=[r_sgc])
    kb.dma("sp", sinkt[:], D["sinks"][0:1, :], writes=[r_sinkt])
    sinkf, r_sinkf = S("sinkf", [128, 8], F32)
    esink, r_esink = S("esink", [128, 8], F32)
    kb.dma("sp", sinkf[:], D["sinks"][:, :], writes=[r_sinkf])
    kb.act(esink[:], sinkf[:], AF.Exp, [r_sinkf], [r_esink])
    kb.tt("dve", lamv[:, 0, :], lamv[:, 0, :], lamv[:, 1, :], ALU.mult, [r_lamv], [r_lamv])
    kb.tt("dve", lamv[:, 2, :], lamv[:, 2, :], lamv[:, 3, :], ALU.mult, [r_lamv], [r_lamv])
    kb.red(lamt[:, 0:1], lamv[:, 0, :], [r_lamv], [r_lamt])
    kb.red(lamt[:, 1:2], lamv[:, 2, :], [r_lamv], [r_lamt])
    kb.act(lamt[:, 2:4], lamt[:, 0:2], AF.Exp, [r_lamt], [r_lamt])
    kb.tt("dve", lamt[:, 4:5], lamt[:, 3:4], lamt[:, 2:3], ALU.subtract, [r_lamt], [r_lamt])
    kb.ts("dve", lamt[:, 4:5], lamt[:, 4:5], -float(lam_init), None, ALU.add, None, [r_lamt], [r_lamt])
    kb.act(sinke[:], sinkt[:], AF.Exp, [r_sinkt], [r_sinke])
    kb.cp("dve", sh16[:], sinke[:], [r_sinke], [r_sh])
    kb.cp("dve", shf[:], sh16[:], [r_sh], [r_shf])
    kb.tt("dve", shf[:], sinke[:], shf[:], ALU.subtract, [r_sinke, r_shf], [r_shf])
    kb.cp("dve", sl16[:], shf[:], [r_shf], [r_sl])
    kb.cp("dve", sinkrow[:, 0, :, :], sh16[:].rearrange("p (h o) -> p h o", o=1).to_broadcast([1, 8, 128]), [r_sh], [r_sinkrow])
    kb.cp("dve", sinkrow[:, 1, :, :], sl16[:].rearrange("p (h o) -> p h o", o=1).to_broadcast([1, 8, 128]), [r_sl], [r_sinkrow])

    def _fin():
        kb.dma("sp", D["yT_out"].rearrange("(k p) t -> p k t", p=128), yT[:], reads=[r_yT])
        return [r_yT]

    if STOP <= 0:
        return []
    cnt = {"s": 0, "p": 0, "o": 0, "x": 0, "r": 0}

    def nxt(k, n):
        v = cnt[k] % n
        cnt[k] += 1
        return v

    class Pipe:
        def __init__(self):
            self.prev = None
            self.cbs = []

        def _S(self, t):
            ps, r_ps = ps_s[nxt("s", 2)]
            kl, ql, mask, rds = t["kl"], t["ql"], t["mask"], t["rds"]
            if isinstance(ql, list):
                kb.mm(ps[:], idb[:], mask[0], True, False, [r_id, mask[1]], [r_ps], signal=False)
                for gi, qp in enumerate(ql):
                    kb.mm(ps[:, gi * 128:(gi + 1) * 128], kl, qp, False, gi == len(ql) - 1, rds, [r_ps])
            else:
                kb.mm(ps[:], kl, ql, True, mask is None, rds, [r_ps])
                if mask is not None:
                    kb.mm(ps[:], idb[:], mask[0], False, True, [r_id, mask[1]], [r_ps])
            return ps, r_ps

        def _drain(self):
            if self.prev is not None:
                t, (ps, r_ps) = self.prev
                p, r_p = pt[nxt("p", 4)]
                kb.act(p[:], ps[:], AF.Exp, [r_ps], [r_p], scale=t["scale"])
                kb.mm(t["acc"][0][:], t["vl"], p[:], t["first"], t["last"], [r_p] + t["rds"], [t["acc"][1]])
                self.prev = None
            keep = []
            for item in self.cbs:
                if item[0] <= 0:
                    item[1]()
                else:
                    item[0] -= 1
                    keep.append(item)
            self.cbs = keep

        def push(self, kl, ql, vl, acc, first, last, scale, rds, mask=None):
            t = dict(kl=kl, ql=ql, vl=vl, acc=acc, first=first, last=last, scale=scale, rds=rds, mask=mask)
            ps = self._S(t)
            self._drain()
            self.prev = (t, ps)

        def after(self, cb, delay=0):
            self.cbs.append([delay, cb])

        def flush(self):
            self._drain()
            for item in self.cbs:
                item[1]()
            self.cbs = []

    pipe = Pipe()

    def attn_tile(kl, ql, vl, acc, first, last, scale, rds, mask=None):
        pipe.push(kl, ql, vl, acc, first, last, scale, rds, mask)

    kb.ms("pool", kbuf[64:128, 0, :], 0.0, [r_kbuf])
    kb.ms("pool", kown[64:96, 0, :], 0.0, [r_kown])
    for h in range(4):
        kb.dma("sp", kbuf[0:64, 0, :], kTf[h * 64:(h + 1) * 64, :], writes=[r_kbuf])
        kb.dma("sp", kbuf[64:96, 0, :], D["ohA"][:, :], writes=[r_kbuf])
        kb.dma("sp", qt[0:64, 0, :], qkT[h * 64:(h + 1) * 64, :], writes=[r_qt])
        kb.dma("sp", vstg[:], vhp[h], writes=[r_vstg])
        kb.cp("pool", vaug[:, :, 0:64], vstg[:], [r_vstg], [r_vaug])
        kb.dma("sp", kown[0:64, 0, :], qkT[256 + h * 64:256 + (h + 1) * 64, :], writes=[r_kown])
        kb.dma("sp", kown[96:128, 0, :], D["ohA_own"][:, :], writes=[r_kown])
        kb.dma("sp", vostg[:], vown_d[:, h * 64:(h + 1) * 64].rearrange("(c p) d -> p c d", p=128), writes=[r_vostg])
        kb.cp("pool", vown[:, :, 0:64], vostg[:], [r_vostg], [r_vown])
        kb.red(kmean[:], kbuf[0:64, 0, :].rearrange("p (n l) -> p n l", l=256), [r_kbuf], [r_kmean])
        kb.ts("dve", kmean16[:], kmean[:], 1.0 / 256, None, ALU.mult, None, [r_kmean], [r_km16])
        pg, r_pg = ps_x[0]
        for c in range(16):
            kb.mm(pg[:, c * 32:(c + 1) * 32], qt[0:64, 0, c * 128:(c + 1) * 128], kmean16[:], True, True,
                  [r_qt, r_km16], [r_pg], signal=(c == 15))
        kb.tt("dve", gm[:], pg[:].rearrange("p (c n) -> p c n", n=32), pmA[:, :, 0, :], ALU.add, [r_pg, r_pmA], [r_gm])
        for c in range(16):
            kb.op("dve", (lambda c: lambda en: en.max(out=t8[:, c, :], in_=gm[:, c, :]))(c), [r_gm], [r_t8])
        kb.tt("dve", sel[:], gm[:], t8[:, :, 2:3].to_broadcast([128, 16, 32]), ALU.is_ge, [r_gm, r_t8], [r_sel])
        for which in range(2):
            kb.tt("dve", tmpb[:], sel[:], pmA[:, :, 1 + which, :], ALU.mult, [r_sel, r_pmA], [r_tmpb])
            if which == 1:
                kb.tt("dve", tmpb[:], tmpb[:], pmA[:, :, 3, :], ALU.add, [r_tmpb, r_pmA], [r_tmpb])
            kb.ts("dve", bst[:, :, which * 32:(which + 1) * 32], tmpb[:], BIG, -BIG, ALU.mult, ALU.add, [r_tmpb], [r_bst])
        for hf in range(2):
            for c8 in range(8):
                c = hf * 8 + c8
                kb.tr(ps_b[0:64, c8 * 128:(c8 + 1) * 128], bst[:, c, :], idb[:], [r_bst, r_id], [r_ps_b], signal=(c8 == 7))
            kb.cp("dve", qt[64:128, 0, hf * 1024:(hf + 1) * 1024], ps_b[0:64, :], [r_ps_b], [r_qt])
        if STOP <= 1:
            return _fin()
        for i in range(4):
            acc = ps_o[nxt("o", 4)]
            qs = qt[:, 0, i * 512:(i + 1) * 512]
            for kt in range(NSTREAM[i]):
                attn_tile(kbuf[:, 0, kt * 128:(kt + 1) * 128], qs, vaug[:, kt, :], acc, kt == 0, False, 0.125,
                          [r_kbuf, r_qt, r_vaug])
            for t in range(4):
                c = 4 * i + t
                attn_tile(kown[:, 0, c * 128:(c + 1) * 128], qs, vown[:, c, :], acc, False, t == 3, 0.125,
                          [r_kown, r_qt, r_vown], mask=(mdiag[:, t, :], r_md))
            def fin_a(acc=acc, h=h, i=i):
                rc, r_rc = rcp[nxt("r", 2)]
                kb.op("dve", lambda en: en.reciprocal(out=rc[:], in_=acc[0][64:128, :]), [acc[1]], [r_rc])
                kb.tt("dve", yT[(h % 2) * 64:(h % 2) * 64 + 64, h // 2, i * 512:(i + 1) * 512], acc[0][0:64, :], rc[:], ALU.mult,
                      [acc[1], r_rc], [r_yT])
            pipe.after(fin_a)
        pipe.flush()

    if STOP <= 2:
        return _fin()
    sc_c = float(32 ** -0.5)
    for m in range(2):
        kb.dma("sp", kbuf[32:48, m, :], D["ohC"][:, :], writes=[r_kbuf])
        kb.dma("sp", qt[32:48, m, :], D["cbC"][:, :], writes=[r_qt])
    for h in range(4):
        for m in range(2):
            r0 = 384 + h * 64 + m * 32
            kb.dma("sp", kbuf[0:32, m, :], kTf[r0:r0 + 32, :], writes=[r_kbuf])
            q0 = 1152 + h * 64 + m * 32
            kb.dma("sp", qt[0:32, m, :], qkT[q0:q0 + 32, :], writes=[r_qt])
            k0 = 1408 + h * 64 + m * 32
            kb.dma("sp", kown[0:32, m, :], qkT[k0:k0 + 32, :], writes=[r_kown])
        kb.dma("sp", vstg[:], vhp[6 + h], writes=[r_vstg])
        kb.cp("pool", vaug[:, :, 0:64], vstg[:], [r_vstg], [r_vaug])
        kb.dma("sp", vostg[:], vown_d[:, 384 + h * 64:384 + (h + 1) * 64].rearrange("(c p) d -> p c d", p=128), writes=[r_vostg])
        kb.cp("pool", vown[:, :, 0:64], vostg[:], [r_vostg], [r_vown])
        for i in range(4):
            par = nxt("o", 2)
            accs = [ps_o[2 * par], ps_o[2 * par + 1]]
            for kt in range(NSTREAM[i]):
                for m in range(2):
                    attn_tile(kbuf[0:48, m, kt * 128:(kt + 1) * 128], qt[0:48, m, i * 512:(i + 1) * 512], vaug[:, kt, :],
                              accs[m], kt == 0, False, sc_c, [r_kbuf, r_qt, r_vaug])
            for t in range(4):
                c = 4 * i + t
                for m in range(2):
                    attn_tile(kown[0:32, m, c * 128:(c + 1) * 128], qt[0:32, m, i * 512:(i + 1) * 512], vown[:, c, :],
                              accs[m], False, t == 3, sc_c, [r_kown, r_qt, r_vown], mask=(mdiag[:, t, :], r_md))

            def fin_c1(accs=accs):
                for m in range(2):
                    kb.op("dve", (lambda m: lambda en: en.reciprocal(out=rcp[m][0][:], in_=accs[m][0][64:128, :]))(m), [accs[m][1]], [rcp[m][1]])
                kb.tt("dve", t1[:], accs[0][0][0:64, :], rcp[0][0][:], ALU.mult, [accs[0][1], rcp[0][1]], [r_t1])
                kb.tt("dve", t2[:], accs[1][0][0:64, :], rcp[1][0][:], ALU.mult, [accs[1][1], rcp[1][1]], [r_t2])
                kb.stt(t1[:], t2[:], lamt[0:64, 4:5], t1[:], ALU.mult, ALU.add, [r_t1, r_t2, r_lamt], [r_t1])
                kb.act(sq16[:], t1[:], AF.Square, [r_t1], [r_sq16])

            def fin_c2(h=h, i=i):
                px, r_px = ps_x[0]
                kb.mm(px[0:64, :], ones64[:], sq16[:], True, True, [r_ones64, r_sq16], [r_px])
                kb.act(rs[:], px[0:64, :], AF.Ln, [r_px, r_eps], [r_rs], bias=epst[0:64, :], scale=1.0 / 64)
                kb.act(rs[:], rs[:], AF.Exp, [r_rs], [r_rs], scale=-0.5)
                kb.tt("dve", t1[:], t1[:], rs[:], ALU.mult, [r_t1, r_rs], [r_t1])
                kb.ts("dve", yT[(h % 2) * 64:(h % 2) * 64 + 64, 6 + h // 2, i * 512:(i + 1) * 512], t1[:], sgc[0:64, :], float(1.0 - lam_init),
                      ALU.mult, ALU.mult, [r_t1, r_sgc], [r_yT])
            pipe.after(fin_c1)
            pipe.after(fin_c2, delay=24)
        pipe.flush()

    if STOP <= 3:
        return _fin()
    qb = qt
    for k in range(2):
        qv = kbuf[0:64, 0, :].rearrange("p (g t) -> p g t", g=4)
        kb.dma("sp", qv, qkT[512 + k * 256:512 + (k + 1) * 256, :].rearrange("(g d) t -> d g t", d=64), writes=[r_kbuf])
        kb.dma("sp", kown[0:64, 0, :], qkT[1024 + k * 64:1024 + (k + 1) * 64, :], writes=[r_kown])
        kb.dma("sp", kown[0:64, 1, 0:512], kTbh[k * 64:(k + 1) * 64, :], writes=[r_kown])
        kb.dma("sp", vostg[:], vown_d[:, 256 + k * 64:256 + (k + 1) * 64].rearrange("(c p) d -> p c d", p=128), writes=[r_vostg])
        kb.cp("pool", vown[:, :, 0:64], vostg[:], [r_vostg], [r_vown])
        kb.dma("sp", vstg[:, 0:4, :], vbh_d[:, k * 64:(k + 1) * 64].rearrange("(s p) d -> p s d", p=128), writes=[r_vstg])
        kb.cp("pool", vaug[:, 0:4, 0:64], vstg[:, 0:4, :], [r_vstg], [r_vaug])
        if k == 0:
            kb.ms("pool", vaug[:, 4:8, 64:128], 1.0, [r_vaug])
        for s in range(4):
            kb.ts("dve", vaug[:, s, 0:64], vaug[:, s, 0:64], hval[:, s:s + 1], None, ALU.mult, None, [r_vaug, r_hval], [r_vaug])
            kb.ts("dve", vaug[:, s, 64:128], vaug[:, 4 + s, 64:128], hval[:, s:s + 1], None, ALU.mult, None, [r_vaug, r_hval], [r_vaug])
        for c in range(16):
            s = c // 4
            acc = ps_o[nxt("o", 4)]
            if c % 4 == 0:
                kprev, vprev = kown[0:64, 1, s * 128:(s + 1) * 128], vaug[:, s, :]
            else:
                kprev, vprev = kown[0:64, 0, (c - 1) * 128:c * 128], vown[:, c - 1, :]
            rds = [r_kown, r_kbuf, r_vown, r_vaug]
            qparts = [qv[:, gi, c * 128:(c + 1) * 128] for gi in range(4)]
            attn_tile(kprev, qparts, vprev, acc, True, False, 0.125, rds, mask=(mB[:, 0, :], r_mB))
            attn_tile(kown[0:64, 0, c * 128:(c + 1) * 128], qparts, vown[:, c, :], acc, False, True, 0.125, rds, mask=(mB[:, 1, :], r_mB))

            def fin_b(acc=acc, c=c, k=k):
                rc, r_rc = rcp[nxt("r", 2)]
                kb.cp("dve", t2[:], acc[0][64:128, :], [acc[1]], [r_t2])
                for gi in range(4):
                    hh = 4 * k + gi
                    kb.ts("dve", t2[:, gi * 128:(gi + 1) * 128], t2[:, gi * 128:(gi + 1) * 128], esink[0:64, hh:hh + 1], None, ALU.add, None,
                          [r_t2, r_esink], [r_t2])
                kb.op("dve", lambda en: en.reciprocal(out=rc[:], in_=t2[:]), [r_t2], [r_rc])
                for gi in range(4):
                    hh = 4 * k + gi
                    kb.tt("dve", yT[(hh % 2) * 64:(hh % 2) * 64 + 64, 2 + hh // 2, c * 128:(c + 1) * 128],
                          acc[0][0:64, gi * 128:(gi + 1) * 128], rc[:, gi * 128:(gi + 1) * 128], ALU.mult, [acc[1], r_rc], [r_yT])
            pipe.after(fin_b)
        pipe.flush()
    return _fin()


def emit_M(kb, D, pfx="m"):
    P = pfx
    gT, x, xmid = D["gT"], D["x"], D["xmid"]

    def S(name, shape, dt):
        return kb.sb(P + name, shape, dt), kb.res(P + name)

    cnt = {"x": 0}

    def nxt(k, n):
        v = cnt[k] % n
        cnt[k] += 1
        return v

    ps_x = [(kb.ps(P + f"ps_x{i}", [128, 512], F32), kb.res(P + f"ps_x{i}")) for i in range(4)]
    yT, r_yT = S("yT", [128, 8, 2048], BF16)
    kb.dma("sp", yT[:], D["yT_in"].rearrange("(k p) t -> p k t", p=128), writes=[r_yT])
    wp, r_wp = S("wp", [128, 8, 1024], BF16)
    wo, r_wo = S("wo", [128, 8, 1024], BF16)
    for nm, k0, nk in (("w_pa", 0, 2), ("w_pb", 2, 4), ("w_pc", 6, 2)):
        wv = D[nm].rearrange("(k p) n -> p k n", p=128)
        for k in range(nk):
            kb.dma("pool", wp[:, k0 + k, :], wv[:, k, :], writes=[r_wp])
    wov = D["w_out"].rearrange("(k p) n -> p k n", p=128)
    for k in range(8):
        kb.dma("pool", wo[:, k, :], wov[:, k, :], writes=[r_wo])
    gt = [S(f"gt{i}", [128, 3, 512], BF16) for i in range(2)]
    macc, r_macc = S("macc", [128, 512], F32)
    mtmp, r_mtmp = S("mtmp", [128, 512], F32)
    mT, r_mT = S("mT", [128, 8, 512], BF16)
    xt = [S(f"xt{i}", [128, 1024], F32) for i in range(2)]
    ob = [S(f"ob{i}", [128, 1024], F32) for i in range(2)]
    gTv = gT.rearrange("(br f) t -> f br t", br=3)
    BR = ((0, 2), (2, 4), (6, 2))
    gi_ = 0
    for g in range(4):
        for fc in range(8):
            gtt, r_gtt = gt[gi_ % 2]; gi_ += 1
            kb.dma("sp", gtt[:], gTv[fc * 128:(fc + 1) * 128, :, g * 512:(g + 1) * 512], writes=[r_gtt])
            for br, (k0, nk) in enumerate(BR):
                px, r_px = ps_x[nxt("x", 4)]
                for k in range(nk):
                    kb.mm(px[:], wp[:, k0 + k, fc * 128:(fc + 1) * 128], yT[:, k0 + k, g * 512:(g + 1) * 512], k == 0, k == nk - 1,
                          [r_wp, r_yT], [r_px])
                if br == 0:
                    kb.tt("dve", macc[:], px[:], gtt[:, 0, :], ALU.mult, [r_px, r_gtt], [r_macc])
                else:
                    kb.tt("dve", mtmp[:], px[:], gtt[:, br, :], ALU.mult, [r_px, r_gtt], [r_mtmp])
                    if br == 1:
                        kb.tt("dve", macc[:], macc[:], mtmp[:], ALU.add, [r_macc, r_mtmp], [r_macc])
                    else:
                        kb.tt("dve", mT[:, fc, :], macc[:], mtmp[:], ALU.add, [r_macc, r_mtmp], [r_mT])
        for tt_ in range(4):
            t = g * 4 + tt_
            b = t % 2
            kb.dma("sp", xt[b][0][:], x[t * 128:(t + 1) * 128, :], writes=[xt[b][1]])
            for hc in range(2):
                px, r_px = ps_x[nxt("x", 4)]
                for fc in range(8):
                    kb.mm(px[:], mT[:, fc, tt_ * 128:(tt_ + 1) * 128], wo[:, fc, hc * 512:(hc + 1) * 512], fc == 0, fc == 7,
                          [r_mT, r_wo], [r_px])
                kb.tt("dve", ob[b][0][:, hc * 512:(hc + 1) * 512], px[:], xt[b][0][:, hc * 512:(hc + 1) * 512], ALU.add,
                      [r_px, xt[b][1]], [ob[b][1]])
            kb.dma("sp", xmid[t * 128:(t + 1) * 128, :], ob[b][0][:], reads=[ob[b][1]])
    return [ob[0][1], ob[1][1]]


GROUPS = [[j, 7 - j, 8 + j, 15 - j] for j in range(4)]
S = 8192
BF = ml_dtypes.bfloat16


def core_pos(j):
    return np.concatenate([np.arange(g * 512, (g + 1) * 512) for g in GROUPS[j]])


def rope_tabs(pos, dim):
    inv = (1.0 / (np.float32(10000.0) ** (np.arange(0, dim, 2, dtype=np.float32) / np.float32(dim)))).astype(np.float32)
    ang = pos.astype(np.float32)[:, None] * inv[None, :]
    c = np.cos(ang).astype(np.float32)
    s = np.sin(ang).astype(np.float32)
    ct = np.concatenate([c, c], axis=1)
    st = np.concatenate([-s, s], axis=1)
    return np.ascontiguousarray(ct), np.ascontiguousarray(st)


def rep128(v):
    return np.ascontiguousarray(np.broadcast_to(np.asarray(v, np.float32)[None, :], (128, v.shape[-1])))


def p_inputs(x_core, l, j, inp):
    pos = core_pos(j)
    ct64, st64 = rope_tabs(pos, 64)
    ct32, st32 = rope_tabs(pos, 32)
    gall = np.ones((2048,), np.float32)
    gall[0:256] = np.tile(inp["qn_a"][l], 4)
    gall[256:512] = np.tile(inp["kn_a"][l], 4)
    gall[768:1280] = np.tile(inp["qn_b"][l], 8)
    gall[1280:1408] = np.tile(inp["kn_b"][l], 2)
    gall[1536:1792] = np.tile(inp["qn_c"][l], 8)
    gall[1792:2048] = np.tile(inp["kn_c"][l], 8)
    return {"x": np.ascontiguousarray(x_core), "anorm": rep128(inp["attn_norm"][l]),
            "w_in": np.ascontiguousarray(inp["w_in"][l]), "gall": rep128(gall),
            "ct64": ct64, "st64": st64, "ct32": ct32, "st32": st32}


def f_inputs(xm_core, xm_full_b, l, j, inp):
    halo = np.zeros((8, 1024), np.float32)
    for s, g in enumerate(GROUPS[j]):
        if g > 0:
            halo[2 * s:2 * s + 2] = xm_full_b[g * 512 - 2:g * 512]
    cw = inp["conv_w"][l]
    cb = inp["conv_b"][l]
    convp = np.zeros((128, 44, 4), np.float32)
    convp[:, :, 0:3] = cw.T.reshape(44, 128, 3).transpose(1, 0, 2)
    convp[:, :, 3] = cb.reshape(44, 128).T
    return {"xm": np.ascontiguousarray(xm_core), "xhalo": halo, "mnorm": rep128(inp["mlp_norm"][l]),
            "w_up": np.ascontiguousarray(inp["w_up"][l]), "convp": convp,
            "w_down": np.ascontiguousarray(inp["w_down"][l])}


NEGV = -30000.0


def a_consts(j):
    gl = GROUPS[j]
    f32 = np.float32
    mdiag = np.zeros((128, 4, 512), f32)
    p = np.arange(128)[:, None]
    f = np.arange(512)[None, :]
    for t in range(4):
        mdiag[:, t, :] = np.where(t * 128 + p > f, NEGV, 0.0)
    mB = np.zeros((128, 2, 512), f32)
    fi = (np.arange(512) % 128)[None, :]
    mB[:, 0, :] = np.where(p <= fi, NEGV, 0.0)
    mB[:, 1, :] = np.where(p > fi, NEGV, 0.0)
    pmA = np.zeros((128, 16, 4, 32), f32)
    n = np.arange(32)
    for c in range(16):
        g = gl[c // 4]
        own = 2 * g + (c % 4) // 2
        pmA[:, c, 0, :] = np.where(n < own, 0.0, -1e30)[None, :]
        pmA[:, c, 1, :] = (n < 2 * g).astype(f32)[None, :]
        pmA[:, c, 2, :] = ((n >= 2 * g) & (n < own)).astype(f32)[None, :]
        pmA[:, c, 3, :] = (n == own).astype(f32)[None, :]
    keys = np.arange(S)
    ohA = (keys[None, :] // 256 == np.arange(32)[:, None]).astype(f32)
    ohC = (keys[None, :] // 512 == np.arange(16)[:, None]).astype(f32)
    pos = core_pos(j)
    ohA_own = (pos[None, :] // 256 == np.arange(32)[:, None]).astype(f32)
    cbC = np.where(np.arange(16)[:, None] < (pos[None, :] // 512), 0.0, NEGV).astype(f32)
    hval = np.zeros((128, 4), f32)
    for s, g in enumerate(gl):
        hval[:, s] = 1.0 if g > 0 else 0.0
    return {"mdiag": mdiag.astype(BF), "mB": mB.astype(BF), "pmA": pmA, "ohA": ohA.astype(BF), "ohC": ohC.astype(BF),
            "ohA_own": ohA_own.astype(BF), "cbC": cbC.astype(BF), "hval": hval}


def gather_kv(qkT_list, v_list):
    kT = np.zeros((640, S), BF)
    vf = np.zeros((S, 640), BF)
    for j in range(4):
        pos = core_pos(j)
        q = qkT_list[j]
        kT[0:256, pos] = q[256:512]
        kT[256:384, pos] = q[1024:1152]
        kT[384:640, pos] = q[1408:1664]
        vf[pos] = v_list[j]
    return kT, vf


def a_inputs(j, l, qkT_own, v_own, kT_full, v_full, gT_own, x_core, inp):
    d = dict(a_consts(j))
    d["qkT"] = np.ascontiguousarray(qkT_own)
    d["v_own"] = np.ascontiguousarray(v_own)
    d["kT_full"] = np.ascontiguousarray(kT_full)
    d["v_hp"] = np.ascontiguousarray(v_full.reshape(64, 128, 10, 64).transpose(2, 1, 0, 3))
    kh = np.zeros((128, 512), BF)
    vh = np.zeros((512, 128), BF)
    for s, g in enumerate(GROUPS[j]):
        if g > 0:
            kh[:, s * 128:(s + 1) * 128] = kT_full[256:384, g * 512 - 128:g * 512]
            vh[s * 128:(s + 1) * 128, :] = v_full[g * 512 - 128:g * 512, 256:384]
    d["kTb_halo"] = kh
    d["vb_halo"] = vh
    d["gT"] = np.ascontiguousarray(gT_own)
    d["x"] = np.ascontiguousarray(x_core)
    lamv = np.stack([inp["lam_q1"][l], inp["lam_k1"][l], inp["lam_q2"][l], inp["lam_k2"][l]]).astype(np.float32)
    d["lamv"] = np.ascontiguousarray(np.broadcast_to(lamv[None], (128, 4, 32)))
    d["sgc"] = np.ascontiguousarray(np.tile(inp["subln"][l], 2).reshape(128, 1).astype(np.float32))
    d["sinks"] = rep128(inp["sinks"][l])
    for nm in ("w_pa", "w_pb", "w_pc", "w_out"):
        d[nm] = np.ascontiguousarray(inp[nm][l])
    return d


A_IN_SPECS = [("qkT", [1664, 2048], "bf"), ("v_own", [2048, 640], "bf"), ("kT_full", [640, 8192], "bf"),
              ("v_hp", [10, 128, 64, 64], "bf"), ("kTb_halo", [128, 512], "bf"), ("vb_halo", [512, 128], "bf"),
              ("gT", [3072, 2048], "bf"), ("x", [2048, 1024], "f"), ("mdiag", [128, 4, 512], "bf"), ("mB", [128, 2, 512], "bf"),
              ("pmA", [128, 16, 4, 32], "f"), ("ohA", [32, 8192], "bf"), ("ohC", [16, 8192], "bf"), ("ohA_own", [32, 2048], "bf"),
              ("cbC", [16, 2048], "bf"), ("hval", [128, 4], "f"), ("lamv", [128, 4, 32], "f"), ("sgc", [128, 1], "f"),
              ("sinks", [128, 8], "f"), ("w_pa", [256, 1024], "f"), ("w_pb", [512, 1024], "f"), ("w_pc", [256, 1024], "f"),
              ("w_out", [1024, 1024], "f")]


def _make_ident(kb):
    idb = kb.sb("idb", [128, 128], BF16)
    r_id = kb.res("idb")
    idf = kb.sb("idf", [128, 128], F32)
    kb.op("pool", lambda e: e.memset(idf[:], 0.0), writes=[r_id])
    kb.op("pool", lambda e: e.affine_select(out=idf[:], in_=idf[:], pattern=[[-1, 128]], compare_op=ALU.not_equal,
                                            fill=1.0, base=0, channel_multiplier=1), reads=[r_id], writes=[r_id])
    kb.op("pool", lambda e: e.tensor_copy(out=idb[:], in_=idf[:]), reads=[r_id], writes=[r_id])
    return idb, r_id


def _build(kind, lam_init=0.0):
    nc = bass.Bass("TRN2", target_bir_lowering=False)

    def di(n, s, dt=F32):
        return nc.dram_tensor(n, s, dt, kind="ExternalInput").ap()

    def do(n, s, dt):
        return nc.dram_tensor(n, s, dt, kind="ExternalOutput").ap()

    with ExitStack() as st:
        kb = KB(nc, st)
        if kind == "P":
            D = {"x": di("x", [2048, 1024]), "anorm": di("anorm", [128, 1024]), "w_in": di("w_in", [1024, 5376]),
                 "gall": di("gall", [128, 2048]), "ct64": di("ct64", [2048, 64]), "st64": di("st64", [2048, 64]),
                 "ct32": di("ct32", [2048, 32]), "st32": di("st32", [2048, 32]),
                 "qkT": do("qkT", [1664, 2048], BF16), "v": do("v", [2048, 640], BF16), "gT": do("gT", [3072, 2048], BF16)}
            fin = emit_P(kb, D, _make_ident(kb))
        elif kind == "A":
            D = {}
            for n, s, dt in A_IN_SPECS:
                if n in ("gT", "x", "w_pa", "w_pb", "w_pc", "w_out"):
                    continue
                D[n] = di(n, s, BF16 if dt == "bf" else F32)
            D["yT_out"] = do("yT_out", [1024, 2048], BF16)
            fin = emit_A(kb, D, _make_ident(kb), lam_init)
        elif kind == "M":
            D = {"yT_in": di("yT_in", [1024, 2048], BF16), "gT": di("gT", [3072, 2048], BF16), "x": di("x", [2048, 1024]),
                 "w_pa": di("w_pa", [256, 1024]), "w_pb": di("w_pb", [512, 1024]), "w_pc": di("w_pc", [256, 1024]),
                 "w_out": di("w_out", [1024, 1024]), "xmid": do("xmid", [2048, 1024], F32)}
            fin = emit_M(kb, D)
        else:
            D = {"xm": di("xm", [2048, 1024]), "xhalo": di("xhalo", [8, 1024]), "mnorm": di("mnorm", [128, 1024]),
                 "w_up": di("w_up", [1024, 5632]), "convp": di("convp", [128, 44, 4]), "w_down": di("w_down", [2816, 1024]),
                 "xo": do("xo", [2048, 1024], F32)}
            fin = emit_F(kb, D, _make_ident(kb))
        kb.finish(fin)
    return nc


def _run(nc, in_maps):
    res = run_bass_kernel_spmd(nc, in_maps, core_ids=list(range(8)))
    return res.results


def kernel(**inp):
    inp = {k: np.asarray(v) for k, v in inp.items()}
    x = inp["x"].astype(np.float32)
    cores = [(c // 4, c % 4) for c in range(8)]
    xs = [np.ascontiguousarray(x[b][core_pos(j)]) for b, j in cores]
    for l in range(2):
        lam_init = 0.8 - 0.6 * float(np.exp(-0.3 * l))
        rp = _run(_build("P"), [p_inputs(xs[c], l, cores[c][1], inp) for c in range(8)])
        full = {}
        for b in range(2):
            full[b] = gather_kv([rp[4 * b + j]["qkT"] for j in range(4)], [rp[4 * b + j]["v"] for j in range(4)])
        a_maps = []
        for c, (b, j) in enumerate(cores):
            d = a_inputs(j, l, rp[c]["qkT"], rp[c]["v"], full[b][0], full[b][1], rp[c]["gT"], xs[c], inp)
            for k in ("gT", "x", "w_pa", "w_pb", "w_pc", "w_out"):
                d.pop(k)
            a_maps.append(d)
        ra = _run(_build("A", lam_init), a_maps)
        m_maps = []
        for c in range(8):
            d = {"yT_in": ra[c]["yT_out"], "gT": rp[c]["gT"], "x": xs[c]}
            for nm in ("w_pa", "w_pb", "w_pc", "w_out"):
                d[nm] = np.ascontiguousarray(inp[nm][l])
            m_maps.append(d)
        rm = _run(_build("M"), m_maps)
        xm_full = np.zeros((2, S, 1024), np.float32)
        for c, (b, j) in enumerate(cores):
            xm_full[b][core_pos(j)] = rm[c]["xmid"]
        rf = _run(_build("F"), [f_inputs(rm[c]["xmid"], xm_full[cores[c][0]], l, cores[c][1], inp) for c in range(8)])
        xs = [np.ascontiguousarray(rf[c]["xo"]) for c in range(8)]
    out = np.zeros((2, S, 1024), np.float32)
    for c, (b, j) in enumerate(cores):
        out[b][core_pos(j)] = xs[c]
    return out
```

```python
import numpy as np
import ml_dtypes
from contextlib import ExitStack
import concourse.bass as bass
import concourse.mybir as mybir
from concourse.bass_utils import run_bass_kernel_spmd


F32 = mybir.dt.float32
BF16 = mybir.dt.bfloat16
AF = mybir.ActivationFunctionType
ALU = mybir.AluOpType
AX = mybir.AxisListType


class Ev:
    __slots__ = ("sem", "val")

    def __init__(self, sem, val):
        self.sem = sem
        self.val = val


class Res:
    def __init__(self, name):
        self.name = name
        self.w = None
        self.r = []
        self.dsem = None
        self.dcnt = 0


class KB:
    ENGS = ("pe", "dve", "act", "pool", "sp")

    def __init__(self, nc, stack):
        self.nc = nc
        self.stack = stack
        self.eng = {"pe": nc.tensor, "dve": nc.vector, "act": nc.scalar,
                    "pool": nc.gpsimd, "sp": nc.sync}
        self.sem = {e: stack.enter_context(nc.semaphore("s_" + e)) for e in self.ENGS}
        self.cnt = {e: 0 for e in self.ENGS}
        self.seen = {e: {} for e in self.ENGS}
        self.stream = {e: [] for e in self.ENGS}
        self.nsem = len(self.ENGS)
        self.allres = []

    def res(self, name):
        r = Res(name)
        self.allres.append(r)
        return r

    def sb(self, name, shape, dt):
        t = self.stack.enter_context(self.nc.sbuf_tensor(name, list(shape), dt))
        return t

    def ps(self, name, shape, dt):
        return self.stack.enter_context(self.nc.psum_tensor(name, list(shape), dt))

    def _waits(self, e, reads, writes, nosame):
        need = {}

        def add(ev, r):
            if ev is None:
                return
            if r in nosame and ev.sem is self.sem[e]:
                return
            k = id(ev.sem)
            if k not in need or need[k][1] < ev.val:
                need[k] = (ev.sem, ev.val)

        for r in reads:
            add(r.w, r)
        for w in writes:
            add(w.w, w)
            for ev in w.r:
                add(ev, w)
        out = []
        seen = self.seen[e]
        for k, (s, v) in need.items():
            if seen.get(k, 0) >= v:
                continue
            seen[k] = v
            out.append((s, v))
        return out

    def op(self, e, fn, reads=(), writes=(), signal=True, nosame=()):
        waits = self._waits(e, reads, writes, nosame)
        if signal:
            self.cnt[e] += 1
            ev = Ev(self.sem[e], self.cnt[e])
            sig = (self.sem[e], 1)
        else:
            ev = Ev(self.sem[e], self.cnt[e] + 1)
            sig = None
        self.stream[e].append((waits, fn, sig))
        for r in reads:
            r.r.append(ev)
        for w in writes:
            w.w = ev
            w.r = []
        return ev

    def dma(self, q, out_ap, in_ap, reads=(), writes=(), **kw):
        tr = (list(writes) + list(reads))[0]
        if tr.dsem is None:
            tr.dsem = self.stack.enter_context(self.nc.semaphore("d_" + tr.name))
            self.nsem += 1
        waits = [w for w in self._waits(q, reads, writes, ()) if w[0] is not tr.dsem]
        tr.dcnt += 16
        ev = Ev(tr.dsem, tr.dcnt)
        self.stream[q].append(
            (waits, lambda en: en.dma_start(out=out_ap, in_=in_ap, **kw), (tr.dsem, 16)))
        for r in reads:
            r.r.append(ev)
        for w in writes:
            w.w = ev
            w.r = []
        return ev

    def finish(self, final_res):
        waits = self._waits("sp", final_res, final_res, ())
        self.stream["sp"].append((waits, None, None))
        nc = self.nc
        with nc.Block() as block:
            def mk(e):
                def body(en):
                    for waits, fn, sig in self.stream[e]:
                        for s, v in waits:
                            en.wait_ge(s, v)
                        if fn is None:
                            continue
                        ins = fn(en)
                        if sig is not None:
                            ins.then_inc(sig[0], sig[1])
                return body
            block.tensor(mk("pe"))
            block.vector(mk("dve"))
            block.scalar(mk("act"))
            block.gpsimd(mk("pool"))
            block.sync(mk("sp"))


def _mk(KBc):
    def tt(self, e, out, in0, in1, op, reads, writes):
        return self.op(e, lambda en: en.tensor_tensor(out=out, in0=in0, in1=in1, op=op), reads, writes)

    def ts(self, e, out, in0, s1, s2, op0, op1=None, reads=(), writes=()):
        if op1 is None:
            return self.op(e, lambda en: en.tensor_scalar(out=out, in0=in0, scalar1=s1, scalar2=None, op0=op0), reads, writes)
        return self.op(e, lambda en: en.tensor_scalar(out=out, in0=in0, scalar1=s1, scalar2=s2, op0=op0, op1=op1), reads, writes)

    def stt(self, out, in0, scalar, in1, op0, op1, reads, writes):
        return self.op("dve", lambda en: en.scalar_tensor_tensor(out=out, in0=in0, scalar=scalar, in1=in1, op0=op0, op1=op1), reads, writes)

    def cp(self, e, out, in_, reads, writes):
        if e == "act":
            return self.op(e, lambda en: en.activation(out=out, in_=in_, func=AF.Copy), reads, writes)
        return self.op(e, lambda en: en.tensor_copy(out=out, in_=in_), reads, writes)

    def act(self, out, in_, func, reads, writes, bias=None, scale=None, accum_out=None):
        kw = {}
        if bias is not None:
            kw["bias"] = bias
        if scale is not None:
            kw["scale"] = scale
        if accum_out is not None:
            kw["accum_out"] = accum_out
        return self.op("act", lambda en: en.activation(out=out, in_=in_, func=func, **kw), reads, writes)

    def mm(self, out, lhsT, rhs, start, stop, reads, writes, signal=None):
        if signal is None:
            signal = stop
        return self.op("pe", lambda en: en.matmul(out, lhsT=lhsT, rhs=rhs, start=start, stop=stop),
                       reads, writes, signal=signal, nosame=writes)

    def tr(self, out, in_, ident, reads, writes, signal=True):
        return self.op("pe", lambda en: en.transpose(out=out, in_=in_, identity=ident), reads, writes,
                       signal=signal, nosame=writes)

    def red(self, out, in_, reads, writes, op=None):
        op = op or ALU.add
        return self.op("dve", lambda en: en.tensor_reduce(out=out, in_=in_, axis=AX.X, op=op), reads, writes)

    def ms(self, e, ap, val, writes):
        return self.op(e, lambda en: en.memset(ap, val), (), writes)

    for f in (tt, ts, stt, cp, act, mm, tr, red, ms):
        setattr(KBc, f.__name__, f)


_mk(KB)


def _mk2(KBc):
    def coll(self, kind, in_ap, out_ap, groups, reads=(), writes=(), inc=1):
        if not hasattr(self, "cc_sem"):
            self.cc_sem = self.stack.enter_context(self.nc.semaphore("cc_sem"))
            self.cc_cnt = 0
        waits = self._waits("pool", reads, writes, ())
        self.cc_cnt += inc
        ev = Ev(self.cc_sem, self.cc_cnt)
        op = ALU.bypass
        self.stream["pool"].append(
            (waits, lambda en: en.collective_compute(kind, op, replica_groups=groups, ins=[in_ap], outs=[out_ap]),
             (self.cc_sem, inc)))
        for r in reads:
            r.r.append(ev)
        for w in writes:
            w.w = ev
            w.r = []
        return ev

    def barrier(self):
        evs = [(self.sem[e], self.cnt[e]) for e in self.ENGS if self.cnt[e] > 0]
        for r in self.allres:
            if r.dsem is not None and r.dcnt > 0:
                evs.append((r.dsem, r.dcnt))
        if hasattr(self, "cc_sem") and self.cc_cnt > 0:
            evs.append((self.cc_sem, self.cc_cnt))
        for e in self.ENGS:
            seen = self.seen[e]
            waits = []
            for s, v in evs:
                if s is self.sem[e]:
                    continue
                if seen.get(id(s), 0) >= v:
                    continue
                seen[id(s)] = v
                waits.append((s, v))
            self.stream[e].append((waits, None, None))

    KBc.coll = coll
    KBc.barrier = barrier


_mk2(KB)


NEG = -30000.0
EPS = 1e-6
GROUPS = [[j, 7 - j, 8 + j, 15 - j] for j in range(4)]
NT = 16
SEGS = [(0, 8, 64), (768, 10, 64), (1536, 16, 32)]
TBLK = [0, 128, 256, 384, 768, 896, 1024, 1152, 1280, 1536, 1664, 1792, 1920]
VSEG = [(512, 256), (1408, 128), (2048, 256)]


def emit_P(kb, D, ident):
    nc = kb.nc
    x, anorm, w_in, gall, ct64, st64, ct32, st32 = (D[k] for k in ("x", "anorm", "w_in", "gall", "ct64", "st64", "ct32", "st32"))
    qkT, vout, gT = D["qkT"], D["v"], D["gT"]
    idb, r_id = ident

    wq = kb.sb("wq", [128, 8, 2304], BF16); r_wq = kb.res("wq")
    wg = [kb.sb(f"wg{i}", [128, 8, 512], BF16) for i in range(2)]; r_wg = [kb.res(f"wg{i}") for i in range(2)]
    hT = kb.sb("hT", [128, 8, 2048], BF16); r_hT = [kb.res(f"hT{t}") for t in range(NT)]
    xt = [kb.sb(f"xt{i}", [128, 1024], F32) for i in range(2)]; r_xt = [kb.res(f"xt{i}") for i in range(2)]
    h16 = [kb.sb(f"h16{i}", [128, 1024], BF16) for i in range(2)]; r_h16 = [kb.res(f"h16{i}") for i in range(2)]
    pj = [kb.sb(f"pj{i}", [128, 2304], F32) for i in range(2)]; r_pj = [kb.res(f"pj{i}") for i in range(2)]
    xc = kb.sb("xc", [128, 2048], F32); r_xc = kb.res("xc")
    xs = kb.sb("xs", [128, 2048], F32); r_xs = kb.res("xs")
    qk16 = [kb.sb(f"qk16{i}", [128, 2048], BF16) for i in range(2)]; r_qk16 = [kb.res(f"qk16{i}") for i in range(2)]
    qkTs = [kb.sb(f"qkTs{i}", [128, 13, 128], BF16) for i in range(2)]; r_qkTs = [kb.res(f"qkTs{i}") for i in range(2)]
    v16 = [kb.sb(f"v16{i}", [128, 640], BF16) for i in range(2)]; r_v16 = [kb.res(f"v16{i}") for i in range(2)]
    g16 = [kb.sb(f"g16{i}", [128, 512], BF16) for i in range(2)]; r_g16 = [kb.res(f"g16{i}") for i in range(2)]
    an = kb.sb("an", [128, 1024], F32); r_an = kb.res("an")
    ga = kb.sb("ga", [128, 2048], F32); r_ga = kb.res("ga")
    c64 = kb.sb("c64", [128, NT, 64], F32); s64 = kb.sb("s64", [128, NT, 64], F32)
    c32 = kb.sb("c32", [128, NT, 32], F32); s32 = kb.sb("s32", [128, NT, 32], F32)
    r_tab = kb.res("tabs")
    epst = kb.sb("epst", [128, 1], F32); r_eps = kb.res("eps")
    st = [kb.sb(f"st{i}", [128, 40], F32) for i in range(2)]; r_st = [kb.res(f"st{i}") for i in range(2)]
    sx = [kb.sb(f"sx{i}", [128, 4], F32) for i in range(2)]; r_sx = [kb.res(f"sx{i}") for i in range(2)]
    ps_t = [kb.ps(f"ps_t{i}", [128, 1024], BF16) for i in range(2)]; r_ps_t = [kb.res(f"ps_t{i}") for i in range(2)]
    ps_m = [kb.ps(f"ps_m{i}", [128, 512], F32) for i in range(4)]; r_ps_m = [kb.res(f"ps_m{i}") for i in range(4)]
    ps_q = [kb.ps(f"ps_q{i}", [128, 1024], BF16) for i in range(2)]; r_ps_q = [kb.res(f"ps_q{i}") for i in range(2)]

    kb.ms("pool", epst[:], EPS, [r_eps])
    kb.dma("sp", an[:], anorm[:, :], writes=[r_an])
    kb.dma("sp", ga[:], gall[:, :], writes=[r_ga])
    kb.dma("sp", c64[:], ct64.rearrange("(t p) d -> p t d", p=128), writes=[r_tab])
    kb.dma("sp", s64[:], st64.rearrange("(t p) d -> p t d", p=128), writes=[r_tab])
    kb.dma("sp", c32[:], ct32.rearrange("(t p) d -> p t d", p=128), writes=[r_tab])
    kb.dma("sp", s32[:], st32.rearrange("(t p) d -> p t d", p=128), writes=[r_tab])
    wv = w_in.rearrange("(k p) c -> p k c", p=128)
    for k in range(8):
        for c0 in range(0, 2304, 1152):
            kb.dma("pool", wq[:, k, c0:c0 + 1152], wv[:, k, c0:c0 + 1152], writes=[r_wq])

    for t in range(NT):
        b = t % 2
        kb.dma("sp", xt[b][:], x[t * 128:(t + 1) * 128, :], writes=[r_xt[b]])
        kb.act(h16[b][:], xt[b][:], AF.Square, [r_xt[b]], [r_h16[b], r_sx[b]], accum_out=sx[b][:, 0:1])
        kb.act(sx[b][:, 1:2], sx[b][:, 0:1], AF.Ln, [r_sx[b], r_eps], [r_sx[b]], bias=epst[:], scale=1.0 / 1024)
        kb.act(sx[b][:, 2:3], sx[b][:, 1:2], AF.Exp, [r_sx[b]], [r_sx[b]], scale=-0.5)
        kb.stt(h16[b][:], xt[b][:], sx[b][:, 2:3], an[:], ALU.mult, ALU.mult, [r_xt[b], r_sx[b], r_an], [r_h16[b]])
        for k in range(8):
            kb.tr(ps_t[b][:, k * 128:(k + 1) * 128], h16[b][:, k * 128:(k + 1) * 128], idb[:],
                  [r_h16[b], r_id], [r_ps_t[b]], signal=(k == 7))
        kb.cp("act", hT[:, :, t * 128:(t + 1) * 128], ps_t[b][:].rearrange("p (k t) -> p k t", k=8), [r_ps_t[b]], [r_hT[t]])

    mcnt = 0
    for t in range(NT):
        b = t % 2
        for ci, (c0, cw) in enumerate([(0, 512), (512, 512), (1024, 512), (1536, 512), (2048, 256)]):
            pm = mcnt % 4; mcnt += 1
            for k in range(8):
                kb.mm(ps_m[pm][:, 0:cw], hT[:, k, t * 128:(t + 1) * 128], wq[:, k, c0:c0 + cw], k == 0, k == 7,
                      [r_hT[t], r_wq], [r_ps_m[pm]])
            kb.cp("act", pj[b][:, c0:c0 + cw], ps_m[pm][:, 0:cw], [r_ps_m[pm]], [r_pj[b]])
        vo = 0
        for (c0, cw) in VSEG:
            kb.cp("pool", v16[b][:, vo:vo + cw], pj[b][:, c0:c0 + cw], [r_pj[b]], [r_v16[b]])
            vo += cw
        kb.dma("sp", vout[t * 128:(t + 1) * 128, :], v16[b][:], reads=[r_v16[b]])
        so = 0
        for (c0, nh, d) in SEGS:
            kb.act(xs[:, c0:c0 + nh * d], pj[b][:, c0:c0 + nh * d], AF.Square, [r_pj[b]], [r_xs])
            kb.red(st[b][:, so:so + nh], xs[:, c0:c0 + nh * d].rearrange("p (h d) -> p h d", d=d), [r_xs], [r_st[b]])
            so += nh
        kb.act(st[b][:, 0:18], st[b][:, 0:18], AF.Ln, [r_st[b], r_eps], [r_st[b]], bias=epst[:], scale=1.0 / 64)
        kb.act(st[b][:, 18:34], st[b][:, 18:34], AF.Ln, [r_st[b], r_eps], [r_st[b]], bias=epst[:], scale=1.0 / 32)
        kb.act(st[b][:, 0:34], st[b][:, 0:34], AF.Exp, [r_st[b]], [r_st[b]], scale=-0.5)
        so = 0
        for si, (c0, nh, d) in enumerate(SEGS):
            w = nh * d
            e1 = "dve" if si != 1 else "pool"
            pv = pj[b][:, c0:c0 + w].rearrange("p (h d) -> p h d", d=d)
            rb = st[b][:, so:so + nh].rearrange("p (h o) -> p h o", o=1).to_broadcast([128, nh, d])
            kb.tt(e1, pv, pv, rb, ALU.mult, [r_pj[b], r_st[b]], [r_pj[b]])
            kb.tt(e1, pj[b][:, c0:c0 + w], pj[b][:, c0:c0 + w], ga[:, c0:c0 + w], ALU.mult, [r_pj[b], r_ga], [r_pj[b]])
            hd = d // 2
            ctab = (c64 if d == 64 else c32)[:, t, :]
            stab = (s64 if d == 64 else s32)[:, t, :]
            cb = ctab.rearrange("p (o d) -> p o d", o=1).to_broadcast([128, nh, d])
            kb.tt(e1, xc[:, c0:c0 + w].rearrange("p (h d) -> p h d", d=d), pv, cb, ALU.mult, [r_pj[b], r_tab], [r_xc])
            p4 = pj[b][:, c0:c0 + w].rearrange("p (h two e) -> p h two e", two=2, e=hd)
            x4 = xs[:, c0:c0 + w].rearrange("p (h two e) -> p h two e", two=2, e=hd)
            s0 = stab[:, 0:hd].rearrange("p (o d) -> p o d", o=1).to_broadcast([128, nh, hd])
            s1 = stab[:, hd:d].rearrange("p (o d) -> p o d", o=1).to_broadcast([128, nh, hd])
            kb.tt(e1, x4[:, :, 0, :], p4[:, :, 1, :], s0, ALU.mult, [r_pj[b], r_tab], [r_xs])
            kb.tt(e1, x4[:, :, 1, :], p4[:, :, 0, :], s1, ALU.mult, [r_pj[b], r_tab], [r_xs])
            kb.tt(e1, qk16[b][:, c0:c0 + w], xc[:, c0:c0 + w], xs[:, c0:c0 + w], ALU.add, [r_xc, r_xs], [r_qk16[b]])
            so += nh
        for bi, c0 in enumerate(TBLK):
            half = 0 if bi < 8 else 1
            col = (bi % 8) * 128
            kb.tr(ps_q[half][:, col:col + 128], qk16[b][:, c0:c0 + 128], idb[:], [r_qk16[b], r_id], [r_ps_q[half]],
                  signal=(bi == 7 or bi == 12))
        kb.cp("dve", qkTs[b][:, 0:8, :], ps_q[0][:].rearrange("p (k t) -> p k t", k=8), [r_ps_q[0]], [r_qkTs[b]])
        kb.cp("dve", qkTs[b][:, 8:13, :], ps_q[1][:, 0:640].rearrange("p (k t) -> p k t", k=5), [r_ps_q[1]], [r_qkTs[b]])
        kb.dma("sp", qkT[:, t * 128:(t + 1) * 128].rearrange("(k p) t -> p k t", p=128), qkTs[b][:], reads=[r_qkTs[b]])

    gcnt = 0
    for wc in range(6):
        wb = wc % 2
        for k in range(8):
            kb.dma("pool", wg[wb][:, k, :], wv[:, k, 2304 + wc * 512:2304 + (wc + 1) * 512], writes=[r_wg[wb]])
        for cc in range(4):
            for g in range(4):
                pm = mcnt % 4; mcnt += 1
                for k in range(8):
                    kb.mm(ps_m[pm][:], wg[wb][:, k, cc * 128:(cc + 1) * 128], hT[:, k, g * 512:(g + 1) * 512],
                          k == 0, k == 7, [r_wg[wb]] + r_hT[4 * g:4 * g + 4], [r_ps_m[pm]])
                gb = gcnt % 2; gcnt += 1
                kb.act(g16[gb][:], ps_m[pm][:], AF.Sigmoid, [r_ps_m[pm]], [r_g16[gb]])
                row = (wc * 4 + cc) * 128
                kb.dma("sp", gT[row:row + 128, g * 512:(g + 1) * 512], g16[gb][:], reads=[r_g16[gb]])
    return [r_v16[0], r_v16[1], r_qkTs[0], r_qkTs[1], r_g16[0], r_g16[1]]


EPS = 1e-6
NT = 16
NC2 = 22


def emit_F(kb, D, ident, pfx="f"):
    xm, xhalo, mnorm, w_up, convp, w_down, xo = (D[k] for k in ("xm", "xhalo", "mnorm", "w_up", "convp", "w_down", "xo"))
    idb, r_id = ident
    P = pfx
    hT = kb.sb(P + "hT", [128, 8, 2048], BF16); r_hT = [kb.res(P + f"hT{t}") for t in range(NT)]
    hTh = kb.sb(P + "hTh", [128, 8, 8], BF16); r_hTh = kb.res(P + "hTh")
    mT = kb.sb(P + "mT", [128, NC2, 1024], BF16); r_mT = [kb.res(P + f"mT{g}") for g in range(2)]
    wd = kb.sb(P + "wd", [128, NC2, 1024], BF16); r_wd = kb.res(P + "wd")
    wu = [kb.sb(P + f"wu{i}", [128, 8, 2, 128], BF16) for i in range(2)]; r_wu = [kb.res(P + f"wu{i}") for i in range(2)]
    xt = [kb.sb(P + f"xt{i}", [128, 1024], F32) for i in range(2)]; r_xt = [kb.res(P + f"xt{i}") for i in range(2)]
    h16 = [kb.sb(P + f"h16{i}", [128, 1024], BF16) for i in range(2)]; r_h16 = [kb.res(P + f"h16{i}") for i in range(2)]
    an = kb.sb(P + "an", [128, 1024], F32); r_an = kb.res(P + "an")
    cpar = kb.sb(P + "cpar", [128, 44, 4], F32); r_cp = kb.res(P + "cpar")
    epst = kb.sb(P + "epst", [128, 1], F32); r_eps = kb.res(P + "eps")
    sx = [kb.sb(P + f"sx{i}", [128, 4], F32) for i in range(2)]; r_sx = [kb.res(P + f"sx{i}") for i in range(2)]
    ub = [[kb.sb(P + f"ub{i}{s}", [128, 514], F32) for s in range(2)] for i in range(2)]
    r_ub = [[kb.res(P + f"ub{i}{s}") for s in range(2)] for i in range(2)]
    yb = [[kb.sb(P + f"yb{i}{s}", [128, 512], F32) for s in range(2)] for i in range(2)]
    r_yb = [[kb.res(P + f"yb{i}{s}") for s in range(2)] for i in range(2)]
    ob = [kb.sb(P + f"ob{i}", [128, 1024], F32) for i in range(2)]; r_ob = [kb.res(P + f"ob{i}") for i in range(2)]
    ps_t = kb.ps(P + "ps_t", [128, 1024], BF16); r_ps_t = kb.res(P + "ps_t")
    pu = [[kb.ps(P + f"pu{i}{s}", [128, 512], F32) for s in range(2)] for i in range(2)]
    r_pu = [[kb.res(P + f"pu{i}{s}") for s in range(2)] for i in range(2)]
    ph = kb.ps(P + "ph", [128, 16], F32); r_ph = kb.res(P + "ph")
    po = [kb.ps(P + f"po{i}", [128, 512], F32) for i in range(2)]; r_po = [kb.res(P + f"po{i}") for i in range(2)]

    kb.ms("pool", epst[:], EPS, [r_eps])
    kb.dma("sp", an[:], mnorm[:, :], writes=[r_an])
    kb.dma("sp", cpar[:], convp[:, :, :], writes=[r_cp])
    wdv = w_down.rearrange("(c p) n -> p c n", p=128)
    for c in range(NC2):
        kb.dma("pool", wd[:, c, :], wdv[:, c, :], writes=[r_wd])

    for t in range(NT + 1):
        b = t % 2
        n = 128 if t < NT else 8
        src = xm[t * 128:(t + 1) * 128, :] if t < NT else xhalo[:, :]
        kb.dma("sp", xt[b][0:n, :], src, writes=[r_xt[b]])
        kb.act(h16[b][0:n, :], xt[b][0:n, :], AF.Square, [r_xt[b]], [r_h16[b], r_sx[b]], accum_out=sx[b][0:n, 0:1])
        kb.act(sx[b][0:n, 1:2], sx[b][0:n, 0:1], AF.Ln, [r_sx[b], r_eps], [r_sx[b]], bias=epst[0:n, :], scale=1.0 / 1024)
        kb.act(sx[b][0:n, 2:3], sx[b][0:n, 1:2], AF.Exp, [r_sx[b]], [r_sx[b]], scale=-0.5)
        kb.stt(h16[b][0:n, :], xt[b][0:n, :], sx[b][0:n, 2:3], an[0:n, :], ALU.mult, ALU.mult, [r_xt[b], r_sx[b], r_an], [r_h16[b]])
        for k in range(8):
            kb.tr(ps_t[:, k * 128:k * 128 + n], h16[b][0:n, k * 128:(k + 1) * 128], idb[0:n, 0:n],
                  [r_h16[b], r_id], [r_ps_t], signal=(k == 7))
        if t < NT:
            kb.cp("act", hT[:, :, t * 128:(t + 1) * 128], ps_t[:].rearrange("p (k t) -> p k t", k=8), [r_ps_t], [r_hT[t]])
        else:
            kb.cp("act", hTh[:], ps_t[:].rearrange("p (k t) -> p k t", k=8)[:, :, 0:8], [r_ps_t], [r_hTh])

    wuv = w_up.rearrange("(k p) c -> p k c", p=128)
    it = 0
    for half in range(2):
        for c in range(NC2):
            wb = it % 2; it += 1
            for s in range(2):
                col = s * 2816 + c * 128
                kb.dma("pool", wu[wb][:, :, s, :], wuv[:, :, col:col + 128], writes=[r_wu[wb]])
            for s in range(2):
                for k in range(8):
                    kb.mm(ph[:, s * 8:(s + 1) * 8], wu[wb][:, k, s, :], hTh[:, k, :], k == 0, k == 7,
                          [r_wu[wb], r_hTh], [r_ph], signal=(s == 1 and k == 7))
            for gi in range(2):
                g = half * 2 + gi
                ib = (c * 2 + gi) % 2
                for s in range(2):
                    for k in range(8):
                        kb.mm(pu[ib][s][:], wu[wb][:, k, s, :], hT[:, k, g * 512:(g + 1) * 512], k == 0, k == 7,
                              [r_wu[wb]] + r_hT[4 * g:4 * g + 4], [r_pu[ib][s]])
                for s in range(2):
                    ci = s * NC2 + c
                    u, ru = ub[ib][s], r_ub[ib][s]
                    y, ry = yb[ib][s], r_yb[ib][s]
                    kb.cp("dve", u[:, 0:2], ph[:, s * 8 + g * 2:s * 8 + g * 2 + 2], [r_ph], [ru])
                    kb.cp("act", u[:, 2:514], pu[ib][s][:], [r_pu[ib][s]], [ru])
                    kb.act(y[:], pu[ib][s][:], AF.Identity, [r_pu[ib][s], r_cp], [ry], bias=cpar[:, ci, 3:4], scale=cpar[:, ci, 2:3])
                    kb.stt(y[:], u[:, 1:513], cpar[:, ci, 1:2], y[:], ALU.mult, ALU.add, [ru, ry, r_cp], [ry])
                    kb.stt(y[:], u[:, 0:512], cpar[:, ci, 0:1], y[:], ALU.mult, ALU.add, [ru, ry, r_cp], [ry])
                yg, yv = yb[ib][0], yb[ib][1]
                kb.act(yg[:], yg[:], AF.Silu, [r_yb[ib][0]], [r_yb[ib][0]])
                kb.tt("dve", mT[:, c, gi * 512:(gi + 1) * 512], yg[:], yv[:], ALU.mult, [r_yb[ib][0], r_yb[ib][1]], [r_mT[gi]])
        for tt_ in range(8):
            t = half * 8 + tt_
            b = t % 2
            kb.dma("sp", xt[b][:], xm[t * 128:(t + 1) * 128, :], writes=[r_xt[b]])
            for hc in range(2):
                for c in range(NC2):
                    kb.mm(po[hc][:], mT[:, c, tt_ * 128:(tt_ + 1) * 128], wd[:, c, hc * 512:(hc + 1) * 512], c == 0, c == NC2 - 1,
                          [r_mT[tt_ // 4], r_wd], [r_po[hc]])
                kb.tt("dve", ob[b][:, hc * 512:(hc + 1) * 512], po[hc][:], xt[b][:, hc * 512:(hc + 1) * 512], ALU.add,
                      [r_po[hc], r_xt[b]], [r_ob[b]])
            kb.dma("sp", xo[t * 128:(t + 1) * 128, :], ob[b][:], reads=[r_ob[b]])
    return [r_ob[0], r_ob[1]]


NEG = -30000.0
BIG = 30000.0
EPS = 1e-6
NSTREAM = [12, 28, 44, 60]
FAST_RECIP = False


def RECIP(en):
    return en.reciprocal_approx_fast if FAST_RECIP else en.reciprocal


STOP = 99


def emit_A(kb, D, ident, lam_init, pfx="a"):
    P = pfx
    idb, r_id = ident
    qkT, vown_d, kTf, vhp, kTbh, vbh_d = (D[k] for k in ("qkT", "v_own", "kT_full", "v_hp", "kTb_halo", "vb_halo"))

    def S(name, shape, dt):
        return kb.sb(P + name, shape, dt), kb.res(P + name)

    kbuf = kb.sb(P + "kbuf", [128, 2, 8192], BF16)
    r_kb2 = [kb.res(P + "kbuf0"), kb.res(P + "kbuf1")]
    vaugs = [S(f"vaug{i}", [128, 64, 128], BF16) for i in range(2)]
    vaug, r_vaug = vaugs[0]
    vstg, r_vstg = S("vstg", [128, 64, 64], BF16)
    kown = kb.sb(P + "kown", [128, 2, 2048], BF16)
    r_ko2 = [kb.res(P + "kown0"), kb.res(P + "kown1")]
    vowns = [S(f"vown{i}", [128, 16, 128], BF16) for i in range(2)]
    vown, r_vown = vowns[0]
    vostg, r_vostg = S("vostg", [128, 16, 64], BF16)
    qt = kb.sb(P + "qt", [128, 2, 2048], BF16)
    r_qt2 = [kb.res(P + "qt0"), kb.res(P + "qt1")]
    yT, r_yT = S("yT", [128, 8, 2048], BF16)
    mdiag, r_md = S("mdiag", [128, 4, 512], BF16)
    mB, r_mB = S("mB", [128, 2, 512], BF16)
    pmA, r_pmA = S("pmA", [128, 16, 4, 32], F32)
    hval, r_hval = S("hval", [128, 4], F32)
    lamv, r_lamv = S("lamv", [128, 4, 32], F32)
    lamt, r_lamt = S("lamt", [128, 8], F32)
    sgc, r_sgc = S("sgc", [128, 1], F32)
    sinkt, r_sinkt = S("sinkt", [1, 8], F32)
    sinke, r_sinke = S("sinke", [1, 8], F32)
    sh16, r_sh = S("sh16", [1, 8], BF16)
    sl16, r_sl = S("sl16", [1, 8], BF16)
    shf, r_shf = S("shf", [1, 8], F32)
    sinkrow, r_sinkrow = S("sinkrow", [1, 2, 8, 128], BF16)
    srow, r_srow = S("srow", [1, 128], BF16)
    ones64, r_ones64 = S("ones64", [64, 64], BF16)
    epst, r_eps = S("epst", [128, 1], F32)
    pt = [S(f"pt{i}", [128, 1024], BF16) for i in range(4)]
    nsb = [S(f"nsb{i}", [64, 512], F32) for i in range(2)]
    dsb = [S(f"dsb{i}", [64, 512], F32) for i in range(2)]
    rcp = [S(f"rcp{i}", [64, 512], F32) for i in range(2)]
    t1, r_t1 = S("t1", [64, 512], F32)
    t2, r_t2 = S("t2", [64, 512], F32)
    sq16, r_sq16 = S("sq16", [64, 512], BF16)
    rs, r_rs = S("rs", [64, 512], F32)
    kmean, r_kmean = S("kmean", [64, 32], F32)
    kmean16, r_km16 = S("kmean16", [64, 32], BF16)
    gm, r_gm = S("gm", [128, 16, 32], F32)
    t8, r_t8 = S("t8", [128, 16, 8], F32)
    sel, r_sel = S("sel", [128, 16, 32], F32)
    bst, r_bst = S("bst", [128, 16, 64], BF16)
    tmpb, r_tmpb = S("tmpb", [128, 16, 32], F32)
    ps_s = [(kb.ps(P + f"ps_s{i}", [128, 1024], F32), kb.res(P + f"ps_s{i}")) for i in range(2)]
    ps_o = [(kb.ps(P + f"ps_o{i}", [128, 512], F32), kb.res(P + f"ps_o{i}")) for i in range(2)]
    ps_x = [(kb.ps(P + f"ps_x{i}", [128, 512], F32), kb.res(P + f"ps_x{i}")) for i in range(1)]
    ps_b, r_ps_b = kb.ps(P + "ps_b", [128, 1024], BF16), kb.res(P + "ps_b")

    kb.ms("pool", epst[:], EPS, [r_eps])
    for i_ in range(2):
        kb.ms("pool", vaugs[i_][0][:, :, 64:128], 1.0, [vaugs[i_][1]])
        kb.ms("pool", vowns[i_][0][:, :, 64:128], 1.0, [vowns[i_][1]])
    kb.ms("pool", ones64[:], 1.0, [r_ones64])
    kb.ms("pool", srow[:, 0:64], 0.0, [r_srow])
    kb.ms("pool", srow[:, 64:128], 1.0, [r_srow])
    kb.dma("sp", mdiag[:], D["mdiag"][:, :, :], writes=[r_md])
    kb.dma("sp", mB[:], D["mB"][:, :, :], writes=[r_mB])
    kb.dma("sp", pmA[:], D["pmA"][:, :, :, :], writes=[r_pmA])
    kb.dma("sp", hval[:], D["hval"][:, :], writes=[r_hval])
    kb.dma("sp", lamv[:], D["lamv"][:, :, :], writes=[r_lamv])
    kb.dma("sp", sgc[:], D["sgc"][:, :], writes=[r_sgc])
    kb.dma("sp", sinkt[:], D["sinks"][0:1, :], writes=[r_sinkt])
    sinkf, r_sinkf = S("sinkf", [128, 8], F32)
    esink, r_esink = S("esink", [128, 8], F32)
    kb.dma("sp", sinkf[:], D["sinks"][:, :], writes=[r_sinkf])
    kb.act(esink[:], sinkf[:], AF.Exp, [r_sinkf], [r_esink])
    kb.tt("dve", lamv[:, 0, :], lamv[:, 0, :], lamv[:, 1, :], ALU.mult, [r_lamv], [r_lamv])
    kb.tt("dve", lamv[:, 2, :], lamv[:, 2, :], lamv[:, 3, :], ALU.mult, [r_lamv], [r_lamv])
    kb.red(lamt[:, 0:1], lamv[:, 0, :], [r_lamv], [r_lamt])
    kb.red(lamt[:, 1:2], lamv[:, 2, :], [r_lamv], [r_lamt])
    kb.act(lamt[:, 2:4], lamt[:, 0:2], AF.Exp, [r_lamt], [r_lamt])
    kb.tt("dve", lamt[:, 4:5], lamt[:, 3:4], lamt[:, 2:3], ALU.subtract, [r_lamt], [r_lamt])
    kb.ts("dve", lamt[:, 4:5], lamt[:, 4:5], -float(lam_init), None, ALU.add, None, [r_lamt], [r_lamt])
    kb.act(sinke[:], sinkt[:], AF.Exp, [r_sinkt], [r_sinke])
    kb.cp("dve", sh16[:], sinke[:], [r_sinke], [r_sh])
    kb.cp("dve", shf[:], sh16[:], [r_sh], [r_shf])
    kb.tt("dve", shf[:], sinke[:], shf[:], ALU.subtract, [r_sinke, r_shf], [r_shf])
    kb.cp("dve", sl16[:], shf[:], [r_shf], [r_sl])
    kb.cp("dve", sinkrow[:, 0, :, :], sh16[:].rearrange("p (h o) -> p h o", o=1).to_broadcast([1, 8, 128]), [r_sh], [r_sinkrow])
    kb.cp("dve", sinkrow[:, 1, :, :], sl16[:].rearrange("p (h o) -> p h o", o=1).to_broadcast([1, 8, 128]), [r_sl], [r_sinkrow])

    def _fin():
        kb.dma("sp", D["yT_out"].rearrange("(k p) t -> p k t", p=128), yT[:], reads=[r_yT])
        return [r_yT]

    if STOP <= 0:
        return []
    cnt = {"s": 0, "p": 0, "o": 0, "x": 0, "r": 0}

    def nxt(k, n):
        v = cnt[k] % n
        cnt[k] += 1
        return v

    class Pipe:
        def __init__(self):
            self.prev = None
            self.cbs = []
            self.group = []

        def _S(self, tiles):
            ps, r_ps = ps_s[nxt("s", 2)]
            for j, t in enumerate(tiles):
                reg = ps[:, j * 512:(j + 1) * 512]
                kl, ql, mask, rds = t["kl"], t["ql"], t["mask"], t["rds"]
                lastt = j == len(tiles) - 1
                if isinstance(ql, list):
                    kb.mm(reg, idb[:], mask[0], True, False, [r_id, mask[1]], [r_ps], signal=False)
                    for gi, qp in enumerate(ql):
                        fin = gi == len(ql) - 1
                        kb.mm(reg[:, gi * 128:(gi + 1) * 128], kl, qp, False, fin, rds, [r_ps], signal=fin and lastt)
                else:
                    kb.mm(reg, kl, ql, True, mask is None, rds, [r_ps], signal=(mask is None) and lastt)
                    if mask is not None:
                        kb.mm(reg, idb[:], mask[0], False, True, [r_id, mask[1]], [r_ps], signal=lastt)
            return ps, r_ps

        def _drain(self):
            if self.prev is not None:
                tiles, (ps, r_ps) = self.prev
                n = len(tiles)
                p, r_p = pt[nxt("p", 4)]
                kb.act(p[:, 0:n * 512], ps[:, 0:n * 512], AF.Exp, [r_ps], [r_p], scale=tiles[0]["scale"])
                for j, t in enumerate(tiles):
                    kb.mm(t["acc"][0][:], t["vl"], p[:, j * 512:(j + 1) * 512], t["first"], t["last"], [r_p] + t["rds"], [t["acc"][1]])
                self.prev = None
            keep = []
            for item in self.cbs:
                if item[0] <= 0:
                    item[1]()
                else:
                    item[0] -= 1
                    keep.append(item)
            self.cbs = keep

        def _emit(self):
            tiles, self.group = self.group, []
            ps = self._S(tiles)
            self._drain()
            self.prev = (tiles, ps)

        def push(self, kl, ql, vl, acc, first, last, scale, rds, mask=None):
            self.group.append(dict(kl=kl, ql=ql, vl=vl, acc=acc, first=first, last=last, scale=scale, rds=rds, mask=mask))
            if len(self.group) == 2:
                self._emit()

        def after(self, cb, delay=0):
            assert not self.group
            self.cbs.append([delay, cb])

        def sync(self):
            if self.group:
                self._emit()
            self._drain()

        def flush(self):
            if self.group:
                self._emit()
            self._drain()
            while self.cbs:
                item = self.cbs.pop(0)
                item[1]()

    def release(acc, k):
        n_, rn_ = nsb[k]
        d_, rd_ = dsb[k]
        kb.cp("dve", n_[:], acc[0][0:64, :], [acc[1]], [rn_])
        kb.cp("dve", d_[:], acc[0][64:128, :], [acc[1]], [rd_])
        return n_, rn_, d_, rd_

    pipe = Pipe()

    def attn_tile(kl, ql, vl, acc, first, last, scale, rds, mask=None):
        pipe.push(kl, ql, vl, acc, first, last, scale, rds, mask)

    for par in range(2):
        kb.ms("pool", kbuf[64:128, par, :], 0.0, [r_kb2[par]])
        kb.ms("pool", kown[64:96, par, :], 0.0, [r_ko2[par]])

    def a_loads(h, par):
        kb.dma("sp", kbuf[0:64, par, :], kTf[h * 64:(h + 1) * 64, :], writes=[r_kb2[par]])
        kb.dma("sp", kbuf[64:96, par, :], D["ohA"][:, :], writes=[r_kb2[par]])
        kb.dma("sp", qt[0:64, par, :], qkT[h * 64:(h + 1) * 64, :], writes=[r_qt2[par]])
        kb.dma("sp", vstg[:], vhp[h], writes=[r_vstg])
        kb.cp("pool", vaugs[par][0][:, :, 0:64], vstg[:], [r_vstg], [vaugs[par][1]])
        kb.dma("sp", kown[0:64, par, :], qkT[256 + h * 64:256 + (h + 1) * 64, :], writes=[r_ko2[par]])
        kb.dma("sp", kown[96:128, par, :], D["ohA_own"][:, :], writes=[r_ko2[par]])
        kb.dma("sp", vostg[:], vown_d[:, h * 64:(h + 1) * 64].rearrange("(c p) d -> p c d", p=128), writes=[r_vostg])
        kb.cp("pool", vowns[par][0][:, :, 0:64], vostg[:], [r_vostg], [vowns[par][1]])

    def a_bias1(h, par):
        r_kbuf, r_qt = r_kb2[par], r_qt2[par]
        kb.red(kmean[:], kbuf[0:64, par, :].rearrange("p (n l) -> p n l", l=256), [r_kbuf], [r_kmean])
        kb.ts("dve", kmean16[:], kmean[:], 1.0 / 256, None, ALU.mult, None, [r_kmean], [r_km16])

    def a_bias2(h, par):
        r_kbuf, r_qt = r_kb2[par], r_qt2[par]
        pg, r_pg = ps_x[0]
        for c in range(16):
            kb.mm(pg[:, c * 32:(c + 1) * 32], qt[0:64, par, c * 128:(c + 1) * 128], kmean16[:], True, True,
                  [r_qt, r_km16], [r_pg], signal=(c == 15))
        kb.tt("dve", gm[:], pg[:].rearrange("p (c n) -> p c n", n=32), pmA[:, :, 0, :], ALU.add, [r_pg, r_pmA], [r_gm])
        for c in range(16):
            kb.op("dve", (lambda c: lambda en: en.max(out=t8[:, c, :], in_=gm[:, c, :]))(c), [r_gm], [r_t8])
        kb.tt("dve", sel[:], gm[:], t8[:, :, 2:3].to_broadcast([128, 16, 32]), ALU.is_ge, [r_gm, r_t8], [r_sel])
        for which in range(2):
            kb.tt("dve", tmpb[:], sel[:], pmA[:, :, 1 + which, :], ALU.mult, [r_sel, r_pmA], [r_tmpb])
            if which == 1:
                kb.tt("dve", tmpb[:], tmpb[:], pmA[:, :, 3, :], ALU.add, [r_tmpb, r_pmA], [r_tmpb])
            kb.ts("dve", bst[:, :, which * 32:(which + 1) * 32], tmpb[:], BIG, -BIG, ALU.mult, ALU.add, [r_tmpb], [r_bst])

    def a_bias3(h, par):
        r_kbuf, r_qt = r_kb2[par], r_qt2[par]
        for hf in range(2):
            for c8 in range(8):
                c = hf * 8 + c8
                kb.tr(ps_b[0:64, c8 * 128:(c8 + 1) * 128], bst[:, c, :], idb[:], [r_bst, r_id], [r_ps_b], signal=(c8 == 7))
            kb.cp("dve", qt[64:128, par, hf * 1024:(hf + 1) * 1024], ps_b[0:64, :], [r_ps_b], [r_qt])

    a_loads(0, 0)
    a_bias1(0, 0)
    a_bias2(0, 0)
    a_bias3(0, 0)
    for h in range(4):
        par = h % 2
        r_kbuf, r_qt, r_kown = r_kb2[par], r_qt2[par], r_ko2[par]
        vaug_h, r_vaug_h = vaugs[par]
        vown_h, r_vown_h = vowns[par]
        if h + 1 < 4:
            pipe.sync()
            a_loads(h + 1, 1 - par)
            pipe.after((lambda hh, pp: lambda: a_bias1(hh, pp))(h + 1, 1 - par), delay=8)
            pipe.after((lambda hh, pp: lambda: a_bias2(hh, pp))(h + 1, 1 - par), delay=24)
            pipe.after((lambda hh, pp: lambda: a_bias3(hh, pp))(h + 1, 1 - par), delay=44)
        for i in range(4):
            acc = ps_o[nxt("o", 2)]
            qs = qt[:, par, i * 512:(i + 1) * 512]
            for kt in range(NSTREAM[i]):
                attn_tile(kbuf[:, par, kt * 128:(kt + 1) * 128], qs, vaug_h[:, kt, :], acc, kt == 0, False, 0.125,
                          [r_kbuf, r_qt, r_vaug_h])
            for t in range(4):
                c = 4 * i + t
                attn_tile(kown[:, par, c * 128:(c + 1) * 128], qs, vown_h[:, c, :], acc, False, t == 3, 0.125,
                          [r_kown, r_qt, r_vown_h], mask=(mdiag[:, t, :], r_md))

            def fin_a(acc=acc, h=h, i=i):
                k_ = nxt("r", 2)
                n_, rn_, d_, rd_ = release(acc, k_)
                rc, r_rc = rcp[k_]
                kb.op("dve", lambda en: en.reciprocal(out=rc[:], in_=d_[:]), [rd_], [r_rc])
                kb.tt("dve", yT[(h % 2) * 64:(h % 2) * 64 + 64, h // 2, i * 512:(i + 1) * 512], n_[:], rc[:], ALU.mult,
                      [rn_, r_rc], [r_yT])
            pipe.after(fin_a)
    pipe.flush()

    if STOP <= 2:
        return _fin()
    sc_c = float(32 ** -0.5)
    for m in range(2):
        kb.dma("sp", kbuf[32:48, m, :], D["ohC"][:, :], writes=[r_kb2[m]])
        kb.dma("sp", qt[32:48, m, :], D["cbC"][:, :], writes=[r_qt2[m]])

    def c_vload(h):
        par = h % 2
        kb.dma("sp", vstg[:], vhp[6 + h], writes=[r_vstg])
        kb.cp("pool", vaugs[par][0][:, :, 0:64], vstg[:], [r_vstg], [vaugs[par][1]])
        kb.dma("sp", vostg[:], vown_d[:, 384 + h * 64:384 + (h + 1) * 64].rearrange("(c p) d -> p c d", p=128), writes=[r_vostg])
        kb.cp("pool", vowns[par][0][:, :, 0:64], vostg[:], [r_vostg], [vowns[par][1]])

    c_vload(0)
    for h in range(4):
        par = h % 2
        vaug_h, r_vaug_h = vaugs[par]
        vown_h, r_vown_h = vowns[par]
        for m in range(2):
            r0 = 384 + h * 64 + m * 32
            kb.dma("sp", kbuf[0:32, m, :], kTf[r0:r0 + 32, :], writes=[r_kb2[m]])
            q0 = 1152 + h * 64 + m * 32
            kb.dma("sp", qt[0:32, m, :], qkT[q0:q0 + 32, :], writes=[r_qt2[m]])
            k0 = 1408 + h * 64 + m * 32
            kb.dma("sp", kown[0:32, m, :], qkT[k0:k0 + 32, :], writes=[r_ko2[m]])
        if h + 1 < 4:
            c_vload(h + 1)
        for i in range(4):
            accs = [ps_o[0], ps_o[1]]
            for kt in range(NSTREAM[i]):
                for m in range(2):
                    attn_tile(kbuf[0:48, m, kt * 128:(kt + 1) * 128], qt[0:48, m, i * 512:(i + 1) * 512], vaug_h[:, kt, :],
                              accs[m], kt == 0, False, sc_c, [r_kb2[m], r_qt2[m], r_vaug_h])
            for t in range(4):
                c = 4 * i + t
                for m in range(2):
                    attn_tile(kown[0:32, m, c * 128:(c + 1) * 128], qt[0:32, m, i * 512:(i + 1) * 512], vown_h[:, c, :],
                              accs[m], False, t == 3, sc_c, [r_ko2[m], r_qt2[m], r_vown_h], mask=(mdiag[:, t, :], r_md))

            def fin_c1(accs=accs):
                rel = [release(accs[m], m) for m in range(2)]
                for m in range(2):
                    kb.op("dve", (lambda m: lambda en: en.reciprocal(out=rcp[m][0][:], in_=rel[m][2][:]))(m), [rel[m][3]], [rcp[m][1]])
                kb.tt("dve", t1[:], rel[0][0][:], rcp[0][0][:], ALU.mult, [rel[0][1], rcp[0][1]], [r_t1])
                kb.tt("dve", t2[:], rel[1][0][:], rcp[1][0][:], ALU.mult, [rel[1][1], rcp[1][1]], [r_t2])
                kb.stt(t1[:], t2[:], lamt[0:64, 4:5], t1[:], ALU.mult, ALU.add, [r_t1, r_t2, r_lamt], [r_t1])
                kb.act(sq16[:], t1[:], AF.Square, [r_t1], [r_sq16])

            def fin_c2(h=h, i=i):
                px, r_px = ps_x[0]
                kb.mm(px[0:64, :], ones64[:], sq16[:], True, True, [r_ones64, r_sq16], [r_px])
                kb.act(rs[:], px[0:64, :], AF.Ln, [r_px, r_eps], [r_rs], bias=epst[0:64, :], scale=1.0 / 64)
                kb.act(rs[:], rs[:], AF.Exp, [r_rs], [r_rs], scale=-0.5)
                kb.tt("dve", t1[:], t1[:], rs[:], ALU.mult, [r_t1, r_rs], [r_t1])
                kb.ts("dve", yT[(h % 2) * 64:(h % 2) * 64 + 64, 6 + h // 2, i * 512:(i + 1) * 512], t1[:], sgc[0:64, :], float(1.0 - lam_init),
                      ALU.mult, ALU.mult, [r_t1, r_sgc], [r_yT])
            pipe.after(fin_c1)
            pipe.after(fin_c2, delay=12)
        pipe.flush()

    if STOP <= 3:
        return _fin()
    qb = qt
    for k in range(2):
        qv = kbuf[0:64, 0, :].rearrange("p (g t) -> p g t", g=4)
        kb.dma("sp", qv, qkT[512 + k * 256:512 + (k + 1) * 256, :].rearrange("(g d) t -> d g t", d=64), writes=[r_kb2[0]])
        kb.dma("sp", kown[0:64, 0, :], qkT[1024 + k * 64:1024 + (k + 1) * 64, :], writes=[r_ko2[0], r_ko2[1]])
        kb.dma("sp", kown[0:64, 1, 0:512], kTbh[k * 64:(k + 1) * 64, :], writes=[r_ko2[0], r_ko2[1]])
        kb.dma("sp", vostg[:], vown_d[:, 256 + k * 64:256 + (k + 1) * 64].rearrange("(c p) d -> p c d", p=128), writes=[r_vostg])
        kb.cp("pool", vown[:, :, 0:64], vostg[:], [r_vostg], [r_vown])
        kb.dma("sp", vstg[:, 0:4, :], vbh_d[:, k * 64:(k + 1) * 64].rearrange("(s p) d -> p s d", p=128), writes=[r_vstg])
        kb.cp("pool", vaug[:, 0:4, 0:64], vstg[:, 0:4, :], [r_vstg], [r_vaug])
        if k == 0:
            kb.ms("pool", vaug[:, 4:8, 64:128], 1.0, [r_vaug])
        for s in range(4):
            kb.ts("dve", vaug[:, s, 0:64], vaug[:, s, 0:64], hval[:, s:s + 1], None, ALU.mult, None, [r_vaug, r_hval], [r_vaug])
            kb.ts("dve", vaug[:, s, 64:128], vaug[:, 4 + s, 64:128], hval[:, s:s + 1], None, ALU.mult, None, [r_vaug, r_hval], [r_vaug])
        for c in range(16):
            s = c // 4
            acc = ps_o[nxt("o", 2)]
            if c % 4 == 0:
                kprev, vprev = kown[0:64, 1, s * 128:(s + 1) * 128], vaug[:, s, :]
            else:
                kprev, vprev = kown[0:64, 0, (c - 1) * 128:c * 128], vown[:, c - 1, :]
            rds = [r_ko2[0], r_ko2[1], r_kb2[0], r_vown, r_vaug]
            qparts = [qv[:, gi, c * 128:(c + 1) * 128] for gi in range(4)]
            attn_tile(kprev, qparts, vprev, acc, True, False, 0.125, rds, mask=(mB[:, 0, :], r_mB))
            attn_tile(kown[0:64, 0, c * 128:(c + 1) * 128], qparts, vown[:, c, :], acc, False, True, 0.125, rds, mask=(mB[:, 1, :], r_mB))

            def fin_b(acc=acc, c=c, k=k):
                k_ = nxt("r", 2)
                n_, rn_, d_, rd_ = release(acc, k_)
                rc, r_rc = rcp[k_]
                kb.cp("dve", t2[:], d_[:], [rd_], [r_t2])
                for gi in range(4):
                    hh = 4 * k + gi
                    kb.ts("dve", t2[:, gi * 128:(gi + 1) * 128], t2[:, gi * 128:(gi + 1) * 128], esink[0:64, hh:hh + 1], None, ALU.add, None,
                          [r_t2, r_esink], [r_t2])
                kb.op("dve", lambda en: en.reciprocal(out=rc[:], in_=t2[:]), [r_t2], [r_rc])
                for gi in range(4):
                    hh = 4 * k + gi
                    kb.tt("dve", yT[(hh % 2) * 64:(hh % 2) * 64 + 64, 2 + hh // 2, c * 128:(c + 1) * 128],
                          n_[:, gi * 128:(gi + 1) * 128], rc[:, gi * 128:(gi + 1) * 128], ALU.mult, [rn_, r_rc], [r_yT])
            pipe.after(fin_b)
        pipe.flush()
    return _fin()


def emit_M(kb, D, pfx="m"):
    P = pfx
    gT, x, xmid = D["gT"], D["x"], D["xmid"]

    def S(name, shape, dt):
        return kb.sb(P + name, shape, dt), kb.res(P + name)

    cnt = {"x": 0}

    def nxt(k, n):
        v = cnt[k] % n
        cnt[k] += 1
        return v

    ps_x = [(kb.ps(P + f"ps_x{i}", [128, 512], F32), kb.res(P + f"ps_x{i}")) for i in range(4)]
    yT, r_yT = S("yT", [128, 8, 2048], BF16)
    kb.dma("sp", yT[:], D["yT_in"].rearrange("(k p) t -> p k t", p=128), writes=[r_yT])
    wp, r_wp = S("wp", [128, 8, 1024], BF16)
    wo, r_wo = S("wo", [128, 8, 1024], BF16)
    for nm, k0, nk in (("w_pa", 0, 2), ("w_pb", 2, 4), ("w_pc", 6, 2)):
        wv = D[nm].rearrange("(k p) n -> p k n", p=128)
        for k in range(nk):
            kb.dma("pool", wp[:, k0 + k, :], wv[:, k, :], writes=[r_wp])
    wov = D["w_out"].rearrange("(k p) n -> p k n", p=128)
    for k in range(8):
        kb.dma("pool", wo[:, k, :], wov[:, k, :], writes=[r_wo])
    gt = [S(f"gt{i}", [128, 3, 512], BF16) for i in range(2)]
    macc, r_macc = S("macc", [128, 512], F32)
    mtmp, r_mtmp = S("mtmp", [128, 512], F32)
    mT, r_mT = S("mT", [128, 8, 512], BF16)
    xt = [S(f"xt{i}", [128, 1024], F32) for i in range(2)]
    ob = [S(f"ob{i}", [128, 1024], F32) for i in range(2)]
    gTv = gT.rearrange("(br f) t -> f br t", br=3)
    BR = ((0, 2), (2, 4), (6, 2))
    gi_ = 0
    for g in range(4):
        for fc in range(8):
            gtt, r_gtt = gt[gi_ % 2]; gi_ += 1
            kb.dma("sp", gtt[:], gTv[fc * 128:(fc + 1) * 128, :, g * 512:(g + 1) * 512], writes=[r_gtt])
            for br, (k0, nk) in enumerate(BR):
                px, r_px = ps_x[nxt("x", 4)]
                for k in range(nk):
                    kb.mm(px[:], wp[:, k0 + k, fc * 128:(fc + 1) * 128], yT[:, k0 + k, g * 512:(g + 1) * 512], k == 0, k == nk - 1,
                          [r_wp, r_yT], [r_px])
                if br == 0:
                    kb.tt("dve", macc[:], px[:], gtt[:, 0, :], ALU.mult, [r_px, r_gtt], [r_macc])
                else:
                    kb.tt("dve", mtmp[:], px[:], gtt[:, br, :], ALU.mult, [r_px, r_gtt], [r_mtmp])
                    if br == 1:
                        kb.tt("dve", macc[:], macc[:], mtmp[:], ALU.add, [r_macc, r_mtmp], [r_macc])
                    else:
                        kb.tt("dve", mT[:, fc, :], macc[:], mtmp[:], ALU.add, [r_macc, r_mtmp], [r_mT])
        for tt_ in range(4):
            t = g * 4 + tt_
            b = t % 2
            kb.dma("sp", xt[b][0][:], x[t * 128:(t + 1) * 128, :], writes=[xt[b][1]])
            for hc in range(2):
                px, r_px = ps_x[nxt("x", 4)]
                for fc in range(8):
                    kb.mm(px[:], mT[:, fc, tt_ * 128:(tt_ + 1) * 128], wo[:, fc, hc * 512:(hc + 1) * 512], fc == 0, fc == 7,
                          [r_mT, r_wo], [r_px])
                kb.tt("dve", ob[b][0][:, hc * 512:(hc + 1) * 512], px[:], xt[b][0][:, hc * 512:(hc + 1) * 512], ALU.add,
                      [r_px, xt[b][1]], [ob[b][1]])
            kb.dma("sp", xmid[t * 128:(t + 1) * 128, :], ob[b][0][:], reads=[ob[b][1]])
    return [ob[0][1], ob[1][1]]


GROUPS = [[j, 7 - j, 8 + j, 15 - j] for j in range(4)]
S = 8192
BF = ml_dtypes.bfloat16


def core_pos(j):
    return np.concatenate([np.arange(g * 512, (g + 1) * 512) for g in GROUPS[j]])


def rope_tabs(pos, dim):
    inv = (1.0 / (np.float32(10000.0) ** (np.arange(0, dim, 2, dtype=np.float32) / np.float32(dim)))).astype(np.float32)
    ang = pos.astype(np.float32)[:, None] * inv[None, :]
    c = np.cos(ang).astype(np.float32)
    s = np.sin(ang).astype(np.float32)
    ct = np.concatenate([c, c], axis=1)
    st = np.concatenate([-s, s], axis=1)
    return np.ascontiguousarray(ct), np.ascontiguousarray(st)


def rep128(v):
    return np.ascontiguousarray(np.broadcast_to(np.asarray(v, np.float32)[None, :], (128, v.shape[-1])))


def p_inputs(x_core, l, j, inp):
    pos = core_pos(j)
    ct64, st64 = rope_tabs(pos, 64)
    ct32, st32 = rope_tabs(pos, 32)
    gall = np.ones((2048,), np.float32)
    gall[0:256] = np.tile(inp["qn_a"][l], 4)
    gall[256:512] = np.tile(inp["kn_a"][l], 4)
    gall[768:1280] = np.tile(inp["qn_b"][l], 8)
    gall[1280:1408] = np.tile(inp["kn_b"][l], 2)
    gall[1536:1792] = np.tile(inp["qn_c"][l], 8)
    gall[1792:2048] = np.tile(inp["kn_c"][l], 8)
    return {"x": np.ascontiguousarray(x_core), "anorm": rep128(inp["attn_norm"][l]),
            "w_in": np.ascontiguousarray(inp["w_in"][l]), "gall": rep128(gall),
            "ct64": ct64, "st64": st64, "ct32": ct32, "st32": st32}


def f_inputs(xm_core, xm_full_b, l, j, inp):
    halo = np.zeros((8, 1024), np.float32)
    for s, g in enumerate(GROUPS[j]):
        if g > 0:
            halo[2 * s:2 * s + 2] = xm_full_b[g * 512 - 2:g * 512]
    cw = inp["conv_w"][l]
    cb = inp["conv_b"][l]
    convp = np.zeros((128, 44, 4), np.float32)
    convp[:, :, 0:3] = cw.T.reshape(44, 128, 3).transpose(1, 0, 2)
    convp[:, :, 3] = cb.reshape(44, 128).T
    return {"xm": np.ascontiguousarray(xm_core), "xhalo": halo, "mnorm": rep128(inp["mlp_norm"][l]),
            "w_up": np.ascontiguousarray(inp["w_up"][l]), "convp": convp,
            "w_down": np.ascontiguousarray(inp["w_down"][l])}


NEGV = -30000.0


def a_consts(j):
    gl = GROUPS[j]
    f32 = np.float32
    mdiag = np.zeros((128, 4, 512), f32)
    p = np.arange(128)[:, None]
    f = np.arange(512)[None, :]
    for t in range(4):
        mdiag[:, t, :] = np.where(t * 128 + p > f, NEGV, 0.0)
    mB = np.zeros((128, 2, 512), f32)
    fi = (np.arange(512) % 128)[None, :]
    mB[:, 0, :] = np.where(p <= fi, NEGV, 0.0)
    mB[:, 1, :] = np.where(p > fi, NEGV, 0.0)
    pmA = np.zeros((128, 16, 4, 32), f32)
    n = np.arange(32)
    for c in range(16):
        g = gl[c // 4]
        own = 2 * g + (c % 4) // 2
        pmA[:, c, 0, :] = np.where(n < own, 0.0, -1e30)[None, :]
        pmA[:, c, 1, :] = (n < 2 * g).astype(f32)[None, :]
        pmA[:, c, 2, :] = ((n >= 2 * g) & (n < own)).astype(f32)[None, :]
        pmA[:, c, 3, :] = (n == own).astype(f32)[None, :]
    keys = np.arange(S)
    ohA = (keys[None, :] // 256 == np.arange(32)[:, None]).astype(f32)
    ohC = (keys[None, :] // 512 == np.arange(16)[:, None]).astype(f32)
    pos = core_pos(j)
    ohA_own = (pos[None, :] // 256 == np.arange(32)[:, None]).astype(f32)
    cbC = np.where(np.arange(16)[:, None] < (pos[None, :] // 512), 0.0, NEGV).astype(f32)
    hval = np.zeros((128, 4), f32)
    for s, g in enumerate(gl):
        hval[:, s] = 1.0 if g > 0 else 0.0
    return {"mdiag": mdiag.astype(BF), "mB": mB.astype(BF), "pmA": pmA, "ohA": ohA.astype(BF), "ohC": ohC.astype(BF),
            "ohA_own": ohA_own.astype(BF), "cbC": cbC.astype(BF), "hval": hval}


def gather_kv(qkT_list, v_list):
    kT = np.zeros((640, S), BF)
    vf = np.zeros((S, 640), BF)
    for j in range(4):
        pos = core_pos(j)
        q = qkT_list[j]
        kT[0:256, pos] = q[256:512]
        kT[256:384, pos] = q[1024:1152]
        kT[384:640, pos] = q[1408:1664]
        vf[pos] = v_list[j]
    return kT, vf


def a_inputs(j, l, qkT_own, v_own, kT_full, v_full, gT_own, x_core, inp):
    d = dict(a_consts(j))
    d["qkT"] = np.ascontiguousarray(qkT_own)
    d["v_own"] = np.ascontiguousarray(v_own)
    d["kT_full"] = np.ascontiguousarray(kT_full)
    d["v_hp"] = np.ascontiguousarray(v_full.reshape(64, 128, 10, 64).transpose(2, 1, 0, 3))
    kh = np.zeros((128, 512), BF)
    vh = np.zeros((512, 128), BF)
    for s, g in enumerate(GROUPS[j]):
        if g > 0:
            kh[:, s * 128:(s + 1) * 128] = kT_full[256:384, g * 512 - 128:g * 512]
            vh[s * 128:(s + 1) * 128, :] = v_full[g * 512 - 128:g * 512, 256:384]
    d["kTb_halo"] = kh
    d["vb_halo"] = vh
    d["gT"] = np.ascontiguousarray(gT_own)
    d["x"] = np.ascontiguousarray(x_core)
    lamv = np.stack([inp["lam_q1"][l], inp["lam_k1"][l], inp["lam_q2"][l], inp["lam_k2"][l]]).astype(np.float32)
    d["lamv"] = np.ascontiguousarray(np.broadcast_to(lamv[None], (128, 4, 32)))
    d["sgc"] = np.ascontiguousarray(np.tile(inp["subln"][l], 2).reshape(128, 1).astype(np.float32))
    d["sinks"] = rep128(inp["sinks"][l])
    for nm in ("w_pa", "w_pb", "w_pc", "w_out"):
        d[nm] = np.ascontiguousarray(inp[nm][l])
    return d


A_IN_SPECS = [("qkT", [1664, 2048], "bf"), ("v_own", [2048, 640], "bf"), ("kT_full", [640, 8192], "bf"),
              ("v_hp", [10, 128, 64, 64], "bf"), ("kTb_halo", [128, 512], "bf"), ("vb_halo", [512, 128], "bf"),
              ("gT", [3072, 2048], "bf"), ("x", [2048, 1024], "f"), ("mdiag", [128, 4, 512], "bf"), ("mB", [128, 2, 512], "bf"),
              ("pmA", [128, 16, 4, 32], "f"), ("ohA", [32, 8192], "bf"), ("ohC", [16, 8192], "bf"), ("ohA_own", [32, 2048], "bf"),
              ("cbC", [16, 2048], "bf"), ("hval", [128, 4], "f"), ("lamv", [128, 4, 32], "f"), ("sgc", [128, 1], "f"),
              ("sinks", [128, 8], "f"), ("w_pa", [256, 1024], "f"), ("w_pb", [512, 1024], "f"), ("w_pc", [256, 1024], "f"),
              ("w_out", [1024, 1024], "f")]


def _make_ident(kb):
    idb = kb.sb("idb", [128, 128], BF16)
    r_id = kb.res("idb")
    idf = kb.sb("idf", [128, 128], F32)
    kb.op("pool", lambda e: e.memset(idf[:], 0.0), writes=[r_id])
    kb.op("pool", lambda e: e.affine_select(out=idf[:], in_=idf[:], pattern=[[-1, 128]], compare_op=ALU.not_equal,
                                            fill=1.0, base=0, channel_multiplier=1), reads=[r_id], writes=[r_id])
    kb.op("pool", lambda e: e.tensor_copy(out=idb[:], in_=idf[:]), reads=[r_id], writes=[r_id])
    return idb, r_id


def _build(kind, lam_init=0.0):
    nc = bass.Bass("TRN2", target_bir_lowering=False)

    def di(n, s, dt=F32):
        return nc.dram_tensor(n, s, dt, kind="ExternalInput").ap()

    def do(n, s, dt):
        return nc.dram_tensor(n, s, dt, kind="ExternalOutput").ap()

    with ExitStack() as st:
        kb = KB(nc, st)
        if kind == "P":
            D = {"x": di("x", [2048, 1024]), "anorm": di("anorm", [128, 1024]), "w_in": di("w_in", [1024, 5376]),
                 "gall": di("gall", [128, 2048]), "ct64": di("ct64", [2048, 64]), "st64": di("st64", [2048, 64]),
                 "ct32": di("ct32", [2048, 32]), "st32": di("st32", [2048, 32]),
                 "qkT": do("qkT", [1664, 2048], BF16), "v": do("v", [2048, 640], BF16), "gT": do("gT", [3072, 2048], BF16)}
            fin = emit_P(kb, D, _make_ident(kb))
        elif kind == "A":
            D = {}
            for n, s, dt in A_IN_SPECS:
                if n in ("gT", "x", "w_pa", "w_pb", "w_pc", "w_out"):
                    continue
                D[n] = di(n, s, BF16 if dt == "bf" else F32)
            D["yT_out"] = do("yT_out", [1024, 2048], BF16)
            fin = emit_A(kb, D, _make_ident(kb), lam_init)
        elif kind == "M":
            D = {"yT_in": di("yT_in", [1024, 2048], BF16), "gT": di("gT", [3072, 2048], BF16), "x": di("x", [2048, 1024]),
                 "w_pa": di("w_pa", [256, 1024]), "w_pb": di("w_pb", [512, 1024]), "w_pc": di("w_pc", [256, 1024]),
                 "w_out": di("w_out", [1024, 1024]), "xmid": do("xmid", [2048, 1024], F32)}
            fin = emit_M(kb, D)
        else:
            D = {"xm": di("xm", [2048, 1024]), "xhalo": di("xhalo", [8, 1024]), "mnorm": di("mnorm", [128, 1024]),
                 "w_up": di("w_up", [1024, 5632]), "convp": di("convp", [128, 44, 4]), "w_down": di("w_down", [2816, 1024]),
                 "xo": do("xo", [2048, 1024], F32)}
            fin = emit_F(kb, D, _make_ident(kb))
        kb.finish(fin)
    return nc


def _run(nc, in_maps):
    res = run_bass_kernel_spmd(nc, in_maps, core_ids=list(range(8)))
    return res.results


def kernel(**inp):
    inp = {k: np.asarray(v) for k, v in inp.items()}
    x = inp["x"].astype(np.float32)
    cores = [(c // 4, c % 4) for c in range(8)]
    xs = [np.ascontiguousarray(x[b][core_pos(j)]) for b, j in cores]
    for l in range(2):
        lam_init = 0.8 - 0.6 * float(np.exp(-0.3 * l))
        rp = _run(_build("P"), [p_inputs(xs[c], l, cores[c][1], inp) for c in range(8)])
        full = {}
        for b in range(2):
            full[b] = gather_kv([rp[4 * b + j]["qkT"] for j in range(4)], [rp[4 * b + j]["v"] for j in range(4)])
        a_maps = []
        for c, (b, j) in enumerate(cores):
            d = a_inputs(j, l, rp[c]["qkT"], rp[c]["v"], full[b][0], full[b][1], rp[c]["gT"], xs[c], inp)
            for k in ("gT", "x", "w_pa", "w_pb", "w_pc", "w_out"):
                d.pop(k)
            a_maps.append(d)
        ra = _run(_build("A", lam_init), a_maps)
        m_maps = []
        for c in range(8):
            d = {"yT_in": ra[c]["yT_out"], "gT": rp[c]["gT"], "x": xs[c]}
            for nm in ("w_pa", "w_pb", "w_pc", "w_out"):
                d[nm] = np.ascontiguousarray(inp[nm][l])
            m_maps.append(d)
        rm = _run(_build("M"), m_maps)
        xm_full = np.zeros((2, S, 1024), np.float32)
        for c, (b, j) in enumerate(cores):
            xm_full[b][core_pos(j)] = rm[c]["xmid"]
        rf = _run(_build("F"), [f_inputs(rm[c]["xmid"], xm_full[cores[c][0]], l, cores[c][1], inp) for c in range(8)])
        xs = [np.ascontiguousarray(rf[c]["xo"]) for c in range(8)]
    out = np.zeros((2, S, 1024), np.float32)
    for c, (b, j) in enumerate(cores):
        out[b][core_pos(j)] = xs[c]
    return out
```

```python
import numpy as np
import ml_dtypes
from contextlib import ExitStack
import concourse.bass as bass
import concourse.mybir as mybir
from concourse.bass_utils import run_bass_kernel_spmd


F32 = mybir.dt.float32
BF16 = mybir.dt.bfloat16
AF = mybir.ActivationFunctionType
ALU = mybir.AluOpType
AX = mybir.AxisListType


class Ev:
    __slots__ = ("sem", "val")

    def __init__(self, sem, val):
        self.sem = sem
        self.val = val


class Res:
    def __init__(self, name):
        self.name = name
        self.w = None
        self.r = []
        self.dsem = None
        self.dcnt = 0


class KB:
    ENGS = ("pe", "dve", "act", "pool", "sp")

    def __init__(self, nc, stack):
        self.nc = nc
        self.stack = stack
        self.eng = {"pe": nc.tensor, "dve": nc.vector, "act": nc.scalar,
                    "pool": nc.gpsimd, "sp": nc.sync}
        self.sem = {e: stack.enter_context(nc.semaphore("s_" + e)) for e in self.ENGS}
        self.cnt = {e: 0 for e in self.ENGS}
        self.seen = {e: {} for e in self.ENGS}
        self.stream = {e: [] for e in self.ENGS}
        self.nsem = len(self.ENGS)
        self.allres = []

    def res(self, name):
        r = Res(name)
        self.allres.append(r)
        return r

    def sb(self, name, shape, dt):
        t = self.stack.enter_context(self.nc.sbuf_tensor(name, list(shape), dt))
        return t

    def ps(self, name, shape, dt):
        return self.stack.enter_context(self.nc.psum_tensor(name, list(shape), dt))

    def _waits(self, e, reads, writes, nosame):
        need = {}

        def add(ev, r):
            if ev is None:
                return
            if r in nosame and ev.sem is self.sem[e]:
                return
            k = id(ev.sem)
            if k not in need or need[k][1] < ev.val:
                need[k] = (ev.sem, ev.val)

        for r in reads:
            add(r.w, r)
        for w in writes:
            add(w.w, w)
            for ev in w.r:
                add(ev, w)
        out = []
        seen = self.seen[e]
        for k, (s, v) in need.items():
            if seen.get(k, 0) >= v:
                continue
            seen[k] = v
            out.append((s, v))
        return out

    def op(self, e, fn, reads=(), writes=(), signal=True, nosame=()):
        waits = self._waits(e, reads, writes, nosame)
        if signal:
            self.cnt[e] += 1
            ev = Ev(self.sem[e], self.cnt[e])
            sig = (self.sem[e], 1)
        else:
            ev = Ev(self.sem[e], self.cnt[e] + 1)
            sig = None
        self.stream[e].append((waits, fn, sig))
        for r in reads:
            r.r.append(ev)
        for w in writes:
            w.w = ev
            w.r = []
        return ev

    def dma(self, q, out_ap, in_ap, reads=(), writes=(), **kw):
        tr = (list(writes) + list(reads))[0]
        if tr.dsem is None:
            tr.dsem = self.stack.enter_context(self.nc.semaphore("d_" + tr.name))
            self.nsem += 1
        waits = [w for w in self._waits(q, reads, writes, ()) if w[0] is not tr.dsem]
        tr.dcnt += 16
        ev = Ev(tr.dsem, tr.dcnt)
        self.stream[q].append(
            (waits, lambda en: en.dma_start(out=out_ap, in_=in_ap, **kw), (tr.dsem, 16)))
        for r in reads:
            r.r.append(ev)
        for w in writes:
            w.w = ev
            w.r = []
        return ev

    def finish(self, final_res):
        waits = self._waits("sp", final_res, final_res, ())
        self.stream["sp"].append((waits, None, None))
        nc = self.nc
        with nc.Block() as block:
            def mk(e):
                def body(en):
                    for waits, fn, sig in self.stream[e]:
                        for s, v in waits:
                            en.wait_ge(s, v)
                        if fn is None:
                            continue
                        ins = fn(en)
                        if sig is not None:
                            ins.then_inc(sig[0], sig[1])
                return body
            block.tensor(mk("pe"))
            block.vector(mk("dve"))
            block.scalar(mk("act"))
            block.gpsimd(mk("pool"))
            block.sync(mk("sp"))


def _mk(KBc):
    def tt(self, e, out, in0, in1, op, reads, writes):
        return self.op(e, lambda en: en.tensor_tensor(out=out, in0=in0, in1=in1, op=op), reads, writes)

    def ts(self, e, out, in0, s1, s2, op0, op1=None, reads=(), writes=()):
        if op1 is None:
            return self.op(e, lambda en: en.tensor_scalar(out=out, in0=in0, scalar1=s1, scalar2=None, op0=op0), reads, writes)
        return self.op(e, lambda en: en.tensor_scalar(out=out, in0=in0, scalar1=s1, scalar2=s2, op0=op0, op1=op1), reads, writes)

    def stt(self, out, in0, scalar, in1, op0, op1, reads, writes):
        return self.op("dve", lambda en: en.scalar_tensor_tensor(out=out, in0=in0, scalar=scalar, in1=in1, op0=op0, op1=op1), reads, writes)

    def cp(self, e, out, in_, reads, writes):
        if e == "act":
            return self.op(e, lambda en: en.activation(out=out, in_=in_, func=AF.Copy), reads, writes)
        return self.op(e, lambda en: en.tensor_copy(out=out, in_=in_), reads, writes)

    def act(self, out, in_, func, reads, writes, bias=None, scale=None, accum_out=None):
        kw = {}
        if bias is not None:
            kw["bias"] = bias
        if scale is not None:
            kw["scale"] = scale
        if accum_out is not None:
            kw["accum_out"] = accum_out
        return self.op("act", lambda en: en.activation(out=out, in_=in_, func=func, **kw), reads, writes)

    def mm(self, out, lhsT, rhs, start, stop, reads, writes, signal=None):
        if signal is None:
            signal = stop
        return self.op("pe", lambda en: en.matmul(out, lhsT=lhsT, rhs=rhs, start=start, stop=stop),
                       reads, writes, signal=signal, nosame=writes)

    def tr(self, out, in_, ident, reads, writes, signal=True):
        return self.op("pe", lambda en: en.transpose(out=out, in_=in_, identity=ident), reads, writes,
                       signal=signal, nosame=writes)

    def red(self, out, in_, reads, writes, op=None):
        op = op or ALU.add
        return self.op("dve", lambda en: en.tensor_reduce(out=out, in_=in_, axis=AX.X, op=op), reads, writes)

    def ms(self, e, ap, val, writes):
        return self.op(e, lambda en: en.memset(ap, val), (), writes)

    for f in (tt, ts, stt, cp, act, mm, tr, red, ms):
        setattr(KBc, f.__name__, f)


_mk(KB)


def _mk2(KBc):
    def coll(self, kind, in_ap, out_ap, groups, reads=(), writes=(), inc=1):
        if not hasattr(self, "cc_sem"):
            self.cc_sem = self.stack.enter_context(self.nc.semaphore("cc_sem"))
            self.cc_cnt = 0
        waits = self._waits("pool", reads, writes, ())
        self.cc_cnt += inc
        ev = Ev(self.cc_sem, self.cc_cnt)
        op = ALU.bypass
        self.stream["pool"].append(
            (waits, lambda en: en.collective_compute(kind, op, replica_groups=groups, ins=[in_ap], outs=[out_ap]),
             (self.cc_sem, inc)))
        for r in reads:
            r.r.append(ev)
        for w in writes:
            w.w = ev
            w.r = []
        return ev

    def barrier(self):
        evs = [(self.sem[e], self.cnt[e]) for e in self.ENGS if self.cnt[e] > 0]
        for r in self.allres:
            if r.dsem is not None and r.dcnt > 0:
                evs.append((r.dsem, r.dcnt))
        if hasattr(self, "cc_sem") and self.cc_cnt > 0:
            evs.append((self.cc_sem, self.cc_cnt))
        for e in self.ENGS:
            seen = self.seen[e]
            waits = []
            for s, v in evs:
                if s is self.sem[e]:
                    continue
                if seen.get(id(s), 0) >= v:
                    continue
                seen[id(s)] = v
                waits.append((s, v))
            self.stream[e].append((waits, None, None))

    KBc.coll = coll
    KBc.barrier = barrier


_mk2(KB)


NEG = -30000.0
EPS = 1e-6
GROUPS = [[j, 7 - j, 8 + j, 15 - j] for j in range(4)]
NT = 16
SEGS = [(0, 8, 64), (768, 10, 64), (1536, 16, 32)]
TBLK = [0, 128, 256, 384, 768, 896, 1024, 1152, 1280, 1536, 1664, 1792, 1920]
VSEG = [(512, 256), (1408, 128), (2048, 256)]


def emit_P(kb, D, ident):
    nc = kb.nc
    x, anorm, w_in, gall, ct64, st64, ct32, st32 = (D[k] for k in ("x", "anorm", "w_in", "gall", "ct64", "st64", "ct32", "st32"))
    qkT, vout, gT = D["qkT"], D["v"], D["gT"]
    idb, r_id = ident

    wq = kb.sb("wq", [128, 8, 2304], BF16); r_wq = kb.res("wq")
    wg = [kb.sb(f"wg{i}", [128, 8, 512], BF16) for i in range(2)]; r_wg = [kb.res(f"wg{i}") for i in range(2)]
    hT = kb.sb("hT", [128, 8, 2048], BF16); r_hT = [kb.res(f"hT{t}") for t in range(NT)]
    xt = [kb.sb(f"xt{i}", [128, 1024], F32) for i in range(2)]; r_xt = [kb.res(f"xt{i}") for i in range(2)]
    h16 = [kb.sb(f"h16{i}", [128, 1024], BF16) for i in range(2)]; r_h16 = [kb.res(f"h16{i}") for i in range(2)]
    pj = [kb.sb(f"pj{i}", [128, 2304], F32) for i in range(2)]; r_pj = [kb.res(f"pj{i}") for i in range(2)]
    xc = kb.sb("xc", [128, 2048], F32); r_xc = kb.res("xc")
    xs = kb.sb("xs", [128, 2048], F32); r_xs = kb.res("xs")
    qk16 = [kb.sb(f"qk16{i}", [128, 2048], BF16) for i in range(2)]; r_qk16 = [kb.res(f"qk16{i}") for i in range(2)]
    qkTs = [kb.sb(f"qkTs{i}", [128, 13, 128], BF16) for i in range(2)]; r_qkTs = [kb.res(f"qkTs{i}") for i in range(2)]
    v16 = [kb.sb(f"v16{i}", [128, 640], BF16) for i in range(2)]; r_v16 = [kb.res(f"v16{i}") for i in range(2)]
    g16 = [kb.sb(f"g16{i}", [128, 512], BF16) for i in range(2)]; r_g16 = [kb.res(f"g16{i}") for i in range(2)]
    an = kb.sb("an", [128, 1024], F32); r_an = kb.res("an")
    ga = kb.sb("ga", [128, 2048], F32); r_ga = kb.res("ga")
    c64 = kb.sb("c64", [128, NT, 64], F32); s64 = kb.sb("s64", [128, NT, 64], F32)
    c32 = kb.sb("c32", [128, NT, 32], F32); s32 = kb.sb("s32", [128, NT, 32], F32)
    r_tab = kb.res("tabs")
    epst = kb.sb("epst", [128, 1], F32); r_eps = kb.res("eps")
    st = [kb.sb(f"st{i}", [128, 40], F32) for i in range(2)]; r_st = [kb.res(f"st{i}") for i in range(2)]
    sx = [kb.sb(f"sx{i}", [128, 4], F32) for i in range(2)]; r_sx = [kb.res(f"sx{i}") for i in range(2)]
    ps_t = [kb.ps(f"ps_t{i}", [128, 1024], BF16) for i in range(2)]; r_ps_t = [kb.res(f"ps_t{i}") for i in range(2)]
    ps_m = [kb.ps(f"ps_m{i}", [128, 512], F32) for i in range(4)]; r_ps_m = [kb.res(f"ps_m{i}") for i in range(4)]
    ps_q = [kb.ps(f"ps_q{i}", [128, 1024], BF16) for i in range(2)]; r_ps_q = [kb.res(f"ps_q{i}") for i in range(2)]

    kb.ms("pool", epst[:], EPS, [r_eps])
    kb.dma("sp", an[:], anorm[:, :], writes=[r_an])
    kb.dma("sp", ga[:], gall[:, :], writes=[r_ga])
    kb.dma("sp", c64[:], ct64.rearrange("(t p) d -> p t d", p=128), writes=[r_tab])
    kb.dma("sp", s64[:], st64.rearrange("(t p) d -> p t d", p=128), writes=[r_tab])
    kb.dma("sp", c32[:], ct32.rearrange("(t p) d -> p t d", p=128), writes=[r_tab])
    kb.dma("sp", s32[:], st32.rearrange("(t p) d -> p t d", p=128), writes=[r_tab])
    wv = w_in.rearrange("(k p) c -> p k c", p=128)
    for k in range(8):
        for c0 in range(0, 2304, 1152):
            kb.dma("pool", wq[:, k, c0:c0 + 1152], wv[:, k, c0:c0 + 1152], writes=[r_wq])

    for t in range(NT):
        b = t % 2
        kb.dma("sp", xt[b][:], x[t * 128:(t + 1) * 128, :], writes=[r_xt[b]])
        kb.act(h16[b][:], xt[b][:], AF.Square, [r_xt[b]], [r_h16[b], r_sx[b]], accum_out=sx[b][:, 0:1])
        kb.act(sx[b][:, 1:2], sx[b][:, 0:1], AF.Ln, [r_sx[b], r_eps], [r_sx[b]], bias=epst[:], scale=1.0 / 1024)
        kb.act(sx[b][:, 2:3], sx[b][:, 1:2], AF.Exp, [r_sx[b]], [r_sx[b]], scale=-0.5)
        kb.stt(h16[b][:], xt[b][:], sx[b][:, 2:3], an[:], ALU.mult, ALU.mult, [r_xt[b], r_sx[b], r_an], [r_h16[b]])
        for k in range(8):
            kb.tr(ps_t[b][:, k * 128:(k + 1) * 128], h16[b][:, k * 128:(k + 1) * 128], idb[:],
                  [r_h16[b], r_id], [r_ps_t[b]], signal=(k == 7))
        kb.cp("act", hT[:, :, t * 128:(t + 1) * 128], ps_t[b][:].rearrange("p (k t) -> p k t", k=8), [r_ps_t[b]], [r_hT[t]])

    mcnt = 0
    for t in range(NT):
        b = t % 2
        for ci, (c0, cw) in enumerate([(0, 512), (512, 512), (1024, 512), (1536, 512), (2048, 256)]):
            pm = mcnt % 4; mcnt += 1
            for k in range(8):
                kb.mm(ps_m[pm][:, 0:cw], hT[:, k, t * 128:(t + 1) * 128], wq[:, k, c0:c0 + cw], k == 0, k == 7,
                      [r_hT[t], r_wq], [r_ps_m[pm]])
            kb.cp("act", pj[b][:, c0:c0 + cw], ps_m[pm][:, 0:cw], [r_ps_m[pm]], [r_pj[b]])
        vo = 0
        for (c0, cw) in VSEG:
            kb.cp("pool", v16[b][:, vo:vo + cw], pj[b][:, c0:c0 + cw], [r_pj[b]], [r_v16[b]])
            vo += cw
        kb.dma("sp", vout[t * 128:(t + 1) * 128, :], v16[b][:], reads=[r_v16[b]])
        so = 0
        for (c0, nh, d) in SEGS:
            kb.act(xs[:, c0:c0 + nh * d], pj[b][:, c0:c0 + nh * d], AF.Square, [r_pj[b]], [r_xs])
            kb.red(st[b][:, so:so + nh], xs[:, c0:c0 + nh * d].rearrange("p (h d) -> p h d", d=d), [r_xs], [r_st[b]])
            so += nh
        kb.act(st[b][:, 0:18], st[b][:, 0:18], AF.Ln, [r_st[b], r_eps], [r_st[b]], bias=epst[:], scale=1.0 / 64)
        kb.act(st[b][:, 18:34], st[b][:, 18:34], AF.Ln, [r_st[b], r_eps], [r_st[b]], bias=epst[:], scale=1.0 / 32)
        kb.act(st[b][:, 0:34], st[b][:, 0:34], AF.Exp, [r_st[b]], [r_st[b]], scale=-0.5)
        so = 0
        for si, (c0, nh, d) in enumerate(SEGS):
            w = nh * d
            e1 = "dve" if si != 1 else "pool"
            pv = pj[b][:, c0:c0 + w].rearrange("p (h d) -> p h d", d=d)
            rb = st[b][:, so:so + nh].rearrange("p (h o) -> p h o", o=1).to_broadcast([128, nh, d])
            kb.tt(e1, pv, pv, rb, ALU.mult, [r_pj[b], r_st[b]], [r_pj[b]])
            kb.tt(e1, pj[b][:, c0:c0 + w], pj[b][:, c0:c0 + w], ga[:, c0:c0 + w], ALU.mult, [r_pj[b], r_ga], [r_pj[b]])
            hd = d // 2
            ctab = (c64 if d == 64 else c32)[:, t, :]
            stab = (s64 if d == 64 else s32)[:, t, :]
            cb = ctab.rearrange("p (o d) -> p o d", o=1).to_broadcast([128, nh, d])
            kb.tt(e1, xc[:, c0:c0 + w].rearrange("p (h d) -> p h d", d=d), pv, cb, ALU.mult, [r_pj[b], r_tab], [r_xc])
            p4 = pj[b][:, c0:c0 + w].rearrange("p (h two e) -> p h two e", two=2, e=hd)
            x4 = xs[:, c0:c0 + w].rearrange("p (h two e) -> p h two e", two=2, e=hd)
            s0 = stab[:, 0:hd].rearrange("p (o d) -> p o d", o=1).to_broadcast([128, nh, hd])
            s1 = stab[:, hd:d].rearrange("p (o d) -> p o d", o=1).to_broadcast([128, nh, hd])
            kb.tt(e1, x4[:, :, 0, :], p4[:, :, 1, :], s0, ALU.mult, [r_pj[b], r_tab], [r_xs])
            kb.tt(e1, x4[:, :, 1, :], p4[:, :, 0, :], s1, ALU.mult, [r_pj[b], r_tab], [r_xs])
            kb.tt(e1, qk16[b][:, c0:c0 + w], xc[:, c0:c0 + w], xs[:, c0:c0 + w], ALU.add, [r_xc, r_xs], [r_qk16[b]])
            so += nh
        for bi, c0 in enumerate(TBLK):
            half = 0 if bi < 8 else 1
            col = (bi % 8) * 128
            kb.tr(ps_q[half][:, col:col + 128], qk16[b][:, c0:c0 + 128], idb[:], [r_qk16[b], r_id], [r_ps_q[half]],
                  signal=(bi == 7 or bi == 12))
        kb.cp("dve", qkTs[b][:, 0:8, :], ps_q[0][:].rearrange("p (k t) -> p k t", k=8), [r_ps_q[0]], [r_qkTs[b]])
        kb.cp("dve", qkTs[b][:, 8:13, :], ps_q[1][:, 0:640].rearrange("p (k t) -> p k t", k=5), [r_ps_q[1]], [r_qkTs[b]])
        kb.dma("sp", qkT[:, t * 128:(t + 1) * 128].rearrange("(k p) t -> p k t", p=128), qkTs[b][:], reads=[r_qkTs[b]])

    gcnt = 0
    for wc in range(6):
        wb = wc % 2
        for k in range(8):
            kb.dma("pool", wg[wb][:, k, :], wv[:, k, 2304 + wc * 512:2304 + (wc + 1) * 512], writes=[r_wg[wb]])
        for cc in range(4):
            for g in range(4):
                pm = mcnt % 4; mcnt += 1
                for k in range(8):
                    kb.mm(ps_m[pm][:], wg[wb][:, k, cc * 128:(cc + 1) * 128], hT[:, k, g * 512:(g + 1) * 512],
                          k == 0, k == 7, [r_wg[wb]] + r_hT[4 * g:4 * g + 4], [r_ps_m[pm]])
                gb = gcnt % 2; gcnt += 1
                kb.act(g16[gb][:], ps_m[pm][:], AF.Sigmoid, [r_ps_m[pm]], [r_g16[gb]])
                row = (wc * 4 + cc) * 128
                kb.dma("sp", gT[row:row + 128, g * 512:(g + 1) * 512], g16[gb][:], reads=[r_g16[gb]])
    return [r_v16[0], r_v16[1], r_qkTs[0], r_qkTs[1], r_g16[0], r_g16[1]]


EPS = 1e-6
NT = 16
NC2 = 22


def emit_F(kb, D, ident, pfx="f"):
    xm, xhalo, mnorm, w_up, convp, w_down, xo = (D[k] for k in ("xm", "xhalo", "mnorm", "w_up", "convp", "w_down", "xo"))
    idb, r_id = ident
    P = pfx
    hT = kb.sb(P + "hT", [128, 8, 2048], BF16); r_hT = [kb.res(P + f"hT{t}") for t in range(NT)]
    hTh = kb.sb(P + "hTh", [128, 8, 8], BF16); r_hTh = kb.res(P + "hTh")
    mT = kb.sb(P + "mT", [128, NC2, 1024], BF16); r_mT = [kb.res(P + f"mT{g}") for g in range(2)]
    wd = kb.sb(P + "wd", [128, NC2, 1024], BF16); r_wd = kb.res(P + "wd")
    wu = [kb.sb(P + f"wu{i}", [128, 8, 2, 128], BF16) for i in range(2)]; r_wu = [kb.res(P + f"wu{i}") for i in range(2)]
    xt = [kb.sb(P + f"xt{i}", [128, 1024], F32) for i in range(2)]; r_xt = [kb.res(P + f"xt{i}") for i in range(2)]
    h16 = [kb.sb(P + f"h16{i}", [128, 1024], BF16) for i in range(2)]; r_h16 = [kb.res(P + f"h16{i}") for i in range(2)]
    an = kb.sb(P + "an", [128, 1024], F32); r_an = kb.res(P + "an")
    cpar = kb.sb(P + "cpar", [128, 44, 4], F32); r_cp = kb.res(P + "cpar")
    epst = kb.sb(P + "epst", [128, 1], F32); r_eps = kb.res(P + "eps")
    sx = [kb.sb(P + f"sx{i}", [128, 4], F32) for i in range(2)]; r_sx = [kb.res(P + f"sx{i}") for i in range(2)]
    ub = [[kb.sb(P + f"ub{i}{s}", [128, 514], F32) for s in range(2)] for i in range(2)]
    r_ub = [[kb.res(P + f"ub{i}{s}") for s in range(2)] for i in range(2)]
    yb = [[kb.sb(P + f"yb{i}{s}", [128, 512], F32) for s in range(2)] for i in range(2)]
    r_yb = [[kb.res(P + f"yb{i}{s}") for s in range(2)] for i in range(2)]
    ob = [kb.sb(P + f"ob{i}", [128, 1024], F32) for i in range(2)]; r_ob = [kb.res(P + f"ob{i}") for i in range(2)]
    ps_t = kb.ps(P + "ps_t", [128, 1024], BF16); r_ps_t = kb.res(P + "ps_t")
    pu = [[kb.ps(P + f"pu{i}{s}", [128, 512], F32) for s in range(2)] for i in range(2)]
    r_pu = [[kb.res(P + f"pu{i}{s}") for s in range(2)] for i in range(2)]
    ph = kb.ps(P + "ph", [128, 16], F32); r_ph = kb.res(P + "ph")
    po = [kb.ps(P + f"po{i}", [128, 512], F32) for i in range(2)]; r_po = [kb.res(P + f"po{i}") for i in range(2)]

    kb.ms("pool", epst[:], EPS, [r_eps])
    kb.dma("sp", an[:], mnorm[:, :], writes=[r_an])
    kb.dma("sp", cpar[:], convp[:, :, :], writes=[r_cp])
    wdv = w_down.rearrange("(c p) n -> p c n", p=128)
    for c in range(NC2):
        kb.dma("pool", wd[:, c, :], wdv[:, c, :], writes=[r_wd])

    for t in range(NT + 1):
        b = t % 2
        n = 128 if t < NT else 8
        src = xm[t * 128:(t + 1) * 128, :] if t < NT else xhalo[:, :]
        kb.dma("sp", xt[b][0:n, :], src, writes=[r_xt[b]])
        kb.act(h16[b][0:n, :], xt[b][0:n, :], AF.Square, [r_xt[b]], [r_h16[b], r_sx[b]], accum_out=sx[b][0:n, 0:1])
        kb.act(sx[b][0:n, 1:2], sx[b][0:n, 0:1], AF.Ln, [r_sx[b], r_eps], [r_sx[b]], bias=epst[0:n, :], scale=1.0 / 1024)
        kb.act(sx[b][0:n, 2:3], sx[b][0:n, 1:2], AF.Exp, [r_sx[b]], [r_sx[b]], scale=-0.5)
        kb.stt(h16[b][0:n, :], xt[b][0:n, :], sx[b][0:n, 2:3], an[0:n, :], ALU.mult, ALU.mult, [r_xt[b], r_sx[b], r_an], [r_h16[b]])
        for k in range(8):
            kb.tr(ps_t[:, k * 128:k * 128 + n], h16[b][0:n, k * 128:(k + 1) * 128], idb[0:n, 0:n],
                  [r_h16[b], r_id], [r_ps_t], signal=(k == 7))
        if t < NT:
            kb.cp("act", hT[:, :, t * 128:(t + 1) * 128], ps_t[:].rearrange("p (k t) -> p k t", k=8), [r_ps_t], [r_hT[t]])
        else:
            kb.cp("act", hTh[:], ps_t[:].rearrange("p (k t) -> p k t", k=8)[:, :, 0:8], [r_ps_t], [r_hTh])

    wuv = w_up.rearrange("(k p) c -> p k c", p=128)
    it = 0
    for half in range(2):
        for c in range(NC2):
            wb = it % 2; it += 1
            for s in range(2):
                col = s * 2816 + c * 128
                kb.dma("pool", wu[wb][:, :, s, :], wuv[:, :, col:col + 128], writes=[r_wu[wb]])
            for s in range(2):
                for k in range(8):
                    kb.mm(ph[:, s * 8:(s + 1) * 8], wu[wb][:, k, s, :], hTh[:, k, :], k == 0, k == 7,
                          [r_wu[wb], r_hTh], [r_ph], signal=(s == 1 and k == 7))
            for gi in range(2):
                g = half * 2 + gi
                ib = (c * 2 + gi) % 2
                for s in range(2):
                    for k in range(8):
                        kb.mm(pu[ib][s][:], wu[wb][:, k, s, :], hT[:, k, g * 512:(g + 1) * 512], k == 0, k == 7,
                              [r_wu[wb]] + r_hT[4 * g:4 * g + 4], [r_pu[ib][s]])
                for s in range(2):
                    ci = s * NC2 + c
                    u, ru = ub[ib][s], r_ub[ib][s]
                    y, ry = yb[ib][s], r_yb[ib][s]
                    kb.cp("dve", u[:, 0:2], ph[:, s * 8 + g * 2:s * 8 + g * 2 + 2], [r_ph], [ru])
                    kb.cp("act", u[:, 2:514], pu[ib][s][:], [r_pu[ib][s]], [ru])
                    kb.act(y[:], pu[ib][s][:], AF.Identity, [r_pu[ib][s], r_cp], [ry], bias=cpar[:, ci, 3:4], scale=cpar[:, ci, 2:3])
                    kb.stt(y[:], u[:, 1:513], cpar[:, ci, 1:2], y[:], ALU.mult, ALU.add, [ru, ry, r_cp], [ry])
                    kb.stt(y[:], u[:, 0:512], cpar[:, ci, 0:1], y[:], ALU.mult, ALU.add, [ru, ry, r_cp], [ry])
                yg, yv = yb[ib][0], yb[ib][1]
                kb.act(yg[:], yg[:], AF.Silu, [r_yb[ib][0]], [r_yb[ib][0]])
                kb.tt("dve", mT[:, c, gi * 512:(gi + 1) * 512], yg[:], yv[:], ALU.mult, [r_yb[ib][0], r_yb[ib][1]], [r_mT[gi]])
        for tt_ in range(8):
            t = half * 8 + tt_
            b = t % 2
            kb.dma("sp", xt[b][:], xm[t * 128:(t + 1) * 128, :], writes=[r_xt[b]])
            for hc in range(2):
                for c in range(NC2):
                    kb.mm(po[hc][:], mT[:, c, tt_ * 128:(tt_ + 1) * 128], wd[:, c, hc * 512:(hc + 1) * 512], c == 0, c == NC2 - 1,
                          [r_mT[tt_ // 4], r_wd], [r_po[hc]])
                kb.tt("dve", ob[b][:, hc * 512:(hc + 1) * 512], po[hc][:], xt[b][:, hc * 512:(hc + 1) * 512], ALU.add,
                      [r_po[hc], r_xt[b]], [r_ob[b]])
            kb.dma("sp", xo[t * 128:(t + 1) * 128, :], ob[b][:], reads=[r_ob[b]])
    return [r_ob[0], r_ob[1]]


NEG = -30000.0
BIG = 30000.0
EPS = 1e-6
NSTREAM = [12, 28, 44, 60]
FAST_RECIP = False


def RECIP(en):
    return en.reciprocal_approx_fast if FAST_RECIP else en.reciprocal


STOP = 99


def emit_A(kb, D, ident, lam_init, pfx="a"):
    P = pfx
    idb, r_id = ident
    qkT, vown_d, kTf, vhp, kTbh, vbh_d = (D[k] for k in ("qkT", "v_own", "kT_full", "v_hp", "kTb_halo", "vb_halo"))

    def S(name, shape, dt):
        return kb.sb(P + name, shape, dt), kb.res(P + name)

    kbuf = kb.sb(P + "kbuf", [128, 2, 8192], BF16)
    r_kb2 = [kb.res(P + "kbuf0"), kb.res(P + "kbuf1")]
    vaugs = [S(f"vaug{i}", [128, 64, 128], BF16) for i in range(2)]
    vaug, r_vaug = vaugs[0]
    vstg, r_vstg = S("vstg", [128, 64, 64], BF16)
    kown = kb.sb(P + "kown", [128, 2, 2048], BF16)
    r_ko2 = [kb.res(P + "kown0"), kb.res(P + "kown1")]
    vowns = [S(f"vown{i}", [128, 16, 128], BF16) for i in range(2)]
    vown, r_vown = vowns[0]
    vostg, r_vostg = S("vostg", [128, 16, 64], BF16)
    qt = kb.sb(P + "qt", [128, 2, 2048], BF16)
    r_qt2 = [kb.res(P + "qt0"), kb.res(P + "qt1")]
    yT, r_yT = S("yT", [128, 8, 2048], BF16)
    mdiag, r_md = S("mdiag", [128, 4, 512], BF16)
    mB, r_mB = S("mB", [128, 2, 512], BF16)
    pmA, r_pmA = S("pmA", [128, 16, 4, 32], F32)
    hval, r_hval = S("hval", [128, 4], F32)
    lamv, r_lamv = S("lamv", [128, 4, 32], F32)
    lamt, r_lamt = S("lamt", [128, 8], F32)
    sgc, r_sgc = S("sgc", [128, 1], F32)
    sinkt, r_sinkt = S("sinkt", [1, 8], F32)
    sinke, r_sinke = S("sinke", [1, 8], F32)
    sh16, r_sh = S("sh16", [1, 8], BF16)
    sl16, r_sl = S("sl16", [1, 8], BF16)
    shf, r_shf = S("shf", [1, 8], F32)
    sinkrow, r_sinkrow = S("sinkrow", [1, 2, 8, 128], BF16)
    srow, r_srow = S("srow", [1, 128], BF16)
    ones64, r_ones64 = S("ones64", [64, 64], BF16)
    epst, r_eps = S("epst", [128, 1], F32)
    pt = [S(f"pt{i}", [128, 1024], BF16) for i in range(4)]
    nsb = [S(f"nsb{i}", [64, 512], F32) for i in range(2)]
    dsb = [S(f"dsb{i}", [64, 512], F32) for i in range(2)]
    rcp = [S(f"rcp{i}", [64, 512], F32) for i in range(2)]
    t1, r_t1 = S("t1", [64, 512], F32)
    t2, r_t2 = S("t2", [64, 512], F32)
    sq16, r_sq16 = S("sq16", [64, 512], BF16)
    rs, r_rs = S("rs", [64, 512], F32)
    kmean, r_kmean = S("kmean", [64, 32], F32)
    kmean16, r_km16 = S("kmean16", [64, 32], BF16)
    gm, r_gm = S("gm", [128, 16, 32], F32)
    t8, r_t8 = S("t8", [128, 16, 8], F32)
    sel, r_sel = S("sel", [128, 16, 32], F32)
    bst, r_bst = S("bst", [128, 16, 64], BF16)
    tmpb, r_tmpb = S("tmpb", [128, 16, 32], F32)
    ps_s = [(kb.ps(P + f"ps_s{i}", [128, 1024], F32), kb.res(P + f"ps_s{i}")) for i in range(2)]
    ps_o = [(kb.ps(P + f"ps_o{i}", [128, 512], F32), kb.res(P + f"ps_o{i}")) for i in range(2)]
    ps_x = [(kb.ps(P + f"ps_x{i}", [128, 512], F32), kb.res(P + f"ps_x{i}")) for i in range(1)]
    ps_b, r_ps_b = kb.ps(P + "ps_b", [128, 1024], BF16), kb.res(P + "ps_b")

    kb.ms("pool", epst[:], EPS, [r_eps])
    for i_ in range(2):
        kb.ms("pool", vaugs[i_][0][:, :, 64:128], 1.0, [vaugs[i_][1]])
        kb.ms("pool", vowns[i_][0][:, :, 64:128], 1.0, [vowns[i_][1]])
    kb.ms("pool", ones64[:], 1.0, [r_ones64])
    kb.ms("pool", srow[:, 0:64], 0.0, [r_srow])
    kb.ms("pool", srow[:, 64:128], 1.0, [r_srow])
    kb.dma("sp", mdiag[:], D["mdiag"][:, :, :], writes=[r_md])
    kb.dma("sp", mB[:], D["mB"][:, :, :], writes=[r_mB])
    kb.dma("sp", pmA[:], D["pmA"][:, :, :, :], writes=[r_pmA])
    kb.dma("sp", hval[:], D["hval"][:, :], writes=[r_hval])
    kb.dma("sp", lamv[:], D["lamv"][:, :, :], writes=[r_lamv])
    kb.dma("sp", sgc[:], D["sgc"][:, :], writes=[r_sgc])
    kb.dma("sp", sinkt[:], D["sinks"][0:1, :], writes=[r_sinkt])
    sinkf, r_sinkf = S("sinkf", [128, 8], F32)
    esink, r_esink = S("esink", [128, 8], F32)
    kb.dma("sp", sinkf[:], D["sinks"][:, :], writes=[r_sinkf])
    kb.act(esink[:], sinkf[:], AF.Exp, [r_sinkf], [r_esink])
    kb.tt("dve", lamv[:, 0, :], lamv[:, 0, :], lamv[:, 1, :], ALU.mult, [r_lamv], [r_lamv])
    kb.tt("dve", lamv[:, 2, :], lamv[:, 2, :], lamv[:, 3, :], ALU.mult, [r_lamv], [r_lamv])
    kb.red(lamt[:, 0:1], lamv[:, 0, :], [r_lamv], [r_lamt])
    kb.red(lamt[:, 1:2], lamv[:, 2, :], [r_lamv], [r_lamt])
    kb.act(lamt[:, 2:4], lamt[:, 0:2], AF.Exp, [r_lamt], [r_lamt])
    kb.tt("dve", lamt[:, 4:5], lamt[:, 3:4], lamt[:, 2:3], ALU.subtract, [r_lamt], [r_lamt])
    kb.ts("dve", lamt[:, 4:5], lamt[:, 4:5], -float(lam_init), None, ALU.add, None, [r_lamt], [r_lamt])
    kb.act(sinke[:], sinkt[:], AF.Exp, [r_sinkt], [r_sinke])
    kb.cp("dve", sh16[:], sinke[:], [r_sinke], [r_sh])
    kb.cp("dve", shf[:], sh16[:], [r_sh], [r_shf])
    kb.tt("dve", shf[:], sinke[:], shf[:], ALU.subtract, [r_sinke, r_shf], [r_shf])
    kb.cp("dve", sl16[:], shf[:], [r_shf], [r_sl])
    kb.cp("dve", sinkrow[:, 0, :, :], sh16[:].rearrange("p (h o) -> p h o", o=1).to_broadcast([1, 8, 128]), [r_sh], [r_sinkrow])
    kb.cp("dve", sinkrow[:, 1, :, :], sl16[:].rearrange("p (h o) -> p h o", o=1).to_broadcast([1, 8, 128]), [r_sl], [r_sinkrow])

    def _fin():
        kb.dma("sp", D["yT_out"].rearrange("(k p) t -> p k t", p=128), yT[:], reads=[r_yT])
        return [r_yT]

    if STOP <= 0:
        return []
    cnt = {"s": 0, "p": 0, "o": 0, "x": 0, "r": 0}

    def nxt(k, n):
        v = cnt[k] % n
        cnt[k] += 1
        return v

    class Pipe:
        def __init__(self):
            self.prev = None
            self.cbs = []
            self.group = []

        def _S(self, tiles):
            ps, r_ps = ps_s[nxt("s", 2)]
            for j, t in enumerate(tiles):
                reg = ps[:, j * 512:(j + 1) * 512]
                kl, ql, mask, rds = t["kl"], t["ql"], t["mask"], t["rds"]
                lastt = j == len(tiles) - 1
                if isinstance(ql, list):
                    kb.mm(reg, idb[:], mask[0], True, False, [r_id, mask[1]], [r_ps], signal=False)
                    for gi, qp in enumerate(ql):
                        fin = gi == len(ql) - 1
                        kb.mm(reg[:, gi * 128:(gi + 1) * 128], kl, qp, False, fin, rds, [r_ps], signal=fin and lastt)
                else:
                    kb.mm(reg, kl, ql, True, mask is None, rds, [r_ps], signal=(mask is None) and lastt)
                    if mask is not None:
                        kb.mm(reg, idb[:], mask[0], False, True, [r_id, mask[1]], [r_ps], signal=lastt)
            return ps, r_ps

        def _drain(self):
            if self.prev is not None:
                tiles, (ps, r_ps) = self.prev
                n = len(tiles)
                p, r_p = pt[nxt("p", 4)]
                kb.act(p[:, 0:n * 512], ps[:, 0:n * 512], AF.Exp, [r_ps], [r_p], scale=tiles[0]["scale"])
                for j, t in enumerate(tiles):
                    kb.mm(t["acc"][0][:], t["vl"], p[:, j * 512:(j + 1) * 512], t["first"], t["last"], [r_p] + t["rds"], [t["acc"][1]])
                self.prev = None
            keep = []
            for item in self.cbs:
                if item[0] <= 0:
                    item[1]()
                else:
                    item[0] -= 1
                    keep.append(item)
            self.cbs = keep

        def _emit(self):
            tiles, self.group = self.group, []
            ps = self._S(tiles)
            self._drain()
            self.prev = (tiles, ps)

        def push(self, kl, ql, vl, acc, first, last, scale, rds, mask=None):
            self.group.append(dict(kl=kl, ql=ql, vl=vl, acc=acc, first=first, last=last, scale=scale, rds=rds, mask=mask))
            if len(self.group) == 2:
                self._emit()

        def after(self, cb, delay=0):
            assert not self.group
            self.cbs.append([delay, cb])

        def sync(self):
            if self.group:
                self._emit()
            self._drain()

        def flush(self):
            if self.group:
                self._emit()
            self._drain()
            while self.cbs:
                item = self.cbs.pop(0)
                item[1]()

    def release(acc, k):
        n_, rn_ = nsb[k]
        d_, rd_ = dsb[k]
        kb.cp("dve", n_[:], acc[0][0:64, :], [acc[1]], [rn_])
        kb.cp("dve", d_[:], acc[0][64:128, :], [acc[1]], [rd_])
        return n_, rn_, d_, rd_

    pipe = Pipe()

    def pe_warm(n):
        px, r_px = ps_x[0]
        for q in range(n):
            kb.mm(px[:], idb[:], mdiag[:, 0, :], True, True, [r_id, r_md], [r_px], signal=(q == n - 1))

    def attn_tile(kl, ql, vl, acc, first, last, scale, rds, mask=None):
        pipe.push(kl, ql, vl, acc, first, last, scale, rds, mask)

    for par in range(2):
        kb.ms("pool", kbuf[64:128, par, :], 0.0, [r_kb2[par]])
        kb.ms("pool", kown[64:96, par, :], 0.0, [r_ko2[par]])

    def a_loads(h, par):
        kb.dma("sp", kbuf[0:64, par, :], kTf[h * 64:(h + 1) * 64, :], writes=[r_kb2[par]])
        kb.dma("sp", kbuf[64:96, par, :], D["ohA"][:, :], writes=[r_kb2[par]])
        kb.dma("sp", qt[0:64, par, :], qkT[h * 64:(h + 1) * 64, :], writes=[r_qt2[par]])
        kb.dma("sp", vstg[:], vhp[h], writes=[r_vstg])
        kb.cp("pool", vaugs[par][0][:, :, 0:64], vstg[:], [r_vstg], [vaugs[par][1]])
        kb.dma("sp", kown[0:64, par, :], qkT[256 + h * 64:256 + (h + 1) * 64, :], writes=[r_ko2[par]])
        kb.dma("sp", kown[96:128, par, :], D["ohA_own"][:, :], writes=[r_ko2[par]])
        kb.dma("sp", vostg[:], vown_d[:, h * 64:(h + 1) * 64].rearrange("(c p) d -> p c d", p=128), writes=[r_vostg])
        kb.cp("pool", vowns[par][0][:, :, 0:64], vostg[:], [r_vostg], [vowns[par][1]])

    def a_bias1(h, par):
        r_kbuf, r_qt = r_kb2[par], r_qt2[par]
        kb.red(kmean[:], kbuf[0:64, par, :].rearrange("p (n l) -> p n l", l=256), [r_kbuf], [r_kmean])
        kb.ts("dve", kmean16[:], kmean[:], 1.0 / 256, None, ALU.mult, None, [r_kmean], [r_km16])

    def a_bias2(h, par):
        r_kbuf, r_qt = r_kb2[par], r_qt2[par]
        pg, r_pg = ps_x[0]
        for c in range(16):
            kb.mm(pg[:, c * 32:(c + 1) * 32], qt[0:64, par, c * 128:(c + 1) * 128], kmean16[:], True, True,
                  [r_qt, r_km16], [r_pg], signal=(c == 15))
        kb.tt("dve", gm[:], pg[:].rearrange("p (c n) -> p c n", n=32), pmA[:, :, 0, :], ALU.add, [r_pg, r_pmA], [r_gm])
        for c in range(16):
            kb.op("dve", (lambda c: lambda en: en.max(out=t8[:, c, :], in_=gm[:, c, :]))(c), [r_gm], [r_t8])
        kb.tt("dve", sel[:], gm[:], t8[:, :, 2:3].to_broadcast([128, 16, 32]), ALU.is_ge, [r_gm, r_t8], [r_sel])
        for which in range(2):
            kb.tt("dve", tmpb[:], sel[:], pmA[:, :, 1 + which, :], ALU.mult, [r_sel, r_pmA], [r_tmpb])
            if which == 1:
                kb.tt("dve", tmpb[:], tmpb[:], pmA[:, :, 3, :], ALU.add, [r_tmpb, r_pmA], [r_tmpb])
            kb.ts("dve", bst[:, :, which * 32:(which + 1) * 32], tmpb[:], BIG, -BIG, ALU.mult, ALU.add, [r_tmpb], [r_bst])

    def a_bias3(h, par):
        r_kbuf, r_qt = r_kb2[par], r_qt2[par]
        for hf in range(2):
            for c8 in range(8):
                c = hf * 8 + c8
                kb.tr(ps_b[0:64, c8 * 128:(c8 + 1) * 128], bst[:, c, :], idb[:], [r_bst, r_id], [r_ps_b], signal=(c8 == 7))
            kb.cp("dve", qt[64:128, par, hf * 1024:(hf + 1) * 1024], ps_b[0:64, :], [r_ps_b], [r_qt])

    a_loads(0, 0)
    pe_warm(48)
    a_bias1(0, 0)
    a_bias2(0, 0)
    a_bias3(0, 0)
    for h in range(4):
        par = h % 2
        r_kbuf, r_qt, r_kown = r_kb2[par], r_qt2[par], r_ko2[par]
        vaug_h, r_vaug_h = vaugs[par]
        vown_h, r_vown_h = vowns[par]
        if h + 1 < 4:
            pipe.sync()
            a_loads(h + 1, 1 - par)
            pipe.after((lambda hh, pp: lambda: a_bias1(hh, pp))(h + 1, 1 - par), delay=8)
            pipe.after((lambda hh, pp: lambda: a_bias2(hh, pp))(h + 1, 1 - par), delay=24)
            pipe.after((lambda hh, pp: lambda: a_bias3(hh, pp))(h + 1, 1 - par), delay=44)
        for i in range(4):
            acc = ps_o[nxt("o", 2)]
            qs = qt[:, par, i * 512:(i + 1) * 512]
            for kt in range(NSTREAM[i]):
                attn_tile(kbuf[:, par, kt * 128:(kt + 1) * 128], qs, vaug_h[:, kt, :], acc, kt == 0, False, 0.125,
                          [r_kbuf, r_qt, r_vaug_h])
            for t in range(4):
                c = 4 * i + t
                attn_tile(kown[:, par, c * 128:(c + 1) * 128], qs, vown_h[:, c, :], acc, False, t == 3, 0.125,
                          [r_kown, r_qt, r_vown_h], mask=(mdiag[:, t, :], r_md))

            def fin_a(acc=acc, h=h, i=i):
                k_ = nxt("r", 2)
                n_, rn_, d_, rd_ = release(acc, k_)
                rc, r_rc = rcp[k_]
                kb.op("dve", lambda en: en.reciprocal(out=rc[:], in_=d_[:]), [rd_], [r_rc])
                kb.tt("dve", yT[(h % 2) * 64:(h % 2) * 64 + 64, h // 2, i * 512:(i + 1) * 512], n_[:], rc[:], ALU.mult,
                      [rn_, r_rc], [r_yT])
            pipe.after(fin_a)
    pipe.flush()

    if STOP <= 2:
        return _fin()
    sc_c = float(32 ** -0.5)
    for m in range(2):
        kb.dma("sp", kbuf[64 * m + 32:64 * m + 48, 0, :], D["ohC"][:, :], writes=[r_kb2[0]])
        kb.dma("sp", qt[64 * m + 32:64 * m + 48, 0, :], D["cbC"][:, :], writes=[r_qt2[0]])

    def c_vload(h):
        par = h % 2
        kb.dma("sp", vstg[:], vhp[6 + h], writes=[r_vstg])
        kb.cp("pool", vaugs[par][0][:, :, 0:64], vstg[:], [r_vstg], [vaugs[par][1]])
        kb.dma("sp", vostg[:], vown_d[:, 384 + h * 64:384 + (h + 1) * 64].rearrange("(c p) d -> p c d", p=128), writes=[r_vostg])
        kb.cp("pool", vowns[par][0][:, :, 0:64], vostg[:], [r_vostg], [vowns[par][1]])

    c_vload(0)
    for h in range(4):
        par = h % 2
        vaug_h, r_vaug_h = vaugs[par]
        vown_h, r_vown_h = vowns[par]
        for m in range(2):
            r0 = 384 + h * 64 + m * 32
            kb.dma("sp", kbuf[64 * m:64 * m + 32, 0, :], kTf[r0:r0 + 32, :], writes=[r_kb2[0]])
            q0 = 1152 + h * 64 + m * 32
            kb.dma("sp", qt[64 * m:64 * m + 32, 0, :], qkT[q0:q0 + 32, :], writes=[r_qt2[0]])
            k0 = 1408 + h * 64 + m * 32
            kb.dma("sp", kown[64 * m:64 * m + 32, 0, :], qkT[k0:k0 + 32, :], writes=[r_ko2[0]])
        if h + 1 < 4:
            c_vload(h + 1)
        pe_warm(64)
        for i in range(4):
            accs = [ps_o[0], ps_o[1]]
            for kt in range(NSTREAM[i]):
                for m in range(2):
                    attn_tile(kbuf[64 * m:64 * m + 48, 0, kt * 128:(kt + 1) * 128], qt[64 * m:64 * m + 48, 0, i * 512:(i + 1) * 512], vaug_h[:, kt, :],
                              accs[m], kt == 0, False, sc_c, [r_kb2[0], r_qt2[0], r_vaug_h])
            for t in range(4):
                c = 4 * i + t
                for m in range(2):
                    attn_tile(kown[64 * m:64 * m + 32, 0, c * 128:(c + 1) * 128], qt[64 * m:64 * m + 32, 0, i * 512:(i + 1) * 512], vown_h[:, c, :],
                              accs[m], False, t == 3, sc_c, [r_ko2[0], r_qt2[0], r_vown_h], mask=(mdiag[:, t, :], r_md))

            def fin_c1(accs=accs):
                rel = [release(accs[m], m) for m in range(2)]
                for m in range(2):
                    kb.op("dve", (lambda m: lambda en: en.reciprocal(out=rcp[m][0][:], in_=rel[m][2][:]))(m), [rel[m][3]], [rcp[m][1]])
                kb.tt("dve", t1[:], rel[0][0][:], rcp[0][0][:], ALU.mult, [rel[0][1], rcp[0][1]], [r_t1])
                kb.tt("dve", t2[:], rel[1][0][:], rcp[1][0][:], ALU.mult, [rel[1][1], rcp[1][1]], [r_t2])
                kb.stt(t1[:], t2[:], lamt[0:64, 4:5], t1[:], ALU.mult, ALU.add, [r_t1, r_t2, r_lamt], [r_t1])
                kb.act(sq16[:], t1[:], AF.Square, [r_t1], [r_sq16])

            def fin_c2(h=h, i=i):
                px, r_px = ps_x[0]
                kb.mm(px[0:64, :], ones64[:], sq16[:], True, True, [r_ones64, r_sq16], [r_px])
                kb.act(rs[:], px[0:64, :], AF.Ln, [r_px, r_eps], [r_rs], bias=epst[0:64, :], scale=1.0 / 64)
                kb.act(rs[:], rs[:], AF.Exp, [r_rs], [r_rs], scale=-0.5)
                kb.tt("dve", t1[:], t1[:], rs[:], ALU.mult, [r_t1, r_rs], [r_t1])
                kb.ts("dve", yT[(h % 2) * 64:(h % 2) * 64 + 64, 6 + h // 2, i * 512:(i + 1) * 512], t1[:], sgc[0:64, :], float(1.0 - lam_init),
                      ALU.mult, ALU.mult, [r_t1, r_sgc], [r_yT])
            pipe.after(fin_c1)
            pipe.after(fin_c2, delay=12)
        pipe.flush()

    if STOP <= 3:
        return _fin()
    qb = qt
    for k in range(2):
        qv = kbuf[0:64, 0, :].rearrange("p (g t) -> p g t", g=4)
        kb.dma("sp", qv, qkT[512 + k * 256:512 + (k + 1) * 256, :].rearrange("(g d) t -> d g t", d=64), writes=[r_kb2[0]])
        kb.dma("sp", kown[0:64, 0, :], qkT[1024 + k * 64:1024 + (k + 1) * 64, :], writes=[r_ko2[0], r_ko2[1]])
        kb.dma("sp", kown[0:64, 1, 0:512], kTbh[k * 64:(k + 1) * 64, :], writes=[r_ko2[0], r_ko2[1]])
        kb.dma("sp", vostg[:], vown_d[:, 256 + k * 64:256 + (k + 1) * 64].rearrange("(c p) d -> p c d", p=128), writes=[r_vostg])
        kb.cp("pool", vown[:, :, 0:64], vostg[:], [r_vostg], [r_vown])
        kb.dma("sp", vstg[:, 0:4, :], vbh_d[:, k * 64:(k + 1) * 64].rearrange("(s p) d -> p s d", p=128), writes=[r_vstg])
        kb.cp("pool", vaug[:, 0:4, 0:64], vstg[:, 0:4, :], [r_vstg], [r_vaug])
        if k == 0:
            kb.ms("pool", vaug[:, 4:8, 64:128], 1.0, [r_vaug])
        for s in range(4):
            kb.ts("dve", vaug[:, s, 0:64], vaug[:, s, 0:64], hval[:, s:s + 1], None, ALU.mult, None, [r_vaug, r_hval], [r_vaug])
            kb.ts("dve", vaug[:, s, 64:128], vaug[:, 4 + s, 64:128], hval[:, s:s + 1], None, ALU.mult, None, [r_vaug, r_hval], [r_vaug])
        pe_warm(48)
        for c in range(16):
            s = c // 4
            acc = ps_o[nxt("o", 2)]
            if c % 4 == 0:
                kprev, vprev = kown[0:64, 1, s * 128:(s + 1) * 128], vaug[:, s, :]
            else:
                kprev, vprev = kown[0:64, 0, (c - 1) * 128:c * 128], vown[:, c - 1, :]
            rds = [r_ko2[0], r_ko2[1], r_kb2[0], r_vown, r_vaug]
            qparts = [qv[:, gi, c * 128:(c + 1) * 128] for gi in range(4)]
            attn_tile(kprev, qparts, vprev, acc, True, False, 0.125, rds, mask=(mB[:, 0, :], r_mB))
            attn_tile(kown[0:64, 0, c * 128:(c + 1) * 128], qparts, vown[:, c, :], acc, False, True, 0.125, rds, mask=(mB[:, 1, :], r_mB))

            def fin_b(acc=acc, c=c, k=k):
                k_ = nxt("r", 2)
                n_, rn_, d_, rd_ = release(acc, k_)
                rc, r_rc = rcp[k_]
                kb.cp("dve", t2[:], d_[:], [rd_], [r_t2])
                for gi in range(4):
                    hh = 4 * k + gi
                    kb.ts("dve", t2[:, gi * 128:(gi + 1) * 128], t2[:, gi * 128:(gi + 1) * 128], esink[0:64, hh:hh + 1], None, ALU.add, None,
                          [r_t2, r_esink], [r_t2])
                kb.op("dve", lambda en: en.reciprocal(out=rc[:], in_=t2[:]), [r_t2], [r_rc])
                for gi in range(4):
                    hh = 4 * k + gi
                    kb.tt("dve", yT[(hh % 2) * 64:(hh % 2) * 64 + 64, 2 + hh // 2, c * 128:(c + 1) * 128],
                          n_[:, gi * 128:(gi + 1) * 128], rc[:, gi * 128:(gi + 1) * 128], ALU.mult, [rn_, r_rc], [r_yT])
            pipe.after(fin_b)
        pipe.flush()
    return _fin()


def emit_M(kb, D, pfx="m"):
    P = pfx
    gT, x, xmid = D["gT"], D["x"], D["xmid"]

    def S(name, shape, dt):
        return kb.sb(P + name, shape, dt), kb.res(P + name)

    cnt = {"x": 0}

    def nxt(k, n):
        v = cnt[k] % n
        cnt[k] += 1
        return v

    ps_x = [(kb.ps(P + f"ps_x{i}", [128, 512], F32), kb.res(P + f"ps_x{i}")) for i in range(4)]
    yT, r_yT = S("yT", [128, 8, 2048], BF16)
    kb.dma("sp", yT[:], D["yT_in"].rearrange("(k p) t -> p k t", p=128), writes=[r_yT])
    wp, r_wp = S("wp", [128, 8, 1024], BF16)
    wo, r_wo = S("wo", [128, 8, 1024], BF16)
    for nm, k0, nk in (("w_pa", 0, 2), ("w_pb", 2, 4), ("w_pc", 6, 2)):
        wv = D[nm].rearrange("(k p) n -> p k n", p=128)
        for k in range(nk):
            kb.dma("pool", wp[:, k0 + k, :], wv[:, k, :], writes=[r_wp])
    wov = D["w_out"].rearrange("(k p) n -> p k n", p=128)
    for k in range(8):
        kb.dma("pool", wo[:, k, :], wov[:, k, :], writes=[r_wo])
    gt = [S(f"gt{i}", [128, 3, 512], BF16) for i in range(2)]
    macc, r_macc = S("macc", [128, 512], F32)
    mtmp, r_mtmp = S("mtmp", [128, 512], F32)
    mT, r_mT = S("mT", [128, 8, 512], BF16)
    xt = [S(f"xt{i}", [128, 1024], F32) for i in range(2)]
    ob = [S(f"ob{i}", [128, 1024], F32) for i in range(2)]
    gTv = gT.rearrange("(br f) t -> f br t", br=3)
    BR = ((0, 2), (2, 4), (6, 2))
    gi_ = 0
    for g in range(4):
        for fc in range(8):
            gtt, r_gtt = gt[gi_ % 2]; gi_ += 1
            kb.dma("sp", gtt[:], gTv[fc * 128:(fc + 1) * 128, :, g * 512:(g + 1) * 512], writes=[r_gtt])
            for br, (k0, nk) in enumerate(BR):
                px, r_px = ps_x[nxt("x", 4)]
                for k in range(nk):
                    kb.mm(px[:], wp[:, k0 + k, fc * 128:(fc + 1) * 128], yT[:, k0 + k, g * 512:(g + 1) * 512], k == 0, k == nk - 1,
                          [r_wp, r_yT], [r_px])
                if br == 0:
                    kb.tt("dve", macc[:], px[:], gtt[:, 0, :], ALU.mult, [r_px, r_gtt], [r_macc])
                else:
                    kb.tt("dve", mtmp[:], px[:], gtt[:, br, :], ALU.mult, [r_px, r_gtt], [r_mtmp])
                    if br == 1:
                        kb.tt("dve", macc[:], macc[:], mtmp[:], ALU.add, [r_macc, r_mtmp], [r_macc])
                    else:
                        kb.tt("dve", mT[:, fc, :], macc[:], mtmp[:], ALU.add, [r_macc, r_mtmp], [r_mT])
        for tt_ in range(4):
            t = g * 4 + tt_
            b = t % 2
            kb.dma("sp", xt[b][0][:], x[t * 128:(t + 1) * 128, :], writes=[xt[b][1]])
            for hc in range(2):
                px, r_px = ps_x[nxt("x", 4)]
                for fc in range(8):
                    kb.mm(px[:], mT[:, fc, tt_ * 128:(tt_ + 1) * 128], wo[:, fc, hc * 512:(hc + 1) * 512], fc == 0, fc == 7,
                          [r_mT, r_wo], [r_px])
                kb.tt("dve", ob[b][0][:, hc * 512:(hc + 1) * 512], px[:], xt[b][0][:, hc * 512:(hc + 1) * 512], ALU.add,
                      [r_px, xt[b][1]], [ob[b][1]])
            kb.dma("sp", xmid[t * 128:(t + 1) * 128, :], ob[b][0][:], reads=[ob[b][1]])
    return [ob[0][1], ob[1][1]]


GROUPS = [[j, 7 - j, 8 + j, 15 - j] for j in range(4)]
S = 8192
BF = ml_dtypes.bfloat16


def core_pos(j):
    return np.concatenate([np.arange(g * 512, (g + 1) * 512) for g in GROUPS[j]])


def rope_tabs(pos, dim):
    inv = (1.0 / (np.float32(10000.0) ** (np.arange(0, dim, 2, dtype=np.float32) / np.float32(dim)))).astype(np.float32)
    ang = pos.astype(np.float32)[:, None] * inv[None, :]
    c = np.cos(ang).astype(np.float32)
    s = np.sin(ang).astype(np.float32)
    ct = np.concatenate([c, c], axis=1)
    st = np.concatenate([-s, s], axis=1)
    return np.ascontiguousarray(ct), np.ascontiguousarray(st)


def rep128(v):
    return np.ascontiguousarray(np.broadcast_to(np.asarray(v, np.float32)[None, :], (128, v.shape[-1])))


def p_inputs(x_core, l, j, inp):
    pos = core_pos(j)
    ct64, st64 = rope_tabs(pos, 64)
    ct32, st32 = rope_tabs(pos, 32)
    gall = np.ones((2048,), np.float32)
    gall[0:256] = np.tile(inp["qn_a"][l], 4)
    gall[256:512] = np.tile(inp["kn_a"][l], 4)
    gall[768:1280] = np.tile(inp["qn_b"][l], 8)
    gall[1280:1408] = np.tile(inp["kn_b"][l], 2)
    gall[1536:1792] = np.tile(inp["qn_c"][l], 8)
    gall[1792:2048] = np.tile(inp["kn_c"][l], 8)
    return {"x": np.ascontiguousarray(x_core), "anorm": rep128(inp["attn_norm"][l]),
            "w_in": np.ascontiguousarray(inp["w_in"][l]), "gall": rep128(gall),
            "ct64": ct64, "st64": st64, "ct32": ct32, "st32": st32}


def f_inputs(xm_core, xm_full_b, l, j, inp):
    halo = np.zeros((8, 1024), np.float32)
    for s, g in enumerate(GROUPS[j]):
        if g > 0:
            halo[2 * s:2 * s + 2] = xm_full_b[g * 512 - 2:g * 512]
    cw = inp["conv_w"][l]
    cb = inp["conv_b"][l]
    convp = np.zeros((128, 44, 4), np.float32)
    convp[:, :, 0:3] = cw.T.reshape(44, 128, 3).transpose(1, 0, 2)
    convp[:, :, 3] = cb.reshape(44, 128).T
    return {"xm": np.ascontiguousarray(xm_core), "xhalo": halo, "mnorm": rep128(inp["mlp_norm"][l]),
            "w_up": np.ascontiguousarray(inp["w_up"][l]), "convp": convp,
            "w_down": np.ascontiguousarray(inp["w_down"][l])}


NEGV = -30000.0


def a_consts(j):
    gl = GROUPS[j]
    f32 = np.float32
    mdiag = np.zeros((128, 4, 512), f32)
    p = np.arange(128)[:, None]
    f = np.arange(512)[None, :]
    for t in range(4):
        mdiag[:, t, :] = np.where(t * 128 + p > f, NEGV, 0.0)
    mB = np.zeros((128, 2, 512), f32)
    fi = (np.arange(512) % 128)[None, :]
    mB[:, 0, :] = np.where(p <= fi, NEGV, 0.0)
    mB[:, 1, :] = np.where(p > fi, NEGV, 0.0)
    pmA = np.zeros((128, 16, 4, 32), f32)
    n = np.arange(32)
    for c in range(16):
        g = gl[c // 4]
        own = 2 * g + (c % 4) // 2
        pmA[:, c, 0, :] = np.where(n < own, 0.0, -1e30)[None, :]
        pmA[:, c, 1, :] = (n < 2 * g).astype(f32)[None, :]
        pmA[:, c, 2, :] = ((n >= 2 * g) & (n < own)).astype(f32)[None, :]
        pmA[:, c, 3, :] = (n == own).astype(f32)[None, :]
    keys = np.arange(S)
    ohA = (keys[None, :] // 256 == np.arange(32)[:, None]).astype(f32)
    ohC = (keys[None, :] // 512 == np.arange(16)[:, None]).astype(f32)
    pos = core_pos(j)
    ohA_own = (pos[None, :] // 256 == np.arange(32)[:, None]).astype(f32)
    cbC = np.where(np.arange(16)[:, None] < (pos[None, :] // 512), 0.0, NEGV).astype(f32)
    hval = np.zeros((128, 4), f32)
    for s, g in enumerate(gl):
        hval[:, s] = 1.0 if g > 0 else 0.0
    return {"mdiag": mdiag.astype(BF), "mB": mB.astype(BF), "pmA": pmA, "ohA": ohA.astype(BF), "ohC": ohC.astype(BF),
            "ohA_own": ohA_own.astype(BF), "cbC": cbC.astype(BF), "hval": hval}


def gather_kv(qkT_list, v_list):
    kT = np.zeros((640, S), BF)
    vf = np.zeros((S, 640), BF)
    for j in range(4):
        pos = core_pos(j)
        q = qkT_list[j]
        kT[0:256, pos] = q[256:512]
        kT[256:384, pos] = q[1024:1152]
        kT[384:640, pos] = q[1408:1664]
        vf[pos] = v_list[j]
    return kT, vf


def a_inputs(j, l, qkT_own, v_own, kT_full, v_full, gT_own, x_core, inp):
    d = dict(a_consts(j))
    d["qkT"] = np.ascontiguousarray(qkT_own)
    d["v_own"] = np.ascontiguousarray(v_own)
    d["kT_full"] = np.ascontiguousarray(kT_full)
    d["v_hp"] = np.ascontiguousarray(v_full.reshape(64, 128, 10, 64).transpose(2, 1, 0, 3))
    kh = np.zeros((128, 512), BF)
    vh = np.zeros((512, 128), BF)
    for s, g in enumerate(GROUPS[j]):
        if g > 0:
            kh[:, s * 128:(s + 1) * 128] = kT_full[256:384, g * 512 - 128:g * 512]
            vh[s * 128:(s + 1) * 128, :] = v_full[g * 512 - 128:g * 512, 256:384]
    d["kTb_halo"] = kh
    d["vb_halo"] = vh
    d["gT"] = np.ascontiguousarray(gT_own)
    d["x"] = np.ascontiguousarray(x_core)
    lamv = np.stack([inp["lam_q1"][l], inp["lam_k1"][l], inp["lam_q2"][l], inp["lam_k2"][l]]).astype(np.float32)
    d["lamv"] = np.ascontiguousarray(np.broadcast_to(lamv[None], (128, 4, 32)))
    d["sgc"] = np.ascontiguousarray(np.tile(inp["subln"][l], 2).reshape(128, 1).astype(np.float32))
    d["sinks"] = rep128(inp["sinks"][l])
    for nm in ("w_pa", "w_pb", "w_pc", "w_out"):
        d[nm] = np.ascontiguousarray(inp[nm][l])
    return d


A_IN_SPECS = [("qkT", [1664, 2048], "bf"), ("v_own", [2048, 640], "bf"), ("kT_full", [640, 8192], "bf"),
              ("v_hp", [10, 128, 64, 64], "bf"), ("kTb_halo", [128, 512], "bf"), ("vb_halo", [512, 128], "bf"),
              ("gT", [3072, 2048], "bf"), ("x", [2048, 1024], "f"), ("mdiag", [128, 4, 512], "bf"), ("mB", [128, 2, 512], "bf"),
              ("pmA", [128, 16, 4, 32], "f"), ("ohA", [32, 8192], "bf"), ("ohC", [16, 8192], "bf"), ("ohA_own", [32, 2048], "bf"),
              ("cbC", [16, 2048], "bf"), ("hval", [128, 4], "f"), ("lamv", [128, 4, 32], "f"), ("sgc", [128, 1], "f"),
              ("sinks", [128, 8], "f"), ("w_pa", [256, 1024], "f"), ("w_pb", [512, 1024], "f"), ("w_pc", [256, 1024], "f"),
              ("w_out", [1024, 1024], "f")]


def _make_ident(kb):
    idb = kb.sb("idb", [128, 128], BF16)
    r_id = kb.res("idb")
    idf = kb.sb("idf", [128, 128], F32)
    kb.op("pool", lambda e: e.memset(idf[:], 0.0), writes=[r_id])
    kb.op("pool", lambda e: e.affine_select(out=idf[:], in_=idf[:], pattern=[[-1, 128]], compare_op=ALU.not_equal,
                                            fill=1.0, base=0, channel_multiplier=1), reads=[r_id], writes=[r_id])
    kb.op("pool", lambda e: e.tensor_copy(out=idb[:], in_=idf[:]), reads=[r_id], writes=[r_id])
    return idb, r_id


def _build(kind, lam_init=0.0):
    nc = bass.Bass("TRN2", target_bir_lowering=False)

    def di(n, s, dt=F32):
        return nc.dram_tensor(n, s, dt, kind="ExternalInput").ap()

    def do(n, s, dt):
        return nc.dram_tensor(n, s, dt, kind="ExternalOutput").ap()

    with ExitStack() as st:
        kb = KB(nc, st)
        if kind == "P":
            D = {"x": di("x", [2048, 1024]), "anorm": di("anorm", [128, 1024]), "w_in": di("w_in", [1024, 5376]),
                 "gall": di("gall", [128, 2048]), "ct64": di("ct64", [2048, 64]), "st64": di("st64", [2048, 64]),
                 "ct32": di("ct32", [2048, 32]), "st32": di("st32", [2048, 32]),
                 "qkT": do("qkT", [1664, 2048], BF16), "v": do("v", [2048, 640], BF16), "gT": do("gT", [3072, 2048], BF16)}
            fin = emit_P(kb, D, _make_ident(kb))
        elif kind == "A":
            D = {}
            for n, s, dt in A_IN_SPECS:
                if n in ("gT", "x", "w_pa", "w_pb", "w_pc", "w_out"):
                    continue
                D[n] = di(n, s, BF16 if dt == "bf" else F32)
            D["yT_out"] = do("yT_out", [1024, 2048], BF16)
            fin = emit_A(kb, D, _make_ident(kb), lam_init)
        elif kind == "M":
            D = {"yT_in": di("yT_in", [1024, 2048], BF16), "gT": di("gT", [3072, 2048], BF16), "x": di("x", [2048, 1024]),
                 "w_pa": di("w_pa", [256, 1024]), "w_pb": di("w_pb", [512, 1024]), "w_pc": di("w_pc", [256, 1024]),
                 "w_out": di("w_out", [1024, 1024]), "xmid": do("xmid", [2048, 1024], F32)}
            fin = emit_M(kb, D)
        else:
            D = {"xm": di("xm", [2048, 1024]), "xhalo": di("xhalo", [8, 1024]), "mnorm": di("mnorm", [128, 1024]),
                 "w_up": di("w_up", [1024, 5632]), "convp": di("convp", [128, 44, 4]), "w_down": di("w_down", [2816, 1024]),
                 "xo": do("xo", [2048, 1024], F32)}
            fin = emit_F(kb, D, _make_ident(kb))
        kb.finish(fin)
    return nc


def _run(nc, in_maps):
    res = run_bass_kernel_spmd(nc, in_maps, core_ids=list(range(8)))
    return res.results


def kernel(**inp):
    inp = {k: np.asarray(v) for k, v in inp.items()}
    x = inp["x"].astype(np.float32)
    cores = [(c // 4, c % 4) for c in range(8)]
    xs = [np.ascontiguousarray(x[b][core_pos(j)]) for b, j in cores]
    for l in range(2):
        lam_init = 0.8 - 0.6 * float(np.exp(-0.3 * l))
        rp = _run(_build("P"), [p_inputs(xs[c], l, cores[c][1], inp) for c in range(8)])
        full = {}
        for b in range(2):
            full[b] = gather_kv([rp[4 * b + j]["qkT"] for j in range(4)], [rp[4 * b + j]["v"] for j in range(4)])
        a_maps = []
        for c, (b, j) in enumerate(cores):
            d = a_inputs(j, l, rp[c]["qkT"], rp[c]["v"], full[b][0], full[b][1], rp[c]["gT"], xs[c], inp)
            for k in ("gT", "x", "w_pa", "w_pb", "w_pc", "w_out"):
                d.pop(k)
            a_maps.append(d)
        ra = _run(_build("A", lam_init), a_maps)
        m_maps = []
        for c in range(8):
            d = {"yT_in": ra[c]["yT_out"], "gT": rp[c]["gT"], "x": xs[c]}
            for nm in ("w_pa", "w_pb", "w_pc", "w_out"):
                d[nm] = np.ascontiguousarray(inp[nm][l])
            m_maps.append(d)
        rm = _run(_build("M"), m_maps)
        xm_full = np.zeros((2, S, 1024), np.float32)
        for c, (b, j) in enumerate(cores):
            xm_full[b][core_pos(j)] = rm[c]["xmid"]
        rf = _run(_build("F"), [f_inputs(rm[c]["xmid"], xm_full[cores[c][0]], l, cores[c][1], inp) for c in range(8)])
        xs = [np.ascontiguousarray(rf[c]["xo"]) for c in range(8)]
    out = np.zeros((2, S, 1024), np.float32)
    for c, (b, j) in enumerate(cores):
        out[b][core_pos(j)] = xs[c]
    return out
```

```python
import numpy as np
import ml_dtypes
from contextlib import ExitStack
import concourse.bass as bass
import concourse.mybir as mybir
from concourse.bass_utils import run_bass_kernel_spmd


F32 = mybir.dt.float32
BF16 = mybir.dt.bfloat16
AF = mybir.ActivationFunctionType
ALU = mybir.AluOpType
AX = mybir.AxisListType


class Ev:
    __slots__ = ("sem", "val")

    def __init__(self, sem, val):
        self.sem = sem
        self.val = val


class Res:
    def __init__(self, name):
        self.name = name
        self.w = None
        self.r = []
        self.dsem = None
        self.dcnt = 0


class KB:
    ENGS = ("pe", "dve", "act", "pool", "sp")

    def __init__(self, nc, stack):
        self.nc = nc
        self.stack = stack
        self.eng = {"pe": nc.tensor, "dve": nc.vector, "act": nc.scalar,
                    "pool": nc.gpsimd, "sp": nc.sync}
        self.sem = {e: stack.enter_context(nc.semaphore("s_" + e)) for e in self.ENGS}
        self.cnt = {e: 0 for e in self.ENGS}
        self.seen = {e: {} for e in self.ENGS}
        self.stream = {e: [] for e in self.ENGS}
        self.nsem = len(self.ENGS)
        self.allres = []

    def res(self, name):
        r = Res(name)
        self.allres.append(r)
        return r

    def sb(self, name, shape, dt):
        t = self.stack.enter_context(self.nc.sbuf_tensor(name, list(shape), dt))
        return t

    def ps(self, name, shape, dt):
        return self.stack.enter_context(self.nc.psum_tensor(name, list(shape), dt))

    def _waits(self, e, reads, writes, nosame):
        need = {}

        def add(ev, r):
            if ev is None:
                return
            if r in nosame and ev.sem is self.sem[e]:
                return
            k = id(ev.sem)
            if k not in need or need[k][1] < ev.val:
                need[k] = (ev.sem, ev.val)

        for r in reads:
            add(r.w, r)
        for w in writes:
            add(w.w, w)
            for ev in w.r:
                add(ev, w)
        out = []
        seen = self.seen[e]
        for k, (s, v) in need.items():
            if seen.get(k, 0) >= v:
                continue
            seen[k] = v
            out.append((s, v))
        return out

    def op(self, e, fn, reads=(), writes=(), signal=True, nosame=()):
        waits = self._waits(e, reads, writes, nosame)
        if signal:
            self.cnt[e] += 1
            ev = Ev(self.sem[e], self.cnt[e])
            sig = (self.sem[e], 1)
        else:
            ev = Ev(self.sem[e], self.cnt[e] + 1)
            sig = None
        self.stream[e].append((waits, fn, sig))
        for r in reads:
            r.r.append(ev)
        for w in writes:
            w.w = ev
            w.r = []
        return ev

    def dma(self, q, out_ap, in_ap, reads=(), writes=(), **kw):
        tr = (list(writes) + list(reads))[0]
        if tr.dsem is None:
            tr.dsem = self.stack.enter_context(self.nc.semaphore("d_" + tr.name))
            self.nsem += 1
        waits = [w for w in self._waits(q, reads, writes, ()) if w[0] is not tr.dsem]
        tr.dcnt += 16
        ev = Ev(tr.dsem, tr.dcnt)
        self.stream[q].append(
            (waits, lambda en: en.dma_start(out=out_ap, in_=in_ap, **kw), (tr.dsem, 16)))
        for r in reads:
            r.r.append(ev)
        for w in writes:
            w.w = ev
            w.r = []
        return ev

    def finish(self, final_res):
        waits = self._waits("sp", final_res, final_res, ())
        self.stream["sp"].append((waits, None, None))
        nc = self.nc
        with nc.Block() as block:
            def mk(e):
                def body(en):
                    for waits, fn, sig in self.stream[e]:
                        for s, v in waits:
                            en.wait_ge(s, v)
                        if fn is None:
                            continue
                        ins = fn(en)
                        if sig is not None:
                            ins.then_inc(sig[0], sig[1])
                return body
            block.tensor(mk("pe"))
            block.vector(mk("dve"))
            block.scalar(mk("act"))
            block.gpsimd(mk("pool"))
            block.sync(mk("sp"))


def _mk(KBc):
    def tt(self, e, out, in0, in1, op, reads, writes):
        return self.op(e, lambda en: en.tensor_tensor(out=out, in0=in0, in1=in1, op=op), reads, writes)

    def ts(self, e, out, in0, s1, s2, op0, op1=None, reads=(), writes=()):
        if op1 is None:
            return self.op(e, lambda en: en.tensor_scalar(out=out, in0=in0, scalar1=s1, scalar2=None, op0=op0), reads, writes)
        return self.op(e, lambda en: en.tensor_scalar(out=out, in0=in0, scalar1=s1, scalar2=s2, op0=op0, op1=op1), reads, writes)

    def stt(self, out, in0, scalar, in1, op0, op1, reads, writes):
        return self.op("dve", lambda en: en.scalar_tensor_tensor(out=out, in0=in0, scalar=scalar, in1=in1, op0=op0, op1=op1), reads, writes)

    def cp(self, e, out, in_, reads, writes):
        if e == "act":
            return self.op(e, lambda en: en.activation(out=out, in_=in_, func=AF.Copy), reads, writes)
        return self.op(e, lambda en: en.tensor_copy(out=out, in_=in_), reads, writes)

    def act(self, out, in_, func, reads, writes, bias=None, scale=None, accum_out=None):
        kw = {}
        if bias is not None:
            kw["bias"] = bias
        if scale is not None:
            kw["scale"] = scale
        if accum_out is not None:
            kw["accum_out"] = accum_out
        return self.op("act", lambda en: en.activation(out=out, in_=in_, func=func, **kw), reads, writes)

    def mm(self, out, lhsT, rhs, start, stop, reads, writes, signal=None):
        if signal is None:
            signal = stop
        return self.op("pe", lambda en: en.matmul(out, lhsT=lhsT, rhs=rhs, start=start, stop=stop),
                       reads, writes, signal=signal, nosame=writes)

    def tr(self, out, in_, ident, reads, writes, signal=True):
        return self.op("pe", lambda en: en.transpose(out=out, in_=in_, identity=ident), reads, writes,
                       signal=signal, nosame=writes)

    def red(self, out, in_, reads, writes, op=None):
        op = op or ALU.add
        return self.op("dve", lambda en: en.tensor_reduce(out=out, in_=in_, axis=AX.X, op=op), reads, writes)

    def ms(self, e, ap, val, writes):
        return self.op(e, lambda en: en.memset(ap, val), (), writes)

    for f in (tt, ts, stt, cp, act, mm, tr, red, ms):
        setattr(KBc, f.__name__, f)


_mk(KB)


def _mk2(KBc):
    def coll(self, kind, in_ap, out_ap, groups, reads=(), writes=(), inc=1):
        if not hasattr(self, "cc_sem"):
            self.cc_sem = self.stack.enter_context(self.nc.semaphore("cc_sem"))
            self.cc_cnt = 0
        waits = self._waits("pool", reads, writes, ())
        self.cc_cnt += inc
        ev = Ev(self.cc_sem, self.cc_cnt)
        op = ALU.bypass
        self.stream["pool"].append(
            (waits, lambda en: en.collective_compute(kind, op, replica_groups=groups, ins=[in_ap], outs=[out_ap]),
             (self.cc_sem, inc)))
        for r in reads:
            r.r.append(ev)
        for w in writes:
            w.w = ev
            w.r = []
        return ev

    def barrier(self):
        evs = [(self.sem[e], self.cnt[e]) for e in self.ENGS if self.cnt[e] > 0]
        for r in self.allres:
            if r.dsem is not None and r.dcnt > 0:
                evs.append((r.dsem, r.dcnt))
        if hasattr(self, "cc_sem") and self.cc_cnt > 0:
            evs.append((self.cc_sem, self.cc_cnt))
        for e in self.ENGS:
            seen = self.seen[e]
            waits = []
            for s, v in evs:
                if s is self.sem[e]:
                    continue
                if seen.get(id(s), 0) >= v:
                    continue
                seen[id(s)] = v
                waits.append((s, v))
            self.stream[e].append((waits, None, None))

    KBc.coll = coll
    KBc.barrier = barrier


_mk2(KB)


NEG = -30000.0
EPS = 1e-6
GROUPS = [[j, 7 - j, 8 + j, 15 - j] for j in range(4)]
NT = 16
SEGS = [(0, 8, 64), (768, 10, 64), (1536, 16, 32)]
TBLK = [0, 128, 256, 384, 768, 896, 1024, 1152, 1280, 1536, 1664, 1792, 1920]
VSEG = [(512, 256), (1408, 128), (2048, 256)]


def emit_P(kb, D, ident):
    nc = kb.nc
    x, anorm, w_in, gall, ct64, st64, ct32, st32 = (D[k] for k in ("x", "anorm", "w_in", "gall", "ct64", "st64", "ct32", "st32"))
    qkT, vout, gT = D["qkT"], D["v"], D["gT"]
    idb, r_id = ident

    wq = kb.sb("wq", [128, 8, 2304], BF16); r_wq = kb.res("wq")
    wg = [kb.sb(f"wg{i}", [128, 8, 512], BF16) for i in range(2)]; r_wg = [kb.res(f"wg{i}") for i in range(2)]
    hT = kb.sb("hT", [128, 8, 2048], BF16); r_hT = [kb.res(f"hT{t}") for t in range(NT)]
    xt = [kb.sb(f"xt{i}", [128, 1024], F32) for i in range(2)]; r_xt = [kb.res(f"xt{i}") for i in range(2)]
    h16 = [kb.sb(f"h16{i}", [128, 1024], BF16) for i in range(2)]; r_h16 = [kb.res(f"h16{i}") for i in range(2)]
    pj = [kb.sb(f"pj{i}", [128, 2304], F32) for i in range(2)]; r_pj = [kb.res(f"pj{i}") for i in range(2)]
    xc = kb.sb("xc", [128, 2048], F32); r_xc = kb.res("xc")
    xs = kb.sb("xs", [128, 2048], F32); r_xs = kb.res("xs")
    qk16 = [kb.sb(f"qk16{i}", [128, 2048], BF16) for i in range(2)]; r_qk16 = [kb.res(f"qk16{i}") for i in range(2)]
    qkTs = [kb.sb(f"qkTs{i}", [128, 13, 128], BF16) for i in range(2)]; r_qkTs = [kb.res(f"qkTs{i}") for i in range(2)]
    v16 = [kb.sb(f"v16{i}", [128, 640], BF16) for i in range(2)]; r_v16 = [kb.res(f"v16{i}") for i in range(2)]
    g16 = [kb.sb(f"g16{i}", [128, 512], BF16) for i in range(2)]; r_g16 = [kb.res(f"g16{i}") for i in range(2)]
    an = kb.sb("an", [128, 1024], F32); r_an = kb.res("an")
    ga = kb.sb("ga", [128, 2048], F32); r_ga = kb.res("ga")
    c64 = kb.sb("c64", [128, NT, 64], F32); s64 = kb.sb("s64", [128, NT, 64], F32)
    c32 = kb.sb("c32", [128, NT, 32], F32); s32 = kb.sb("s32", [128, NT, 32], F32)
    r_tab = kb.res("tabs")
    epst = kb.sb("epst", [128, 1], F32); r_eps = kb.res("eps")
    st = [kb.sb(f"st{i}", [128, 40], F32) for i in range(2)]; r_st = [kb.res(f"st{i}") for i in range(2)]
    sx = [kb.sb(f"sx{i}", [128, 4], F32) for i in range(2)]; r_sx = [kb.res(f"sx{i}") for i in range(2)]
    ps_t = [kb.ps(f"ps_t{i}", [128, 1024], BF16) for i in range(2)]; r_ps_t = [kb.res(f"ps_t{i}") for i in range(2)]
    ps_m = [kb.ps(f"ps_m{i}", [128, 512], F32) for i in range(4)]; r_ps_m = [kb.res(f"ps_m{i}") for i in range(4)]
    ps_q = [kb.ps(f"ps_q{i}", [128, 1024], BF16) for i in range(2)]; r_ps_q = [kb.res(f"ps_q{i}") for i in range(2)]

    kb.ms("pool", epst[:], EPS, [r_eps])
    kb.dma("sp", an[:], anorm[:, :], writes=[r_an])
    kb.dma("sp", ga[:], gall[:, :], writes=[r_ga])
    kb.dma("sp", c64[:], ct64.rearrange("(t p) d -> p t d", p=128), writes=[r_tab])
    kb.dma("sp", s64[:], st64.rearrange("(t p) d -> p t d", p=128), writes=[r_tab])
    kb.dma("sp", c32[:], ct32.rearrange("(t p) d -> p t d", p=128), writes=[r_tab])
    kb.dma("sp", s32[:], st32.rearrange("(t p) d -> p t d", p=128), writes=[r_tab])
    wv = w_in.rearrange("(k p) c -> p k c", p=128)
    for k in range(8):
        for c0 in range(0, 2304, 1152):
            kb.dma("pool", wq[:, k, c0:c0 + 1152], wv[:, k, c0:c0 + 1152], writes=[r_wq])

    for t in range(NT):
        b = t % 2
        kb.dma("sp", xt[b][:], x[t * 128:(t + 1) * 128, :], writes=[r_xt[b]])
        kb.act(h16[b][:], xt[b][:], AF.Square, [r_xt[b]], [r_h16[b], r_sx[b]], accum_out=sx[b][:, 0:1])
        kb.act(sx[b][:, 1:2], sx[b][:, 0:1], AF.Ln, [r_sx[b], r_eps], [r_sx[b]], bias=epst[:], scale=1.0 / 1024)
        kb.act(sx[b][:, 2:3], sx[b][:, 1:2], AF.Exp, [r_sx[b]], [r_sx[b]], scale=-0.5)
        kb.stt(h16[b][:], xt[b][:], sx[b][:, 2:3], an[:], ALU.mult, ALU.mult, [r_xt[b], r_sx[b], r_an], [r_h16[b]])
        for k in range(8):
            kb.tr(ps_t[b][:, k * 128:(k + 1) * 128], h16[b][:, k * 128:(k + 1) * 128], idb[:],
                  [r_h16[b], r_id], [r_ps_t[b]], signal=(k == 7))
        kb.cp("act", hT[:, :, t * 128:(t + 1) * 128], ps_t[b][:].rearrange("p (k t) -> p k t", k=8), [r_ps_t[b]], [r_hT[t]])

    mcnt = 0
    for t in range(NT):
        b = t % 2
        for ci, (c0, cw) in enumerate([(0, 512), (512, 512), (1024, 512), (1536, 512), (2048, 256)]):
            pm = mcnt % 4; mcnt += 1
            for k in range(8):
                kb.mm(ps_m[pm][:, 0:cw], hT[:, k, t * 128:(t + 1) * 128], wq[:, k, c0:c0 + cw], k == 0, k == 7,
                      [r_hT[t], r_wq], [r_ps_m[pm]])
            kb.cp("act", pj[b][:, c0:c0 + cw], ps_m[pm][:, 0:cw], [r_ps_m[pm]], [r_pj[b]])
        vo = 0
        for (c0, cw) in VSEG:
            kb.cp("pool", v16[b][:, vo:vo + cw], pj[b][:, c0:c0 + cw], [r_pj[b]], [r_v16[b]])
            vo += cw
        kb.dma("sp", vout[t * 128:(t + 1) * 128, :], v16[b][:], reads=[r_v16[b]])
        so = 0
        for (c0, nh, d) in SEGS:
            kb.act(xs[:, c0:c0 + nh * d], pj[b][:, c0:c0 + nh * d], AF.Square, [r_pj[b]], [r_xs])
            kb.red(st[b][:, so:so + nh], xs[:, c0:c0 + nh * d].rearrange("p (h d) -> p h d", d=d), [r_xs], [r_st[b]])
            so += nh
        kb.act(st[b][:, 0:18], st[b][:, 0:18], AF.Ln, [r_st[b], r_eps], [r_st[b]], bias=epst[:], scale=1.0 / 64)
        kb.act(st[b][:, 18:34], st[b][:, 18:34], AF.Ln, [r_st[b], r_eps], [r_st[b]], bias=epst[:], scale=1.0 / 32)
        kb.act(st[b][:, 0:34], st[b][:, 0:34], AF.Exp, [r_st[b]], [r_st[b]], scale=-0.5)
        so = 0
        for si, (c0, nh, d) in enumerate(SEGS):
            w = nh * d
            e1 = "dve" if si != 1 else "pool"
            pv = pj[b][:, c0:c0 + w].rearrange("p (h d) -> p h d", d=d)
            rb = st[b][:, so:so + nh].rearrange("p (h o) -> p h o", o=1).to_broadcast([128, nh, d])
            kb.tt(e1, pv, pv, rb, ALU.mult, [r_pj[b], r_st[b]], [r_pj[b]])
            kb.tt(e1, pj[b][:, c0:c0 + w], pj[b][:, c0:c0 + w], ga[:, c0:c0 + w], ALU.mult, [r_pj[b], r_ga], [r_pj[b]])
            hd = d // 2
            ctab = (c64 if d == 64 else c32)[:, t, :]
            stab = (s64 if d == 64 else s32)[:, t, :]
            cb = ctab.rearrange("p (o d) -> p o d", o=1).to_broadcast([128, nh, d])
            kb.tt(e1, xc[:, c0:c0 + w].rearrange("p (h d) -> p h d", d=d), pv, cb, ALU.mult, [r_pj[b], r_tab], [r_xc])
            p4 = pj[b][:, c0:c0 + w].rearrange("p (h two e) -> p h two e", two=2, e=hd)
            x4 = xs[:, c0:c0 + w].rearrange("p (h two e) -> p h two e", two=2, e=hd)
            s0 = stab[:, 0:hd].rearrange("p (o d) -> p o d", o=1).to_broadcast([128, nh, hd])
            s1 = stab[:, hd:d].rearrange("p (o d) -> p o d", o=1).to_broadcast([128, nh, hd])
            kb.tt(e1, x4[:, :, 0, :], p4[:, :, 1, :], s0, ALU.mult, [r_pj[b], r_tab], [r_xs])
            kb.tt(e1, x4[:, :, 1, :], p4[:, :, 0, :], s1, ALU.mult, [r_pj[b], r_tab], [r_xs])
            kb.tt(e1, qk16[b][:, c0:c0 + w], xc[:, c0:c0 + w], xs[:, c0:c0 + w], ALU.add, [r_xc, r_xs], [r_qk16[b]])
            so += nh
        for bi, c0 in enumerate(TBLK):
            half = 0 if bi < 8 else 1
            col = (bi % 8) * 128
            kb.tr(ps_q[half][:, col:col + 128], qk16[b][:, c0:c0 + 128], idb[:], [r_qk16[b], r_id], [r_ps_q[half]],
                  signal=(bi == 7 or bi == 12))
        kb.cp("dve", qkTs[b][:, 0:8, :], ps_q[0][:].rearrange("p (k t) -> p k t", k=8), [r_ps_q[0]], [r_qkTs[b]])
        kb.cp("dve", qkTs[b][:, 8:13, :], ps_q[1][:, 0:640].rearrange("p (k t) -> p k t", k=5), [r_ps_q[1]], [r_qkTs[b]])
        kb.dma("sp", qkT[:, t * 128:(t + 1) * 128].rearrange("(k p) t -> p k t", p=128), qkTs[b][:], reads=[r_qkTs[b]])

    gcnt = 0
    for wc in range(6):
        wb = wc % 2
        for k in range(8):
            kb.dma("pool", wg[wb][:, k, :], wv[:, k, 2304 + wc * 512:2304 + (wc + 1) * 512], writes=[r_wg[wb]])
        for cc in range(4):
            for g in range(4):
                pm = mcnt % 4; mcnt += 1
                for k in range(8):
                    kb.mm(ps_m[pm][:], wg[wb][:, k, cc * 128:(cc + 1) * 128], hT[:, k, g * 512:(g + 1) * 512],
                          k == 0, k == 7, [r_wg[wb]] + r_hT[4 * g:4 * g + 4], [r_ps_m[pm]])
                gb = gcnt % 2; gcnt += 1
                kb.act(g16[gb][:], ps_m[pm][:], AF.Sigmoid, [r_ps_m[pm]], [r_g16[gb]])
                row = (wc * 4 + cc) * 128
                kb.dma("sp", gT[row:row + 128, g * 512:(g + 1) * 512], g16[gb][:], reads=[r_g16[gb]])
    return [r_v16[0], r_v16[1], r_qkTs[0], r_qkTs[1], r_g16[0], r_g16[1]]


EPS = 1e-6
NT = 16
NC2 = 22


def emit_F(kb, D, ident, pfx="f"):
    xm, xhalo, mnorm, w_up, convp, w_down, xo = (D[k] for k in ("xm", "xhalo", "mnorm", "w_up", "convp", "w_down", "xo"))
    idb, r_id = ident
    P = pfx
    hT = kb.sb(P + "hT", [128, 8, 2048], BF16); r_hT = [kb.res(P + f"hT{t}") for t in range(NT)]
    hTh = kb.sb(P + "hTh", [128, 8, 8], BF16); r_hTh = kb.res(P + "hTh")
    mT = kb.sb(P + "mT", [128, NC2, 1024], BF16); r_mT = [kb.res(P + f"mT{g}") for g in range(2)]
    wd = kb.sb(P + "wd", [128, NC2, 1024], BF16); r_wd = kb.res(P + "wd")
    wu = [kb.sb(P + f"wu{i}", [128, 8, 2, 128], BF16) for i in range(2)]; r_wu = [kb.res(P + f"wu{i}") for i in range(2)]
    xt = [kb.sb(P + f"xt{i}", [128, 1024], F32) for i in range(2)]; r_xt = [kb.res(P + f"xt{i}") for i in range(2)]
    h16 = [kb.sb(P + f"h16{i}", [128, 1024], BF16) for i in range(2)]; r_h16 = [kb.res(P + f"h16{i}") for i in range(2)]
    an = kb.sb(P + "an", [128, 1024], F32); r_an = kb.res(P + "an")
    cpar = kb.sb(P + "cpar", [128, 44, 4], F32); r_cp = kb.res(P + "cpar")
    epst = kb.sb(P + "epst", [128, 1], F32); r_eps = kb.res(P + "eps")
    sx = [kb.sb(P + f"sx{i}", [128, 4], F32) for i in range(2)]; r_sx = [kb.res(P + f"sx{i}") for i in range(2)]
    ub = [[kb.sb(P + f"ub{i}{s}", [128, 514], F32) for s in range(2)] for i in range(2)]
    r_ub = [[kb.res(P + f"ub{i}{s}") for s in range(2)] for i in range(2)]
    yb = [[kb.sb(P + f"yb{i}{s}", [128, 512], F32) for s in range(2)] for i in range(2)]
    r_yb = [[kb.res(P + f"yb{i}{s}") for s in range(2)] for i in range(2)]
    ob = [kb.sb(P + f"ob{i}", [128, 1024], F32) for i in range(2)]; r_ob = [kb.res(P + f"ob{i}") for i in range(2)]
    ps_t = kb.ps(P + "ps_t", [128, 1024], BF16); r_ps_t = kb.res(P + "ps_t")
    pu = [[kb.ps(P + f"pu{i}{s}", [128, 512], F32) for s in range(2)] for i in range(2)]
    r_pu = [[kb.res(P + f"pu{i}{s}") for s in range(2)] for i in range(2)]
    ph = kb.ps(P + "ph", [128, 16], F32); r_ph = kb.res(P + "ph")
    po = [kb.ps(P + f"po{i}", [128, 512], F32) for i in range(2)]; r_po = [kb.res(P + f"po{i}") for i in range(2)]

    kb.ms("pool", epst[:], EPS, [r_eps])
    kb.dma("sp", an[:], mnorm[:, :], writes=[r_an])
    kb.dma("sp", cpar[:], convp[:, :, :], writes=[r_cp])
    wdv = w_down.rearrange("(c p) n -> p c n", p=128)
    for c in range(NC2):
        kb.dma("pool", wd[:, c, :], wdv[:, c, :], writes=[r_wd])

    for t in range(NT + 1):
        b = t % 2
        n = 128 if t < NT else 8
        src = xm[t * 128:(t + 1) * 128, :] if t < NT else xhalo[:, :]
        kb.dma("sp", xt[b][0:n, :], src, writes=[r_xt[b]])
        kb.act(h16[b][0:n, :], xt[b][0:n, :], AF.Square, [r_xt[b]], [r_h16[b], r_sx[b]], accum_out=sx[b][0:n, 0:1])
        kb.act(sx[b][0:n, 1:2], sx[b][0:n, 0:1], AF.Ln, [r_sx[b], r_eps], [r_sx[b]], bias=epst[0:n, :], scale=1.0 / 1024)
        kb.act(sx[b][0:n, 2:3], sx[b][0:n, 1:2], AF.Exp, [r_sx[b]], [r_sx[b]], scale=-0.5)
        kb.stt(h16[b][0:n, :], xt[b][0:n, :], sx[b][0:n, 2:3], an[0:n, :], ALU.mult, ALU.mult, [r_xt[b], r_sx[b], r_an], [r_h16[b]])
        for k in range(8):
            kb.tr(ps_t[:, k * 128:k * 128 + n], h16[b][0:n, k * 128:(k + 1) * 128], idb[0:n, 0:n],
                  [r_h16[b], r_id], [r_ps_t], signal=(k == 7))
        if t < NT:
            kb.cp("act", hT[:, :, t * 128:(t + 1) * 128], ps_t[:].rearrange("p (k t) -> p k t", k=8), [r_ps_t], [r_hT[t]])
        else:
            kb.cp("act", hTh[:], ps_t[:].rearrange("p (k t) -> p k t", k=8)[:, :, 0:8], [r_ps_t], [r_hTh])

    wuv = w_up.rearrange("(k p) c -> p k c", p=128)
    it = 0
    for half in range(2):
        for c in range(NC2):
            wb = it % 2; it += 1
            for s in range(2):
                col = s * 2816 + c * 128
                kb.dma("pool", wu[wb][:, :, s, :], wuv[:, :, col:col + 128], writes=[r_wu[wb]])
            for s in range(2):
                for k in range(8):
                    kb.mm(ph[:, s * 8:(s + 1) * 8], wu[wb][:, k, s, :], hTh[:, k, :], k == 0, k == 7,
                          [r_wu[wb], r_hTh], [r_ph], signal=(s == 1 and k == 7))
            for gi in range(2):
                g = half * 2 + gi
                ib = (c * 2 + gi) % 2
                for s in range(2):
                    for k in range(8):
                        kb.mm(pu[ib][s][:], wu[wb][:, k, s, :], hT[:, k, g * 512:(g + 1) * 512], k == 0, k == 7,
                              [r_wu[wb]] + r_hT[4 * g:4 * g + 4], [r_pu[ib][s]])
                for s in range(2):
                    ci = s * NC2 + c
                    u, ru = ub[ib][s], r_ub[ib][s]
                    y, ry = yb[ib][s], r_yb[ib][s]
                    kb.cp("dve", u[:, 0:2], ph[:, s * 8 + g * 2:s * 8 + g * 2 + 2], [r_ph], [ru])
                    kb.cp("act", u[:, 2:514], pu[ib][s][:], [r_pu[ib][s]], [ru])
                    kb.act(y[:], pu[ib][s][:], AF.Identity, [r_pu[ib][s], r_cp], [ry], bias=cpar[:, ci, 3:4], scale=cpar[:, ci, 2:3])
                    kb.stt(y[:], u[:, 1:513], cpar[:, ci, 1:2], y[:], ALU.mult, ALU.add, [ru, ry, r_cp], [ry])
                    kb.stt(y[:], u[:, 0:512], cpar[:, ci, 0:1], y[:], ALU.mult, ALU.add, [ru, ry, r_cp], [ry])
                yg, yv = yb[ib][0], yb[ib][1]
                kb.act(yg[:], yg[:], AF.Silu, [r_yb[ib][0]], [r_yb[ib][0]])
                kb.tt("dve", mT[:, c, gi * 512:(gi + 1) * 512], yg[:], yv[:], ALU.mult, [r_yb[ib][0], r_yb[ib][1]], [r_mT[gi]])
        for tt_ in range(8):
            t = half * 8 + tt_
            b = t % 2
            kb.dma("sp", xt[b][:], xm[t * 128:(t + 1) * 128, :], writes=[r_xt[b]])
            for hc in range(2):
                for c in range(NC2):
                    kb.mm(po[hc][:], mT[:, c, tt_ * 128:(tt_ + 1) * 128], wd[:, c, hc * 512:(hc + 1) * 512], c == 0, c == NC2 - 1,
                          [r_mT[tt_ // 4], r_wd], [r_po[hc]])
                kb.tt("dve", ob[b][:, hc * 512:(hc + 1) * 512], po[hc][:], xt[b][:, hc * 512:(hc + 1) * 512], ALU.add,
                      [r_po[hc], r_xt[b]], [r_ob[b]])
            kb.dma("sp", xo[t * 128:(t + 1) * 128, :], ob[b][:], reads=[r_ob[b]])
    return [r_ob[0], r_ob[1]]


NEG = -30000.0
BIG = 30000.0
EPS = 1e-6
NSTREAM = [12, 28, 44, 60]
FAST_RECIP = False


def RECIP(en):
    return en.reciprocal_approx_fast if FAST_RECIP else en.reciprocal


STOP = 99


def emit_A(kb, D, ident, lam_init, pfx="a"):
    P = pfx
    idb, r_id = ident
    qkT, vown_d, kTf, vhp, kTbh, vbh_d = (D[k] for k in ("qkT", "v_own", "kT_full", "v_hp", "kTb_halo", "vb_halo"))

    def S(name, shape, dt):
        return kb.sb(P + name, shape, dt), kb.res(P + name)

    kbuf = kb.sb(P + "kbuf", [128, 2, 8192], BF16)
    r_kb2 = [kb.res(P + "kbuf0"), kb.res(P + "kbuf1")]
    vaugs = [S(f"vaug{i}", [128, 64, 128], BF16) for i in range(2)]
    vaug, r_vaug = vaugs[0]
    vstg, r_vstg = S("vstg", [128, 64, 64], BF16)
    kown = kb.sb(P + "kown", [128, 2, 2048], BF16)
    r_ko2 = [kb.res(P + "kown0"), kb.res(P + "kown1")]
    vowns = [S(f"vown{i}", [128, 16, 128], BF16) for i in range(2)]
    vown, r_vown = vowns[0]
    vostg, r_vostg = S("vostg", [128, 16, 64], BF16)
    qt = kb.sb(P + "qt", [128, 2, 2048], BF16)
    r_qt2 = [kb.res(P + "qt0"), kb.res(P + "qt1")]
    yT, r_yT = S("yT", [128, 8, 2048], BF16)
    mdiag, r_md = S("mdiag", [128, 4, 512], BF16)
    mB, r_mB = S("mB", [128, 2, 512], BF16)
    pmA, r_pmA = S("pmA", [128, 16, 4, 32], F32)
    hval, r_hval = S("hval", [128, 4], F32)
    lamv, r_lamv = S("lamv", [128, 4, 32], F32)
    lamt, r_lamt = S("lamt", [128, 8], F32)
    sgc, r_sgc = S("sgc", [128, 1], F32)
    sinkt, r_sinkt = S("sinkt", [1, 8], F32)
    sinke, r_sinke = S("sinke", [1, 8], F32)
    sh16, r_sh = S("sh16", [1, 8], BF16)
    sl16, r_sl = S("sl16", [1, 8], BF16)
    shf, r_shf = S("shf", [1, 8], F32)
    sinkrow, r_sinkrow = S("sinkrow", [1, 2, 8, 128], BF16)
    srow, r_srow = S("srow", [1, 128], BF16)
    ones64, r_ones64 = S("ones64", [64, 64], BF16)
    epst, r_eps = S("epst", [128, 1], F32)
    pt = [S(f"pt{i}", [128, 1024], BF16) for i in range(4)]
    nsb = [S(f"nsb{i}", [64, 512], F32) for i in range(2)]
    dsb = [S(f"dsb{i}", [64, 512], F32) for i in range(2)]
    rcp = [S(f"rcp{i}", [64, 512], F32) for i in range(2)]
    t1, r_t1 = S("t1", [64, 512], F32)
    t2, r_t2 = S("t2", [64, 512], F32)
    sq16, r_sq16 = S("sq16", [64, 512], BF16)
    rs, r_rs = S("rs", [64, 512], F32)
    kmean, r_kmean = S("kmean", [64, 32], F32)
    kmean16, r_km16 = S("kmean16", [64, 32], BF16)
    gm, r_gm = S("gm", [128, 16, 32], F32)
    t8, r_t8 = S("t8", [128, 16, 8], F32)
    sel, r_sel = S("sel", [128, 16, 32], F32)
    bst, r_bst = S("bst", [128, 16, 64], BF16)
    tmpb, r_tmpb = S("tmpb", [128, 16, 32], F32)
    ps_s = [(kb.ps(P + f"ps_s{i}", [128, 1024], F32), kb.res(P + f"ps_s{i}")) for i in range(2)]
    ps_o = [(kb.ps(P + f"ps_o{i}", [128, 512], F32), kb.res(P + f"ps_o{i}")) for i in range(2)]
    ps_x = [(kb.ps(P + f"ps_x{i}", [128, 512], F32), kb.res(P + f"ps_x{i}")) for i in range(1)]
    ps_b, r_ps_b = kb.ps(P + "ps_b", [128, 1024], BF16), kb.res(P + "ps_b")

    kb.ms("pool", epst[:], EPS, [r_eps])
    for i_ in range(2):
        kb.ms("pool", vaugs[i_][0][:, :, 64:128], 1.0, [vaugs[i_][1]])
        kb.ms("pool", vowns[i_][0][:, :, 64:128], 1.0, [vowns[i_][1]])
    kb.ms("pool", ones64[:], 1.0, [r_ones64])
    kb.ms("pool", srow[:, 0:64], 0.0, [r_srow])
    kb.ms("pool", srow[:, 64:128], 1.0, [r_srow])
    kb.dma("sp", mdiag[:], D["mdiag"][:, :, :], writes=[r_md])
    kb.dma("sp", mB[:], D["mB"][:, :, :], writes=[r_mB])
    kb.dma("sp", pmA[:], D["pmA"][:, :, :, :], writes=[r_pmA])
    kb.dma("sp", hval[:], D["hval"][:, :], writes=[r_hval])
    kb.dma("sp", lamv[:], D["lamv"][:, :, :], writes=[r_lamv])
    kb.dma("sp", sgc[:], D["sgc"][:, :], writes=[r_sgc])
    kb.dma("sp", sinkt[:], D["sinks"][0:1, :], writes=[r_sinkt])
    sinkf, r_sinkf = S("sinkf", [128, 8], F32)
    esink, r_esink = S("esink", [128, 8], F32)
    kb.dma("sp", sinkf[:], D["sinks"][:, :], writes=[r_sinkf])
    kb.act(esink[:], sinkf[:], AF.Exp, [r_sinkf], [r_esink])
    kb.tt("dve", lamv[:, 0, :], lamv[:, 0, :], lamv[:, 1, :], ALU.mult, [r_lamv], [r_lamv])
    kb.tt("dve", lamv[:, 2, :], lamv[:, 2, :], lamv[:, 3, :], ALU.mult, [r_lamv], [r_lamv])
    kb.red(lamt[:, 0:1], lamv[:, 0, :], [r_lamv], [r_lamt])
    kb.red(lamt[:, 1:2], lamv[:, 2, :], [r_lamv], [r_lamt])
    kb.act(lamt[:, 2:4], lamt[:, 0:2], AF.Exp, [r_lamt], [r_lamt])
    kb.tt("dve", lamt[:, 4:5], lamt[:, 3:4], lamt[:, 2:3], ALU.subtract, [r_lamt], [r_lamt])
    kb.ts("dve", lamt[:, 4:5], lamt[:, 4:5], -float(lam_init), None, ALU.add, None, [r_lamt], [r_lamt])
    kb.act(sinke[:], sinkt[:], AF.Exp, [r_sinkt], [r_sinke])
    kb.cp("dve", sh16[:], sinke[:], [r_sinke], [r_sh])
    kb.cp("dve", shf[:], sh16[:], [r_sh], [r_shf])
    kb.tt("dve", shf[:], sinke[:], shf[:], ALU.subtract, [r_sinke, r_shf], [r_shf])
    kb.cp("dve", sl16[:], shf[:], [r_shf], [r_sl])
    kb.cp("dve", sinkrow[:, 0, :, :], sh16[:].rearrange("p (h o) -> p h o", o=1).to_broadcast([1, 8, 128]), [r_sh], [r_sinkrow])
    kb.cp("dve", sinkrow[:, 1, :, :], sl16[:].rearrange("p (h o) -> p h o", o=1).to_broadcast([1, 8, 128]), [r_sl], [r_sinkrow])

    def _fin():
        kb.dma("sp", D["yT_out"].rearrange("(k p) t -> p k t", p=128), yT[:], reads=[r_yT])
        return [r_yT]

    if STOP <= 0:
        return []
    cnt = {"s": 0, "p": 0, "o": 0, "x": 0, "r": 0}

    def nxt(k, n):
        v = cnt[k] % n
        cnt[k] += 1
        return v

    class Pipe:
        def __init__(self):
            self.prev = None
            self.cbs = []
            self.group = []

        def _S(self, tiles):
            ps, r_ps = ps_s[nxt("s", 2)]
            for j, t in enumerate(tiles):
                reg = ps[:, j * 512:(j + 1) * 512]
                kl, ql, mask, rds = t["kl"], t["ql"], t["mask"], t["rds"]
                lastt = j == len(tiles) - 1
                if isinstance(ql, list):
                    kb.mm(reg, idb[:], mask[0], True, False, [r_id, mask[1]], [r_ps], signal=False)
                    for gi, qp in enumerate(ql):
                        fin = gi == len(ql) - 1
                        kb.mm(reg[:, gi * 128:(gi + 1) * 128], kl, qp, False, fin, rds, [r_ps], signal=fin and lastt)
                else:
                    kb.mm(reg, kl, ql, True, mask is None, rds, [r_ps], signal=(mask is None) and lastt)
                    if mask is not None:
                        kb.mm(reg, idb[:], mask[0], False, True, [r_id, mask[1]], [r_ps], signal=lastt)
            return ps, r_ps

        def _drain(self):
            if self.prev is not None:
                tiles, (ps, r_ps) = self.prev
                n = len(tiles)
                p, r_p = pt[nxt("p", 4)]
                kb.act(p[:, 0:n * 512], ps[:, 0:n * 512], AF.Exp, [r_ps], [r_p], scale=tiles[0]["scale"])
                for j, t in enumerate(tiles):
                    kb.mm(t["acc"][0][:], t["vl"], p[:, j * 512:(j + 1) * 512], t["first"], t["last"], [r_p] + t["rds"], [t["acc"][1]])
                self.prev = None
            keep = []
            for item in self.cbs:
                if item[0] <= 0:
                    item[1]()
                else:
                    item[0] -= 1
                    keep.append(item)
            self.cbs = keep

        def _emit(self):
            tiles, self.group = self.group, []
            ps = self._S(tiles)
            self._drain()
            self.prev = (tiles, ps)

        def push(self, kl, ql, vl, acc, first, last, scale, rds, mask=None):
            self.group.append(dict(kl=kl, ql=ql, vl=vl, acc=acc, first=first, last=last, scale=scale, rds=rds, mask=mask))
            if len(self.group) == 2:
                self._emit()

        def after(self, cb, delay=0):
            assert not self.group
            self.cbs.append([delay, cb])

        def sync(self):
            if self.group:
                self._emit()
            self._drain()

        def flush(self):
            if self.group:
                self._emit()
            self._drain()
            while self.cbs:
                item = self.cbs.pop(0)
                item[1]()

    def release(acc, k):
        n_, rn_ = nsb[k]
        d_, rd_ = dsb[k]
        kb.cp("dve", n_[:], acc[0][0:64, :], [acc[1]], [rn_])
        kb.cp("dve", d_[:], acc[0][64:128, :], [acc[1]], [rd_])
        return n_, rn_, d_, rd_

    pipe = Pipe()

    def pe_warm(n):
        px, r_px = ps_x[0]
        for q in range(n):
            kb.mm(px[:], idb[:], mdiag[:, 0, :], True, True, [r_id, r_md], [r_px], signal=(q == n - 1))

    def attn_tile(kl, ql, vl, acc, first, last, scale, rds, mask=None):
        pipe.push(kl, ql, vl, acc, first, last, scale, rds, mask)

    for par in range(2):
        kb.ms("pool", kbuf[64:128, par, :], 0.0, [r_kb2[par]])
        kb.ms("pool", kown[64:96, par, :], 0.0, [r_ko2[par]])

    def a_loads(h, par):
        kb.dma("sp", kbuf[0:64, par, :], kTf[h * 64:(h + 1) * 64, :], writes=[r_kb2[par]])
        kb.dma("sp", kbuf[64:96, par, :], D["ohA"][:, :], writes=[r_kb2[par]])
        kb.dma("sp", qt[0:64, par, :], qkT[h * 64:(h + 1) * 64, :], writes=[r_qt2[par]])
        kb.dma("sp", vstg[:], vhp[h], writes=[r_vstg])
        kb.cp("pool", vaugs[par][0][:, :, 0:64], vstg[:], [r_vstg], [vaugs[par][1]])
        kb.dma("sp", kown[0:64, par, :], qkT[256 + h * 64:256 + (h + 1) * 64, :], writes=[r_ko2[par]])
        kb.dma("sp", kown[96:128, par, :], D["ohA_own"][:, :], writes=[r_ko2[par]])
        kb.dma("sp", vostg[:], vown_d[:, h * 64:(h + 1) * 64].rearrange("(c p) d -> p c d", p=128), writes=[r_vostg])
        kb.cp("pool", vowns[par][0][:, :, 0:64], vostg[:], [r_vostg], [vowns[par][1]])

    def a_bias1(h, par):
        r_kbuf, r_qt = r_kb2[par], r_qt2[par]
        kb.red(kmean[:], kbuf[0:64, par, :].rearrange("p (n l) -> p n l", l=256), [r_kbuf], [r_kmean])
        kb.ts("dve", kmean16[:], kmean[:], 1.0 / 256, None, ALU.mult, None, [r_kmean], [r_km16])

    def a_bias2(h, par):
        r_kbuf, r_qt = r_kb2[par], r_qt2[par]
        pg, r_pg = ps_x[0]
        for c in range(16):
            kb.mm(pg[:, c * 32:(c + 1) * 32], qt[0:64, par, c * 128:(c + 1) * 128], kmean16[:], True, True,
                  [r_qt, r_km16], [r_pg], signal=(c == 15))
        kb.tt("dve", gm[:], pg[:].rearrange("p (c n) -> p c n", n=32), pmA[:, :, 0, :], ALU.add, [r_pg, r_pmA], [r_gm])
        for c in range(16):
            kb.op("dve", (lambda c: lambda en: en.max(out=t8[:, c, :], in_=gm[:, c, :]))(c), [r_gm], [r_t8])
        kb.tt("dve", sel[:], gm[:], t8[:, :, 2:3].to_broadcast([128, 16, 32]), ALU.is_ge, [r_gm, r_t8], [r_sel])
        for which in range(2):
            kb.tt("dve", tmpb[:], sel[:], pmA[:, :, 1 + which, :], ALU.mult, [r_sel, r_pmA], [r_tmpb])
            if which == 1:
                kb.tt("dve", tmpb[:], tmpb[:], pmA[:, :, 3, :], ALU.add, [r_tmpb, r_pmA], [r_tmpb])
            kb.ts("dve", bst[:, :, which * 32:(which + 1) * 32], tmpb[:], BIG, -BIG, ALU.mult, ALU.add, [r_tmpb], [r_bst])

    def a_bias3(h, par):
        r_kbuf, r_qt = r_kb2[par], r_qt2[par]
        for hf in range(2):
            for c8 in range(8):
                c = hf * 8 + c8
                kb.tr(ps_b[0:64, c8 * 128:(c8 + 1) * 128], bst[:, c, :], idb[:], [r_bst, r_id], [r_ps_b], signal=(c8 == 7))
            kb.cp("dve", qt[64:128, par, hf * 1024:(hf + 1) * 1024], ps_b[0:64, :], [r_ps_b], [r_qt])

    a_loads(0, 0)
    pe_warm(48)
    a_bias1(0, 0)
    a_bias2(0, 0)
    a_bias3(0, 0)
    for h in range(4):
        par = h % 2
        r_kbuf, r_qt, r_kown = r_kb2[par], r_qt2[par], r_ko2[par]
        vaug_h, r_vaug_h = vaugs[par]
        vown_h, r_vown_h = vowns[par]
        if h + 1 < 4:
            pipe.sync()
            a_loads(h + 1, 1 - par)
            pipe.after((lambda hh, pp: lambda: a_bias1(hh, pp))(h + 1, 1 - par), delay=8)
            pipe.after((lambda hh, pp: lambda: a_bias2(hh, pp))(h + 1, 1 - par), delay=24)
            pipe.after((lambda hh, pp: lambda: a_bias3(hh, pp))(h + 1, 1 - par), delay=44)
        for i in range(4):
            acc = ps_o[nxt("o", 2)]
            qs = qt[:, par, i * 512:(i + 1) * 512]
            for kt in range(NSTREAM[i]):
                attn_tile(kbuf[:, par, kt * 128:(kt + 1) * 128], qs, vaug_h[:, kt, :], acc, kt == 0, False, 0.125,
                          [r_kbuf, r_qt, r_vaug_h])
            for t in range(4):
                c = 4 * i + t
                attn_tile(kown[:, par, c * 128:(c + 1) * 128], qs, vown_h[:, c, :], acc, False, t == 3, 0.125,
                          [r_kown, r_qt, r_vown_h], mask=(mdiag[:, t, :], r_md))

            def fin_a(acc=acc, h=h, i=i):
                k_ = nxt("r", 2)
                n_, rn_, d_, rd_ = release(acc, k_)
                rc, r_rc = rcp[k_]
                kb.op("dve", lambda en: en.reciprocal(out=rc[:], in_=d_[:]), [rd_], [r_rc])
                kb.tt("dve", yT[(h % 2) * 64:(h % 2) * 64 + 64, h // 2, i * 512:(i + 1) * 512], n_[:], rc[:], ALU.mult,
                      [rn_, r_rc], [r_yT])
            pipe.after(fin_a)
    pipe.flush()

    if STOP <= 2:
        return _fin()
    sc_c = float(32 ** -0.5)
    for m in range(2):
        kb.dma("sp", kbuf[64 * m + 32:64 * m + 48, 0, :], D["ohC"][:, :], writes=[r_kb2[0]])
        kb.dma("sp", qt[64 * m + 32:64 * m + 48, 0, :], D["cbC"][:, :], writes=[r_qt2[0]])

    def c_vload(h):
        par = h % 2
        kb.dma("sp", vstg[:], vhp[6 + h], writes=[r_vstg])
        kb.cp("pool", vaugs[par][0][:, :, 0:64], vstg[:], [r_vstg], [vaugs[par][1]])
        kb.dma("sp", vostg[:], vown_d[:, 384 + h * 64:384 + (h + 1) * 64].rearrange("(c p) d -> p c d", p=128), writes=[r_vostg])
        kb.cp("pool", vowns[par][0][:, :, 0:64], vostg[:], [r_vostg], [vowns[par][1]])

    c_vload(0)
    for h in range(4):
        par = h % 2
        vaug_h, r_vaug_h = vaugs[par]
        vown_h, r_vown_h = vowns[par]
        for m in range(2):
            r0 = 384 + h * 64 + m * 32
            kb.dma("sp", kbuf[64 * m:64 * m + 32, 0, :], kTf[r0:r0 + 32, :], writes=[r_kb2[0]])
            q0 = 1152 + h * 64 + m * 32
            kb.dma("sp", qt[64 * m:64 * m + 32, 0, :], qkT[q0:q0 + 32, :], writes=[r_qt2[0]])
            k0 = 1408 + h * 64 + m * 32
            kb.dma("sp", kown[64 * m:64 * m + 32, 0, :], qkT[k0:k0 + 32, :], writes=[r_ko2[0]])
        if h + 1 < 4:
            c_vload(h + 1)
        pe_warm(64)
        for i in range(4):
            accs = [ps_o[0], ps_o[1]]
            for kt in range(NSTREAM[i]):
                for m in range(2):
                    attn_tile(kbuf[64 * m:64 * m + 48, 0, kt * 128:(kt + 1) * 128], qt[64 * m:64 * m + 48, 0, i * 512:(i + 1) * 512], vaug_h[:, kt, :],
                              accs[m], kt == 0, False, sc_c, [r_kb2[0], r_qt2[0], r_vaug_h])
            for t in range(4):
                c = 4 * i + t
                for m in range(2):
                    attn_tile(kown[64 * m:64 * m + 32, 0, c * 128:(c + 1) * 128], qt[64 * m:64 * m + 32, 0, i * 512:(i + 1) * 512], vown_h[:, c, :],
                              accs[m], False, t == 3, sc_c, [r_ko2[0], r_qt2[0], r_vown_h], mask=(mdiag[:, t, :], r_md))

            def fin_c1(accs=accs):
                rel = [release(accs[m], m) for m in range(2)]
                for m in range(2):
                    kb.op("dve", (lambda m: lambda en: en.reciprocal(out=rcp[m][0][:], in_=rel[m][2][:]))(m), [rel[m][3]], [rcp[m][1]])
                kb.tt("dve", t1[:], rel[0][0][:], rcp[0][0][:], ALU.mult, [rel[0][1], rcp[0][1]], [r_t1])
                kb.tt("dve", t2[:], rel[1][0][:], rcp[1][0][:], ALU.mult, [rel[1][1], rcp[1][1]], [r_t2])
                kb.stt(t1[:], t2[:], lamt[0:64, 4:5], t1[:], ALU.mult, ALU.add, [r_t1, r_t2, r_lamt], [r_t1])
                kb.act(sq16[:], t1[:], AF.Square, [r_t1], [r_sq16])

            def fin_c2(h=h, i=i):
                px, r_px = ps_x[0]
                kb.mm(px[0:64, :], ones64[:], sq16[:], True, True, [r_ones64, r_sq16], [r_px])
                kb.act(rs[:], px[0:64, :], AF.Ln, [r_px, r_eps], [r_rs], bias=epst[0:64, :], scale=1.0 / 64)
                kb.act(rs[:], rs[:], AF.Exp, [r_rs], [r_rs], scale=-0.5)
                kb.tt("dve", t1[:], t1[:], rs[:], ALU.mult, [r_t1, r_rs], [r_t1])
                kb.ts("dve", yT[(h % 2) * 64:(h % 2) * 64 + 64, 6 + h // 2, i * 512:(i + 1) * 512], t1[:], sgc[0:64, :], float(1.0 - lam_init),
                      ALU.mult, ALU.mult, [r_t1, r_sgc], [r_yT])
            pipe.after(fin_c1)
            pipe.after(fin_c2, delay=12)
        pipe.flush()

    if STOP <= 3:
        return _fin()
    qb = qt
    for k in range(2):
        qv = kbuf[0:64, 0, :].rearrange("p (g t) -> p g t", g=4)
        kb.dma("sp", qv, qkT[512 + k * 256:512 + (k + 1) * 256, :].rearrange("(g d) t -> d g t", d=64), writes=[r_kb2[0]])
        kb.dma("sp", kown[0:64, 0, :], qkT[1024 + k * 64:1024 + (k + 1) * 64, :], writes=[r_ko2[0], r_ko2[1]])
        kb.dma("sp", kown[0:64, 1, 0:512], kTbh[k * 64:(k + 1) * 64, :], writes=[r_ko2[0], r_ko2[1]])
        kb.dma("sp", vostg[:], vown_d[:, 256 + k * 64:256 + (k + 1) * 64].rearrange("(c p) d -> p c d", p=128), writes=[r_vostg])
        kb.cp("pool", vown[:, :, 0:64], vostg[:], [r_vostg], [r_vown])
        kb.dma("sp", vstg[:, 0:4, :], vbh_d[:, k * 64:(k + 1) * 64].rearrange("(s p) d -> p s d", p=128), writes=[r_vstg])
        kb.cp("pool", vaug[:, 0:4, 0:64], vstg[:, 0:4, :], [r_vstg], [r_vaug])
        if k == 0:
            kb.ms("pool", vaug[:, 4:8, 64:128], 1.0, [r_vaug])
        for s in range(4):
            kb.ts("dve", vaug[:, s, 0:64], vaug[:, s, 0:64], hval[:, s:s + 1], None, ALU.mult, None, [r_vaug, r_hval], [r_vaug])
            kb.ts("dve", vaug[:, s, 64:128], vaug[:, 4 + s, 64:128], hval[:, s:s + 1], None, ALU.mult, None, [r_vaug, r_hval], [r_vaug])
        pe_warm(48)
        for c in range(16):
            s = c // 4
            acc = ps_o[nxt("o", 2)]
            if c % 4 == 0:
                kprev, vprev = kown[0:64, 1, s * 128:(s + 1) * 128], vaug[:, s, :]
            else:
                kprev, vprev = kown[0:64, 0, (c - 1) * 128:c * 128], vown[:, c - 1, :]
            rds = [r_ko2[0], r_ko2[1], r_kb2[0], r_vown, r_vaug]
            qparts = [qv[:, gi, c * 128:(c + 1) * 128] for gi in range(4)]
            attn_tile(kprev, qparts, vprev, acc, True, False, 0.125, rds, mask=(mB[:, 0, :], r_mB))
            attn_tile(kown[0:64, 0, c * 128:(c + 1) * 128], qparts, vown[:, c, :], acc, False, True, 0.125, rds, mask=(mB[:, 1, :], r_mB))

            def fin_b(acc=acc, c=c, k=k):
                k_ = nxt("r", 2)
                n_, rn_, d_, rd_ = release(acc, k_)
                rc, r_rc = rcp[k_]
                kb.cp("dve", t2[:], d_[:], [rd_], [r_t2])
                for gi in range(4):
                    hh = 4 * k + gi
                    kb.ts("dve", t2[:, gi * 128:(gi + 1) * 128], t2[:, gi * 128:(gi + 1) * 128], esink[0:64, hh:hh + 1], None, ALU.add, None,
                          [r_t2, r_esink], [r_t2])
                kb.act(rc[:], t2[:], AF.Ln, [r_t2], [r_rc])
                kb.act(rc[:], rc[:], AF.Exp, [r_rc], [r_rc], scale=-1.0)
                for gi in range(4):
                    hh = 4 * k + gi
                    kb.tt("dve", yT[(hh % 2) * 64:(hh % 2) * 64 + 64, 2 + hh // 2, c * 128:(c + 1) * 128],
                          n_[:, gi * 128:(gi + 1) * 128], rc[:, gi * 128:(gi + 1) * 128], ALU.mult, [rn_, r_rc], [r_yT])
            pipe.after(fin_b)
        pipe.flush()
    if "xmid" not in D:
        return _fin()
    kb.barrier()
    gT, x, xmid = D["gT"], D["x"], D["xmid"]
    wp, r_wp = kbuf[:, 0, :].rearrange("p (k n) -> p k n", k=8), kb.res(P + "m_wp")
    wo, r_wo = kbuf[:, 1, :].rearrange("p (k n) -> p k n", k=8), kb.res(P + "m_wo")
    for nm, k0, nk in (("w_pa", 0, 2), ("w_pb", 2, 4), ("w_pc", 6, 2)):
        wv = D[nm].rearrange("(k p) n -> p k n", p=128)
        for k in range(nk):
            kb.dma("pool", wp[:, k0 + k, :], wv[:, k, :], writes=[r_wp])
    wov = D["w_out"].rearrange("(k p) n -> p k n", p=128)
    for k in range(8):
        kb.dma("pool", wo[:, k, :], wov[:, k, :], writes=[r_wo])
    va1 = vaugs[1][0][:].rearrange("p a b -> p (a b)")
    gt = [(va1[:, i_ * 1536:(i_ + 1) * 1536].rearrange("p (k n) -> p k n", k=3), kb.res(P + f"m_gt{i_}")) for i_ in range(2)]
    macc, r_macc = gm[:].rearrange("p a b -> p (a b)"), kb.res(P + "m_macc")
    mtmp, r_mtmp = sel[:].rearrange("p a b -> p (a b)"), kb.res(P + "m_mtmp")
    mT, r_mT = vstg[:].rearrange("p a b -> p (a b)").rearrange("p (k n) -> p k n", k=8), kb.res(P + "m_mT")
    pmf = pmA[:].rearrange("p a b c -> p (a b c)")
    xt = [(pmf[:, i_ * 1024:(i_ + 1) * 1024], kb.res(P + f"m_xt{i_}")) for i_ in range(2)]
    ob = [S(f"m_ob{i_}", [128, 512], F32) for i_ in range(2)]
    banks = [ps_o[0], ps_o[1], ps_x[0]]
    gTv = gT.rearrange("(br f) t -> f br t", br=3)
    BR = ((0, 2), (2, 4), (6, 2))
    gi_ = 0
    for g in range(4):
        for fc in range(8):
            gtt, r_gtt = gt[gi_ % 2]; gi_ += 1
            kb.dma("sp", gtt, gTv[fc * 128:(fc + 1) * 128, :, g * 512:(g + 1) * 512], writes=[r_gtt])
            for br, (k0, nk) in enumerate(BR):
                px, r_px = banks[nxt("x", 3)]
                for k in range(nk):
                    kb.mm(px[:], wp[:, k0 + k, fc * 128:(fc + 1) * 128], yT[:, k0 + k, g * 512:(g + 1) * 512], k == 0, k == nk - 1,
                          [r_wp, r_yT], [r_px])
                if br == 0:
                    kb.tt("dve", macc, px[:], gtt[:, 0, :], ALU.mult, [r_px, r_gtt], [r_macc])
                else:
                    kb.tt("dve", mtmp, px[:], gtt[:, br, :], ALU.mult, [r_px, r_gtt], [r_mtmp])
                    if br == 1:
                        kb.tt("dve", macc, macc, mtmp, ALU.add, [r_macc, r_mtmp], [r_macc])
                    else:
                        kb.tt("dve", mT[:, fc, :], macc, mtmp, ALU.add, [r_macc, r_mtmp], [r_mT])
        for tt_ in range(4):
            t = g * 4 + tt_
            b = t % 2
            kb.dma("sp", xt[b][0], x[t * 128:(t + 1) * 128, :], writes=[xt[b][1]])
            for hc in range(2):
                px, r_px = banks[nxt("x", 3)]
                for fc in range(8):
                    kb.mm(px[:], mT[:, fc, tt_ * 128:(tt_ + 1) * 128], wo[:, fc, hc * 512:(hc + 1) * 512], fc == 0, fc == 7,
                          [r_mT, r_wo], [r_px])
                kb.tt("dve", ob[hc][0][:], px[:], xt[b][0][:, hc * 512:(hc + 1) * 512], ALU.add, [r_px, xt[b][1]], [ob[hc][1]])
                kb.dma("sp", xmid[t * 128:(t + 1) * 128, hc * 512:(hc + 1) * 512], ob[hc][0][:], reads=[ob[hc][1]])
    return [ob[0][1], ob[1][1]]


def emit_M(kb, D, pfx="m"):
    P = pfx
    gT, x, xmid = D["gT"], D["x"], D["xmid"]

    def S(name, shape, dt):
        return kb.sb(P + name, shape, dt), kb.res(P + name)

    cnt = {"x": 0}

    def nxt(k, n):
        v = cnt[k] % n
        cnt[k] += 1
        return v

    ps_x = [(kb.ps(P + f"ps_x{i}", [128, 512], F32), kb.res(P + f"ps_x{i}")) for i in range(4)]
    yT, r_yT = S("yT", [128, 8, 2048], BF16)
    kb.dma("sp", yT[:], D["yT_in"].rearrange("(k p) t -> p k t", p=128), writes=[r_yT])
    wp, r_wp = S("wp", [128, 8, 1024], BF16)
    wo, r_wo = S("wo", [128, 8, 1024], BF16)
    for nm, k0, nk in (("w_pa", 0, 2), ("w_pb", 2, 4), ("w_pc", 6, 2)):
        wv = D[nm].rearrange("(k p) n -> p k n", p=128)
        for k in range(nk):
            kb.dma("pool", wp[:, k0 + k, :], wv[:, k, :], writes=[r_wp])
    wov = D["w_out"].rearrange("(k p) n -> p k n", p=128)
    for k in range(8):
        kb.dma("pool", wo[:, k, :], wov[:, k, :], writes=[r_wo])
    gt = [S(f"gt{i}", [128, 3, 512], BF16) for i in range(2)]
    macc, r_macc = S("macc", [128, 512], F32)
    mtmp, r_mtmp = S("mtmp", [128, 512], F32)
    mT, r_mT = S("mT", [128, 8, 512], BF16)
    xt = [S(f"xt{i}", [128, 1024], F32) for i in range(2)]
    ob = [S(f"ob{i}", [128, 1024], F32) for i in range(2)]
    gTv = gT.rearrange("(br f) t -> f br t", br=3)
    BR = ((0, 2), (2, 4), (6, 2))
    gi_ = 0
    for g in range(4):
        for fc in range(8):
            gtt, r_gtt = gt[gi_ % 2]; gi_ += 1
            kb.dma("sp", gtt[:], gTv[fc * 128:(fc + 1) * 128, :, g * 512:(g + 1) * 512], writes=[r_gtt])
            for br, (k0, nk) in enumerate(BR):
                px, r_px = ps_x[nxt("x", 4)]
                for k in range(nk):
                    kb.mm(px[:], wp[:, k0 + k, fc * 128:(fc + 1) * 128], yT[:, k0 + k, g * 512:(g + 1) * 512], k == 0, k == nk - 1,
                          [r_wp, r_yT], [r_px])
                if br == 0:
                    kb.tt("dve", macc[:], px[:], gtt[:, 0, :], ALU.mult, [r_px, r_gtt], [r_macc])
                else:
                    kb.tt("dve", mtmp[:], px[:], gtt[:, br, :], ALU.mult, [r_px, r_gtt], [r_mtmp])
                    if br == 1:
                        kb.tt("dve", macc[:], macc[:], mtmp[:], ALU.add, [r_macc, r_mtmp], [r_macc])
                    else:
                        kb.tt("dve", mT[:, fc, :], macc[:], mtmp[:], ALU.add, [r_macc, r_mtmp], [r_mT])
        for tt_ in range(4):
            t = g * 4 + tt_
            b = t % 2
            kb.dma("sp", xt[b][0][:], x[t * 128:(t + 1) * 128, :], writes=[xt[b][1]])
            for hc in range(2):
                px, r_px = ps_x[nxt("x", 4)]
                for fc in range(8):
                    kb.mm(px[:], mT[:, fc, tt_ * 128:(tt_ + 1) * 128], wo[:, fc, hc * 512:(hc + 1) * 512], fc == 0, fc == 7,
                          [r_mT, r_wo], [r_px])
                kb.tt("dve", ob[b][0][:, hc * 512:(hc + 1) * 512], px[:], xt[b][0][:, hc * 512:(hc + 1) * 512], ALU.add,
                      [r_px, xt[b][1]], [ob[b][1]])
            kb.dma("sp", xmid[t * 128:(t + 1) * 128, :], ob[b][0][:], reads=[ob[b][1]])
    return [ob[0][1], ob[1][1]]


GROUPS = [[j, 7 - j, 8 + j, 15 - j] for j in range(4)]
S = 8192
BF = ml_dtypes.bfloat16


def core_pos(j):
    return np.concatenate([np.arange(g * 512, (g + 1) * 512) for g in GROUPS[j]])


def rope_tabs(pos, dim):
    inv = (1.0 / (np.float32(10000.0) ** (np.arange(0, dim, 2, dtype=np.float32) / np.float32(dim)))).astype(np.float32)
    ang = pos.astype(np.float32)[:, None] * inv[None, :]
    c = np.cos(ang).astype(np.float32)
    s = np.sin(ang).astype(np.float32)
    ct = np.concatenate([c, c], axis=1)
    st = np.concatenate([-s, s], axis=1)
    return np.ascontiguousarray(ct), np.ascontiguousarray(st)


def rep128(v):
    return np.ascontiguousarray(np.broadcast_to(np.asarray(v, np.float32)[None, :], (128, v.shape[-1])))


def p_inputs(x_core, l, j, inp):
    pos = core_pos(j)
    ct64, st64 = rope_tabs(pos, 64)
    ct32, st32 = rope_tabs(pos, 32)
    gall = np.ones((2048,), np.float32)
    gall[0:256] = np.tile(inp["qn_a"][l], 4)
    gall[256:512] = np.tile(inp["kn_a"][l], 4)
    gall[768:1280] = np.tile(inp["qn_b"][l], 8)
    gall[1280:1408] = np.tile(inp["kn_b"][l], 2)
    gall[1536:1792] = np.tile(inp["qn_c"][l], 8)
    gall[1792:2048] = np.tile(inp["kn_c"][l], 8)
    return {"x": np.ascontiguousarray(x_core), "anorm": rep128(inp["attn_norm"][l]),
            "w_in": np.ascontiguousarray(inp["w_in"][l]), "gall": rep128(gall),
            "ct64": ct64, "st64": st64, "ct32": ct32, "st32": st32}


def f_inputs(xm_core, xm_full_b, l, j, inp):
    halo = np.zeros((8, 1024), np.float32)
    for s, g in enumerate(GROUPS[j]):
        if g > 0:
            halo[2 * s:2 * s + 2] = xm_full_b[g * 512 - 2:g * 512]
    cw = inp["conv_w"][l]
    cb = inp["conv_b"][l]
    convp = np.zeros((128, 44, 4), np.float32)
    convp[:, :, 0:3] = cw.T.reshape(44, 128, 3).transpose(1, 0, 2)
    convp[:, :, 3] = cb.reshape(44, 128).T
    return {"xm": np.ascontiguousarray(xm_core), "xhalo": halo, "mnorm": rep128(inp["mlp_norm"][l]),
            "w_up": np.ascontiguousarray(inp["w_up"][l]), "convp": convp,
            "w_down": np.ascontiguousarray(inp["w_down"][l])}


NEGV = -30000.0


def a_consts(j):
    gl = GROUPS[j]
    f32 = np.float32
    mdiag = np.zeros((128, 4, 512), f32)
    p = np.arange(128)[:, None]
    f = np.arange(512)[None, :]
    for t in range(4):
        mdiag[:, t, :] = np.where(t * 128 + p > f, NEGV, 0.0)
    mB = np.zeros((128, 2, 512), f32)
    fi = (np.arange(512) % 128)[None, :]
    mB[:, 0, :] = np.where(p <= fi, NEGV, 0.0)
    mB[:, 1, :] = np.where(p > fi, NEGV, 0.0)
    pmA = np.zeros((128, 16, 4, 32), f32)
    n = np.arange(32)
    for c in range(16):
        g = gl[c // 4]
        own = 2 * g + (c % 4) // 2
        pmA[:, c, 0, :] = np.where(n < own, 0.0, -1e30)[None, :]
        pmA[:, c, 1, :] = (n < 2 * g).astype(f32)[None, :]
        pmA[:, c, 2, :] = ((n >= 2 * g) & (n < own)).astype(f32)[None, :]
        pmA[:, c, 3, :] = (n == own).astype(f32)[None, :]
    keys = np.arange(S)
    ohA = (keys[None, :] // 256 == np.arange(32)[:, None]).astype(f32)
    ohC = (keys[None, :] // 512 == np.arange(16)[:, None]).astype(f32)
    pos = core_pos(j)
    ohA_own = (pos[None, :] // 256 == np.arange(32)[:, None]).astype(f32)
    cbC = np.where(np.arange(16)[:, None] < (pos[None, :] // 512), 0.0, NEGV).astype(f32)
    hval = np.zeros((128, 4), f32)
    for s, g in enumerate(gl):
        hval[:, s] = 1.0 if g > 0 else 0.0
    return {"mdiag": mdiag.astype(BF), "mB": mB.astype(BF), "pmA": pmA, "ohA": ohA.astype(BF), "ohC": ohC.astype(BF),
            "ohA_own": ohA_own.astype(BF), "cbC": cbC.astype(BF), "hval": hval}


def gather_kv(qkT_list, v_list):
    kT = np.zeros((640, S), BF)
    vf = np.zeros((S, 640), BF)
    for j in range(4):
        pos = core_pos(j)
        q = qkT_list[j]
        kT[0:256, pos] = q[256:512]
        kT[256:384, pos] = q[1024:1152]
        kT[384:640, pos] = q[1408:1664]
        vf[pos] = v_list[j]
    return kT, vf


def a_inputs(j, l, qkT_own, v_own, kT_full, v_full, gT_own, x_core, inp):
    d = dict(a_consts(j))
    d["qkT"] = np.ascontiguousarray(qkT_own)
    d["v_own"] = np.ascontiguousarray(v_own)
    d["kT_full"] = np.ascontiguousarray(kT_full)
    d["v_hp"] = np.ascontiguousarray(v_full.reshape(64, 128, 10, 64).transpose(2, 1, 0, 3))
    kh = np.zeros((128, 512), BF)
    vh = np.zeros((512, 128), BF)
    for s, g in enumerate(GROUPS[j]):
        if g > 0:
            kh[:, s * 128:(s + 1) * 128] = kT_full[256:384, g * 512 - 128:g * 512]
            vh[s * 128:(s + 1) * 128, :] = v_full[g * 512 - 128:g * 512, 256:384]
    d["kTb_halo"] = kh
    d["vb_halo"] = vh
    d["gT"] = np.ascontiguousarray(gT_own)
    d["x"] = np.ascontiguousarray(x_core)
    lamv = np.stack([inp["lam_q1"][l], inp["lam_k1"][l], inp["lam_q2"][l], inp["lam_k2"][l]]).astype(np.float32)
    d["lamv"] = np.ascontiguousarray(np.broadcast_to(lamv[None], (128, 4, 32)))
    d["sgc"] = np.ascontiguousarray(np.tile(inp["subln"][l], 2).reshape(128, 1).astype(np.float32))
    d["sinks"] = rep128(inp["sinks"][l])
    for nm in ("w_pa", "w_pb", "w_pc", "w_out"):
        d[nm] = np.ascontiguousarray(inp[nm][l])
    return d


A_IN_SPECS = [("qkT", [1664, 2048], "bf"), ("v_own", [2048, 640], "bf"), ("kT_full", [640, 8192], "bf"),
              ("v_hp", [10, 128, 64, 64], "bf"), ("kTb_halo", [128, 512], "bf"), ("vb_halo", [512, 128], "bf"),
              ("gT", [3072, 2048], "bf"), ("x", [2048, 1024], "f"), ("mdiag", [128, 4, 512], "bf"), ("mB", [128, 2, 512], "bf"),
              ("pmA", [128, 16, 4, 32], "f"), ("ohA", [32, 8192], "bf"), ("ohC", [16, 8192], "bf"), ("ohA_own", [32, 2048], "bf"),
              ("cbC", [16, 2048], "bf"), ("hval", [128, 4], "f"), ("lamv", [128, 4, 32], "f"), ("sgc", [128, 1], "f"),
              ("sinks", [128, 8], "f"), ("w_pa", [256, 1024], "f"), ("w_pb", [512, 1024], "f"), ("w_pc", [256, 1024], "f"),
              ("w_out", [1024, 1024], "f")]


def _make_ident(kb):
    idb = kb.sb("idb", [128, 128], BF16)
    r_id = kb.res("idb")
    idf = kb.sb("idf", [128, 128], F32)
    kb.op("pool", lambda e: e.memset(idf[:], 0.0), writes=[r_id])
    kb.op("pool", lambda e: e.affine_select(out=idf[:], in_=idf[:], pattern=[[-1, 128]], compare_op=ALU.not_equal,
                                            fill=1.0, base=0, channel_multiplier=1), reads=[r_id], writes=[r_id])
    kb.op("pool", lambda e: e.tensor_copy(out=idb[:], in_=idf[:]), reads=[r_id], writes=[r_id])
    return idb, r_id


def _build(kind, lam_init=0.0):
    nc = bass.Bass("TRN2", target_bir_lowering=False)

    def di(n, s, dt=F32):
        return nc.dram_tensor(n, s, dt, kind="ExternalInput").ap()

    def do(n, s, dt):
        return nc.dram_tensor(n, s, dt, kind="ExternalOutput").ap()

    with ExitStack() as st:
        kb = KB(nc, st)
        if kind == "P":
            D = {"x": di("x", [2048, 1024]), "anorm": di("anorm", [128, 1024]), "w_in": di("w_in", [1024, 5376]),
                 "gall": di("gall", [128, 2048]), "ct64": di("ct64", [2048, 64]), "st64": di("st64", [2048, 64]),
                 "ct32": di("ct32", [2048, 32]), "st32": di("st32", [2048, 32]),
                 "qkT": do("qkT", [1664, 2048], BF16), "v": do("v", [2048, 640], BF16), "gT": do("gT", [3072, 2048], BF16)}
            fin = emit_P(kb, D, _make_ident(kb))
        elif kind == "A":
            D = {}
            for n, s, dt in A_IN_SPECS:
                D[n] = di(n, s, BF16 if dt == "bf" else F32)
            D["xmid"] = do("xmid", [2048, 1024], F32)
            fin = emit_A(kb, D, _make_ident(kb), lam_init)
        elif kind == "M":
            D = {"yT_in": di("yT_in", [1024, 2048], BF16), "gT": di("gT", [3072, 2048], BF16), "x": di("x", [2048, 1024]),
                 "w_pa": di("w_pa", [256, 1024]), "w_pb": di("w_pb", [512, 1024]), "w_pc": di("w_pc", [256, 1024]),
                 "w_out": di("w_out", [1024, 1024]), "xmid": do("xmid", [2048, 1024], F32)}
            fin = emit_M(kb, D)
        else:
            D = {"xm": di("xm", [2048, 1024]), "xhalo": di("xhalo", [8, 1024]), "mnorm": di("mnorm", [128, 1024]),
                 "w_up": di("w_up", [1024, 5632]), "convp": di("convp", [128, 44, 4]), "w_down": di("w_down", [2816, 1024]),
                 "xo": do("xo", [2048, 1024], F32)}
            fin = emit_F(kb, D, _make_ident(kb))
        kb.finish(fin)
    return nc


def _run(nc, in_maps):
    res = run_bass_kernel_spmd(nc, in_maps, core_ids=list(range(8)))
    return res.results


def kernel(**inp):
    inp = {k: np.asarray(v) for k, v in inp.items()}
    x = inp["x"].astype(np.float32)
    cores = [(c // 4, c % 4) for c in range(8)]
    xs = [np.ascontiguousarray(x[b][core_pos(j)]) for b, j in cores]
    for l in range(2):
        lam_init = 0.8 - 0.6 * float(np.exp(-0.3 * l))
        rp = _run(_build("P"), [p_inputs(xs[c], l, cores[c][1], inp) for c in range(8)])
        full = {}
        for b in range(2):
            full[b] = gather_kv([rp[4 * b + j]["qkT"] for j in range(4)], [rp[4 * b + j]["v"] for j in range(4)])
        a_maps = []
        for c, (b, j) in enumerate(cores):
            d = a_inputs(j, l, rp[c]["qkT"], rp[c]["v"], full[b][0], full[b][1], rp[c]["gT"], xs[c], inp)
            a_maps.append(d)
        ra = _run(_build("A", lam_init), a_maps)
        rm = ra
        xm_full = np.zeros((2, S, 1024), np.float32)
        for c, (b, j) in enumerate(cores):
            xm_full[b][core_pos(j)] = rm[c]["xmid"]
        rf = _run(_build("F"), [f_inputs(rm[c]["xmid"], xm_full[cores[c][0]], l, cores[c][1], inp) for c in range(8)])
        xs = [np.ascontiguousarray(rf[c]["xo"]) for c in range(8)]
    out = np.zeros((2, S, 1024), np.float32)
    for c, (b, j) in enumerate(cores):
        out[b][core_pos(j)] = xs[c]
    return out
```
